# Optimizing a Trainium2 kernel written in Bass

```python
import jax, jax.numpy as jnp
from jax import lax
import numpy as np

D_MODEL = 1024
BATCH = 2
SEQ = 8192
DEPTH = 4

N_MEM = 256
HEAD_DIM = 64
FOX_HEADS = D_MODEL // HEAD_DIM
FOX_W = FOX_HEADS * HEAD_DIM
SB_HEADS = D_MODEL // HEAD_DIM
SB_W = SB_HEADS * HEAD_DIM
LRU_W = D_MODEL
LRU_BLOCKS = D_MODEL // HEAD_DIM
LRU_BW = LRU_W // LRU_BLOCKS
LRU_CONV = 4
LRU_C = 8.0
MEM_HEADS = 4
MEM_HD = D_MODEL // MEM_HEADS
MEM_W = MEM_HEADS * MEM_HD
N_BRANCH = 4
BRANCH_W = D_MODEL
D_FF = ((8 * D_MODEL // 3 + 255) // 256) * 256
FFN_CONV = 3
Q_BLOCK = 128
EPS = 1e-6
N_IN = 3 * FOX_W + FOX_HEADS + 2 * LRU_W + 3 * SB_W + MEM_W + N_BRANCH * D_MODEL

kernel_name = "hybrid_fox_rglru_stickbreak_memory_trunk"


def _column_slices():
    widths = (("fox_q", FOX_W), ("fox_k", FOX_W), ("fox_v", FOX_W), ("fox_f", FOX_HEADS),
              ("lru_x", LRU_W), ("lru_g", LRU_W),
              ("sb_q", SB_W), ("sb_k", SB_W), ("sb_v", SB_W),
              ("mem_q", MEM_W), ("gates", N_BRANCH * D_MODEL))
    out, start = {}, 0
    for name, w in widths:
        out[name] = (start, start + w)
        start += w
    return out


def _rms(x, g):
    x32 = x.astype(jnp.float32)
    y = x32 * lax.rsqrt(jnp.mean(x32 * x32, axis=-1, keepdims=True) + EPS)
    return (y * g.astype(jnp.float32)).astype(x.dtype)


def _causal_dwconv(x, w, b):
    k, s = w.shape[0], x.shape[1]
    xp = jnp.pad(x, ((0, 0), (k - 1, 0), (0, 0)))
    out = b
    for i in range(k):
        out = out + xp[:, i:i + s] * w[i]
    return out


def _heads(t, n):
    b, s, _ = t.shape
    return t.reshape(b, s, n, -1).transpose(0, 2, 1, 3)


def _merge_heads(t):
    b, h, s, d = t.shape
    return t.transpose(0, 2, 1, 3).reshape(b, s, h * d)


def _fox_attention(q, k, v, log_f):
    b, h, s, d = q.shape
    nb = s // Q_BLOCK
    scale = d ** -0.5
    cum_f = jnp.cumsum(log_f, axis=-1)
    q_blk = q.reshape(b, h, nb, Q_BLOCK, d).transpose(2, 0, 1, 3, 4)
    f_blk = cum_f.reshape(b, h, nb, Q_BLOCK).transpose(2, 0, 1, 3)
    k_pos = jnp.arange(s)

    def block(args):
        qi, fi, bi = args
        q_pos = bi * Q_BLOCK + jnp.arange(Q_BLOCK)
        logits = jnp.einsum('bhqd,bhkd->bhqk', qi, k) * scale + (fi[..., :, None] - cum_f[..., None, :])
        logits = jnp.where(k_pos[None, :] <= q_pos[:, None], logits, -jnp.inf)
        p = jax.nn.softmax(logits, axis=-1)
        return jnp.einsum('bhqk,bhkd->bhqd', p, v)

    out = lax.map(block, (q_blk, f_blk, jnp.arange(nb)))
    return out.transpose(1, 2, 0, 3, 4).reshape(b, h, s, d)


def _stick_breaking_attention(q, k, v):
    b, h, s, d = q.shape
    nb = s // Q_BLOCK
    scale = d ** -0.5
    q_blk = q.reshape(b, h, nb, Q_BLOCK, d).transpose(2, 0, 1, 3, 4)
    k_pos = jnp.arange(s)

    def block(args):
        qi, bi = args
        q_pos = bi * Q_BLOCK + jnp.arange(Q_BLOCK)
        causal = k_pos[None, :] < q_pos[:, None]
        z = jnp.einsum('bhqd,bhkd->bhqk', qi, k) * scale
        log_1m_beta = jnp.where(causal, jax.nn.log_sigmoid(-z), 0.0)
        excl = lax.cumsum(log_1m_beta, axis=3, reverse=True) - log_1m_beta
        weights = jnp.where(causal, jnp.exp(jax.nn.log_sigmoid(z) + excl), 0.0)
        return jnp.einsum('bhqk,bhkd->bhqd', weights, v)

    out = lax.map(block, (q_blk, jnp.arange(nb)))
    return out.transpose(1, 2, 0, 3, 4).reshape(b, h, s, d)


def _mem_attention(q, k, v):
    scale = q.shape[-1] ** -0.5
    p = jax.nn.softmax(jnp.einsum('bhsd,bhmd->bhsm', q, k) * scale, axis=-1)
    return jnp.einsum('bhsm,bhmd->bhsd', p, v)


def _rg_lru_branch(x, gate, conv_w, conv_b, w_a, b_a, w_x, b_x, lam):
    b, s, w = x.shape
    xc = _causal_dwconv(x, conv_w, conv_b).astype(jnp.float32)
    xg = xc.reshape(b, s, LRU_BLOCKS, LRU_BW)
    r = jax.nn.sigmoid(jnp.einsum('bsnd,nde->bsne', xg, w_a).reshape(b, s, w) + b_a)
    i = jax.nn.sigmoid(jnp.einsum('bsnd,nde->bsne', xg, w_x).reshape(b, s, w) + b_x)
    log_a = -LRU_C * r * jax.nn.softplus(-lam)
    a = jnp.exp(log_a)
    u = jnp.sqrt(-jnp.expm1(2.0 * log_a)) * (i * xc)

    def combine(c1, c2):
        a1, h1 = c1
        a2, h2 = c2
        return a1 * a2, a2 * h1 + h2

    _, hs = lax.associative_scan(combine, (a, u), axis=1)
    return hs * jax.nn.gelu(gate.astype(jnp.float32))


def setup_inputs(seed: int = 0) -> dict:
    key = jax.random.key(seed)
    ks = jax.random.split(key, 32)
    f32 = jnp.float32

    def nrm(k, shape, scale):
        return jax.random.normal(k, shape, f32) * scale

    def gain(k, shape):
        return 1.0 + 0.02 * jax.random.normal(k, shape, f32)

    res_scale = (2 * DEPTH) ** -0.5
    u = jax.random.uniform(ks[12], (DEPTH, LRU_W), f32, 0.9, 0.999)
    sig = u ** (1.0 / LRU_C)
    lru_lambda = jnp.log(sig) - jnp.log1p(-sig)
    return {
        "x": nrm(ks[0], (BATCH, SEQ, D_MODEL), 1.0),
        "mem": nrm(ks[1], (BATCH, N_MEM, D_MODEL), 1.0),
        "attn_norm_g": gain(ks[2], (DEPTH, D_MODEL)),
        "mem_norm_g": gain(ks[3], (DEPTH, D_MODEL)),
        "w_in": nrm(ks[4], (DEPTH, D_MODEL, N_IN), D_MODEL ** -0.5),
        "b_forget": jax.random.uniform(ks[5], (DEPTH, FOX_HEADS), f32, 2.0, 6.0),
        "fox_q_norm_g": gain(ks[6], (DEPTH, HEAD_DIM)),
        "fox_k_norm_g": gain(ks[7], (DEPTH, HEAD_DIM)),
        "lru_conv_w": nrm(ks[8], (DEPTH, LRU_CONV, LRU_W), LRU_CONV ** -0.5),
        "lru_conv_b": nrm(ks[9], (DEPTH, LRU_W), 0.01),
        "lru_w_a": nrm(ks[10], (DEPTH, LRU_BLOCKS, LRU_BW, LRU_BW), LRU_BW ** -0.5),
        "lru_b_a": nrm(ks[11], (DEPTH, LRU_W), 0.01),
        "lru_w_x": nrm(ks[13], (DEPTH, LRU_BLOCKS, LRU_BW, LRU_BW), LRU_BW ** -0.5),
        "lru_b_x": nrm(ks[14], (DEPTH, LRU_W), 0.01),
        "lru_lambda": lru_lambda,
        "w_mem_kv": nrm(ks[15], (DEPTH, D_MODEL, 2 * MEM_W), D_MODEL ** -0.5),
        "mem_q_norm_g": gain(ks[16], (DEPTH, MEM_HD)),
        "mem_k_norm_g": gain(ks[17], (DEPTH, MEM_HD)),
        "b_gate": nrm(ks[18], (DEPTH, N_BRANCH, D_MODEL), 0.01),
        "w_branch": nrm(ks[19], (DEPTH, N_BRANCH, BRANCH_W, D_MODEL), BRANCH_W ** -0.5),
        "w_out": nrm(ks[20], (DEPTH, D_MODEL, D_MODEL), D_MODEL ** -0.5 * res_scale),
        "ffn_norm_g": gain(ks[21], (DEPTH, D_MODEL)),
        "w_up": nrm(ks[22], (DEPTH, D_MODEL, 2 * D_FF), D_MODEL ** -0.5),
        "ffn_conv_w": nrm(ks[23], (DEPTH, FFN_CONV, D_FF), FFN_CONV ** -0.5),
        "ffn_conv_b": nrm(ks[24], (DEPTH, D_FF), 0.01),
        "w_down": nrm(ks[25], (DEPTH, D_FF, D_MODEL), D_FF ** -0.5 * res_scale),
    }


def reference(x, mem, attn_norm_g, mem_norm_g, w_in, b_forget, fox_q_norm_g, fox_k_norm_g,
              lru_conv_w, lru_conv_b, lru_w_a, lru_b_a, lru_w_x, lru_b_x, lru_lambda,
              w_mem_kv, mem_q_norm_g, mem_k_norm_g, b_gate, w_branch, w_out,
              ffn_norm_g, w_up, ffn_conv_w, ffn_conv_b, w_down):
    f32 = jnp.float32
    cols = _column_slices()
    b, s, _ = x.shape
    for l in range(DEPTH):
        h = _rms(x, attn_norm_g[l])
        w_l = w_in[l]

        def proj(name):
            lo, hi = cols[name]
            return h @ w_l[:, lo:hi]

        fq = _rms(_heads(proj("fox_q"), FOX_HEADS), fox_q_norm_g[l]).astype(f32)
        fk = _rms(_heads(proj("fox_k"), FOX_HEADS), fox_k_norm_g[l]).astype(f32)
        fv = _heads(proj("fox_v"), FOX_HEADS).astype(f32)
        log_f = jax.nn.log_sigmoid(proj("fox_f").astype(f32) + b_forget[l]).transpose(0, 2, 1)
        y_fox = _merge_heads(_fox_attention(fq, fk, fv, log_f))

        y_lru = _rg_lru_branch(proj("lru_x"), proj("lru_g"), lru_conv_w[l], lru_conv_b[l],
                               lru_w_a[l], lru_b_a[l], lru_w_x[l], lru_b_x[l], lru_lambda[l])

        sq = _heads(proj("sb_q"), SB_HEADS).astype(f32)
        sk = _heads(proj("sb_k"), SB_HEADS).astype(f32)
        sv = _heads(proj("sb_v"), SB_HEADS).astype(f32)
        y_sb = _merge_heads(_stick_breaking_attention(sq, sk, sv))

        mem_n = _rms(mem, mem_norm_g[l])
        w_kv = w_mem_kv[l]
        mk = _rms(_heads(mem_n @ w_kv[:, :MEM_W], MEM_HEADS), mem_k_norm_g[l]).astype(f32)
        mv = _heads(mem_n @ w_kv[:, MEM_W:], MEM_HEADS).astype(f32)
        mq = _rms(_heads(proj("mem_q"), MEM_HEADS), mem_q_norm_g[l]).astype(f32)
        y_mem = _merge_heads(_mem_attention(mq, mk, mv))

        branches = jnp.stack([y_fox, y_lru, y_sb, y_mem], axis=2).astype(x.dtype)
        gates = jax.nn.sigmoid(proj("gates").reshape(b, s, N_BRANCH, D_MODEL) + b_gate[l])
        projected = jnp.einsum('bsnc,ncd->bsnd', branches, w_branch[l])
        mixed = jnp.sum(gates * projected, axis=2)
        x = x + mixed @ w_out[l]

        h2 = _rms(x, ffn_norm_g[l])
        w_u = w_up[l]
        gate_pre = _causal_dwconv(h2 @ w_u[:, :D_FF], ffn_conv_w[l], ffn_conv_b[l])
        val = h2 @ w_u[:, D_FF:]
        x = x + (jax.nn.silu(gate_pre) * val) @ w_down[l]
    return x
```

```python
import contextlib
import numpy as np
import ml_dtypes
import concourse.bass as bass
import concourse.mybir as mybir
from concourse.bass_utils import run_bass_kernel_spmd

F32 = mybir.dt.float32
BF16 = mybir.dt.bfloat16
AF = mybir.ActivationFunctionType
ALU = mybir.AluOpType

ENGS = ("pe", "act", "dve", "pool", "sp")
NDMASEM = 8

D = 1024
S_LEN = 8192
TOK = 2048
NT = 4
DFF = 2816
NCC = 22
N_IN = 13328
EPS = 1e-6
COLS = dict(fox_q=0, fox_k=1024, fox_v=2048, fox_f=3072, lru_x=3088, lru_g=4112,
            sb_q=5136, sb_k=6160, sb_v=7184, mem_q=8208, gates=9232)


class Buf:
    __slots__ = ("name", "w", "r", "pre")

    def __init__(self, name=""):
        self.name = name
        self.w = []
        self.r = {}
        self.pre = []


def I(name, **kw):
    return (name, kw)


class Ins:
    __slots__ = ("eng", "fn", "deps", "signal", "sem", "val", "dma", "slot")

    def __init__(self, eng, fn, dma):
        self.eng = eng
        self.fn = fn
        self.deps = []
        self.signal = False
        self.sem = None
        self.val = 0
        self.dma = dma
        self.slot = -1


class Sched:
    def __init__(self, nc):
        self.nc = nc
        self.q = {e: [] for e in ENGS}
        self.ndma = {e: 0 for e in ENGS}
        self.uid = 0

    def add(self, eng, fn, reads=(), writes=(), dma=False, join=False):
        ins = Ins(eng, fn, dma)
        deps = {}
        for b in reads:
            for d in b.w:
                deps[id(d)] = d
        for b in writes:
            if join:
                for d in b.pre:
                    deps[id(d)] = d
            else:
                pre = list(b.w) + list(b.r.values())
                for d in pre:
                    deps[id(d)] = d
                b.pre = pre
        for b in writes:
            if join:
                b.w.append(ins)
            else:
                b.w = [ins]
                b.r = {}
        for b in reads:
            if dma:
                self.uid += 1
                b.r[("dma", self.uid)] = ins
            else:
                b.r[eng] = ins
        if dma:
            n = self.ndma[eng]
            ins.slot = n % NDMASEM
            self.ndma[eng] = n + 1
        deps.pop(id(ins), None)
        ins.deps = list(deps.values())
        for d in ins.deps:
            if not (eng == "pe" and d.eng == "pe" and not d.dma):
                d.signal = True
        self.q[eng].append(ins)
        return ins

    def pe(self, fn, reads=(), writes=(), join=False):
        return self.add("pe", fn, reads, writes, join=join)

    def act(self, fn, reads=(), writes=(), join=False):
        return self.add("act", fn, reads, writes, join=join)

    def dve(self, fn, reads=(), writes=(), join=False):
        return self.add("dve", fn, reads, writes, join=join)

    def pool(self, fn, reads=(), writes=(), join=False):
        return self.add("pool", fn, reads, writes, join=join)

    def dma(self, fn, reads=(), writes=(), q="sp", join=False):
        return self.add(q, fn, reads, writes, dma=True, join=join)

    def emit(self, st, final_wait=()):
        nc = self.nc
        esem = {e: st.enter_context(nc.semaphore("tl_" + e)) for e in ENGS}
        dsem = {e: [st.enter_context(nc.semaphore("dm_%s%d" % (e, k))) for k in range(NDMASEM)]
                for e in ENGS if self.ndma[e] > 0}
        for e in ENGS:
            cnt = 0
            dcnt = [0] * NDMASEM
            prev = [None] * NDMASEM
            for ins in self.q[e]:
                if ins.dma:
                    k = ins.slot
                    dcnt[k] += 16
                    ins.sem = dsem[e][k]
                    ins.val = dcnt[k]
                    if prev[k] is not None:
                        ins.deps.append(prev[k])
                    prev[k] = ins
                elif ins.signal:
                    cnt += 1
                    ins.sem = esem[e]
                    ins.val = cnt
        fin = list(final_wait)
        block = st.enter_context(nc.Block())
        engobj = {"pe": block.tensor, "act": block.scalar, "dve": block.vector,
                  "pool": block.gpsimd, "sp": block.sync}

        def make(e):
            def body(eng):
                waited = {}

                def dowaits(deps):
                    need = {}
                    for d in deps:
                        if d.sem is None:
                            continue
                        if (not d.dma) and d.eng == e and e == "pe":
                            continue
                        k = id(d.sem)
                        if k not in need or need[k][1] < d.val:
                            need[k] = (d.sem, d.val)
                    for k, (s, v) in need.items():
                        if waited.get(k, 0) >= v:
                            continue
                        eng.wait_ge(s, v)
                        waited[k] = v

                for ins in self.q[e]:
                    dowaits(ins.deps)
                    if ins.fn is None:
                        continue
                    r = getattr(eng, ins.fn[0])(**ins.fn[1]) if isinstance(ins.fn, tuple) else ins.fn(eng)
                    if ins.dma:
                        r.then_inc(ins.sem, 16)
                    elif ins.signal:
                        r.then_inc(ins.sem, 1)
                if e == "sp":
                    dowaits(fin)
            return body

        for e in ENGS:
            if self.q[e] or (e == "sp" and fin):
                engobj[e](make(e))


class Ctx:
    def __init__(self):
        self.nc = bass.Bass("TRN2", target_bir_lowering=False)
        self.st = contextlib.ExitStack()
        self.S = Sched(self.nc)
        self.n = 0
        self.outs = []

    def din(self, name, shape, dt=F32):
        return self.nc.dram_tensor(name, list(shape), dt, kind="ExternalInput").ap()

    def dout(self, name, shape, dt=F32):
        return self.nc.dram_tensor(name, list(shape), dt, kind="ExternalOutput").ap()

    def sb(self, shape, dt, name="t"):
        self.n += 1
        return self.st.enter_context(self.nc.sbuf_tensor("%s_%d" % (name, self.n), list(shape), dt))

    def psum(self):
        self.n += 1
        return self.st.enter_context(self.nc.psum_tensor("ps_%d" % self.n, [128, 512], F32))

    def store(self, dst, src, reads):
        ins = self.S.dma(lambda e: e.dma_start(out=dst, in_=src), reads=reads)
        self.outs.append(ins)
        return ins

    def finish(self):
        self.S.emit(self.st, final_wait=self.outs)
        self.st.close()
        return self.nc


class Rot:
    def __init__(self, c, shape, dt, n, name):
        self.t = [c.sb(shape, dt, name) for _ in range(n)]
        self.b = [Buf(name) for _ in range(n)]
        self.i = 0

    def next(self):
        k = self.i % len(self.t)
        self.i += 1
        return self.t[k], self.b[k]


class PsRot:
    def __init__(self, c, n):
        self.t = [c.psum() for _ in range(n)]
        self.b = [Buf("ps") for _ in range(n)]
        self.i = 0

    def next(self):
        k = self.i % len(self.t)
        self.i += 1
        return self.t[k], self.b[k]


PV = {}
_o = 0
for _n, _w in (("attn_g", 8), ("mem_g", 8), ("ffn_g", 8), ("fq_g", 1), ("fk_g", 1), ("mq_g", 2), ("mk_g", 2),
               ("b_gate", 32), ("b_f", 1), ("ffn_cw", 66), ("ffn_cb", 22)):
    PV[_n] = _o
    _o += _w
NPV = _o
PL = dict(cw=0, cb=8, ba=10, bx=12, lam=14)
NPL = 16
CST = dict(ident=0, tri=128, strict=256, bones=384, ones=512, niu=640)
NCST = 768


def host_consts():
    c = np.zeros((128, NCST), np.float32)
    k = np.arange(128)[:, None]
    q = np.arange(128)[None, :]
    c[:, 0:128] = np.eye(128)
    c[:, 128:256] = (q >= k)
    c[:, 256:384] = (k < q)
    c[:, 384:512] = ((k // 64) == (q // 64))
    c[:, 512:640] = 1.0
    c[:, 640:768] = -(k >= q).astype(np.float32)
    return c


def host_pvec(inp, l):
    pv = np.zeros((128, NPV), np.float32)

    def cols(v):
        return np.ascontiguousarray(v.reshape(-1, 128).T)
    pv[:, PV["attn_g"]:PV["attn_g"] + 8] = cols(inp["attn_norm_g"][l])
    pv[:, PV["mem_g"]:PV["mem_g"] + 8] = cols(inp["mem_norm_g"][l])
    pv[:, PV["ffn_g"]:PV["ffn_g"] + 8] = cols(inp["ffn_norm_g"][l])
    pv[:, PV["fq_g"]] = np.tile(inp["fox_q_norm_g"][l], 2)
    pv[:, PV["fk_g"]] = np.tile(inp["fox_k_norm_g"][l], 2)
    pv[:, PV["mq_g"]:PV["mq_g"] + 2] = cols(inp["mem_q_norm_g"][l])
    pv[:, PV["mk_g"]:PV["mk_g"] + 2] = cols(inp["mem_k_norm_g"][l])
    pv[:, PV["b_gate"]:PV["b_gate"] + 32] = cols(inp["b_gate"][l].reshape(-1))
    pv[0:16, PV["b_f"]] = inp["b_forget"][l]
    for i in range(3):
        pv[:, PV["ffn_cw"] + 22 * i:PV["ffn_cw"] + 22 * (i + 1)] = cols(inp["ffn_conv_w"][l][i])
    pv[:, PV["ffn_cb"]:PV["ffn_cb"] + 22] = cols(inp["ffn_conv_b"][l])
    return pv


def host_plru(inp, l, g):
    pl = np.zeros((128, NPL), np.float32)
    sl = slice(256 * g, 256 * (g + 1))

    def cols(v):
        return np.ascontiguousarray(v.reshape(-1, 128).T)
    for i in range(4):
        pl[:, PL["cw"] + 2 * i:PL["cw"] + 2 * i + 2] = cols(inp["lru_conv_w"][l][i, sl])
    pl[:, PL["cb"]:PL["cb"] + 2] = cols(inp["lru_conv_b"][l][sl])
    pl[:, PL["ba"]:PL["ba"] + 2] = cols(inp["lru_b_a"][l][sl])
    pl[:, PL["bx"]:PL["bx"] + 2] = cols(inp["lru_b_x"][l][sl])
    pl[:, PL["lam"]:PL["lam"] + 2] = cols(inp["lru_lambda"][l][sl])
    wbd = np.zeros((128, 2, 2, 128), np.float32)
    for ci in range(2):
        for blk in range(2):
            n = 4 * g + 2 * ci + blk
            wbd[64 * blk:64 * blk + 64, 0, ci, 64 * blk:64 * blk + 64] = inp["lru_w_a"][l][n]
            wbd[64 * blk:64 * blk + 64, 1, ci, 64 * blk:64 * blk + 64] = inp["lru_w_x"][l][n]
    return pl, wbd.reshape(128, 512)


def load_consts(c, cst_d):
    S = c.S
    cst = c.sb([128, NCST], F32, "cst")
    cstb = c.sb([128, NCST], BF16, "cstb")
    B = Buf("cst")
    S.dma(lambda e: e.dma_start(out=cst[:], in_=cst_d), writes=[B])
    S.dve(lambda e: e.tensor_copy(out=cstb[:], in_=cst[:]), reads=[B], writes=[B])
    return cst, cstb, B


def rms_stats(c, PS, n_items, sq_fn, inv_n, ones_ap, Bcst, tmp_rot, N=512):
    S = c.S
    ps, Bp = PS.next()
    for i in range(n_items):
        ap, B = sq_fn(i)
        S.pe(lambda e, ap=ap, i=i: e.matmul(ps[:, 0:N], ones_ap, ap, start=(i == 0), stop=(i == n_items - 1)),
             reads=[B, Bcst], writes=[Bp])
    t, Bt = tmp_rot.next()
    S.dve(lambda e: e.tensor_scalar(out=t[:, 0:N], in0=ps[:, 0:N], scalar1=inv_n, scalar2=EPS, op0=ALU.mult, op1=ALU.add),
          reads=[Bp], writes=[Bt])
    S.act(lambda e: e.activation(out=t[:, 0:N], in_=t[:, 0:N], func=AF.Ln), reads=[Bt], writes=[Bt])
    S.act(lambda e: e.activation(out=t[:, 0:N], in_=t[:, 0:N], func=AF.Exp, scale=-0.5), reads=[Bt], writes=[Bt])
    return t, Bt


def wload(c, wrot, w_d, c0, ncols, nk=8):
    wt, Bw = wrot.next()
    src = w_d[:, c0:c0 + ncols].rearrange("(kc p) n -> p kc n", p=128)
    c.S.dma(lambda e: e.dma_start(out=wt[:, 0:nk, 0:ncols], in_=src), writes=[Bw], q="pool")
    return wt, Bw


def rmsnorm_tile(c, PS, xT, Bx, ts, gcol0, pv, Bpv, ones_f, Bcst, tmpf, sqr, hT, hts, Bh):
    S = c.S

    def sqf(kc):
        sq, Bs = sqr.next()
        S.act(lambda e: e.activation(out=sq[:], in_=xT[:, kc, ts], func=AF.Square), reads=[Bx], writes=[Bs])
        return sq[:], Bs
    rstd, Br = rms_stats(c, PS, 8, sqf, 1.0 / 1024, ones_f, Bcst, tmpf)
    for kc in range(8):
        S.dve(lambda e, kc=kc: e.scalar_tensor_tensor(out=hT[:, kc, hts], in0=xT[:, kc, ts],
                                                      scalar=pv[:, gcol0 + kc:gcol0 + kc + 1], in1=rstd[:],
                                                      op0=ALU.mult, op1=ALU.mult),
              reads=[Bx, Br, Bpv], writes=[Bh])


def phase_A(c, io):
    S = c.S
    cst, cstb, Bcst = load_consts(c, io["cst"])
    ones_f = cst[:, CST["ones"]:CST["ones"] + 128]
    bones_f = cst[:, CST["bones"]:CST["bones"] + 128]
    ones_b = cstb[:, CST["ones"]:CST["ones"] + 128]
    pv = c.sb([128, NPV], F32, "pv")
    Bpv = Buf("pv")
    S.dma(lambda e: e.dma_start(out=pv[:], in_=io["pv"]), writes=[Bpv])
    dpar = c.sb([128, 8], F32, "dpar")
    Bdp = Buf("dpar")
    S.dve(lambda e: e.tensor_scalar(out=dpar[:, 0:1], in0=pv[:, PV["fq_g"]:PV["fq_g"] + 1], scalar1=0.125, scalar2=None, op0=ALU.mult),
          reads=[Bpv], writes=[Bdp])
    S.dve(lambda e: e.tensor_scalar(out=dpar[:, 1:3], in0=pv[:, PV["mq_g"]:PV["mq_g"] + 2], scalar1=0.0625, scalar2=None, op0=ALU.mult),
          reads=[Bpv], writes=[Bdp])
    S.dve(lambda e: e.tensor_scalar(out=dpar[:, 3:4], in0=pv[:, PV["b_f"]:PV["b_f"] + 1], scalar1=-1.0, scalar2=None, op0=ALU.mult),
          reads=[Bpv], writes=[Bdp])

    PS = PsRot(c, 8)
    tmpf = Rot(c, [128, 512], F32, 4, "tmpf")
    sqr = Rot(c, [128, 512], F32, 4, "sqr")
    outb = Rot(c, [128, 512], BF16, 4, "outb")
    outf = Rot(c, [128, 512], F32, 3, "outf")
    wrot = Rot(c, [128, 8, 512], BF16, 3, "w")

    xT = c.sb([128, 8, TOK], F32, "xT")
    hT = c.sb([128, 8, TOK], BF16, "hT")
    Bx = [Buf("x%d" % t) for t in range(NT)]
    Bh = [Buf("h%d" % t) for t in range(NT)]
    xsrc = io["xT"].rearrange("(kc p) t -> p kc t", p=128)
    TS = [slice(t * 512, (t + 1) * 512) for t in range(NT)]
    for t in range(NT):
        S.dma(lambda e, t=t: e.dma_start(out=xT[:, :, TS[t]], in_=xsrc[:, :, TS[t]]), writes=[Bx[t]])
    for t in range(NT):
        rmsnorm_tile(c, PS, xT, Bx[t], TS[t], PV["attn_g"], pv, Bpv, ones_f, Bcst, tmpf, sqr, hT, TS[t], Bh[t])

    if io.get('stop', 99) <= 1:
        return
    memT = c.sb([128, 8, 256], F32, "memT")
    memn = c.sb([128, 8, 256], BF16, "memn")
    Bmem, Bmemn = Buf("mem"), Buf("memn")
    S.dma(lambda e: e.dma_start(out=memT[:], in_=io["memT"].rearrange("(kc p) m -> p kc m", p=128)), writes=[Bmem])

    def sqm(kc):
        sq, Bs = sqr.next()
        S.act(lambda e: e.activation(out=sq[:, 0:256], in_=memT[:, kc, :], func=AF.Square), reads=[Bmem], writes=[Bs])
        return sq[:, 0:256], Bs
    rstd, Br = rms_stats(c, PS, 8, sqm, 1.0 / 1024, ones_f, Bcst, tmpf, N=256)
    for kc in range(8):
        S.dve(lambda e, kc=kc, rstd=rstd: e.scalar_tensor_tensor(out=memn[:, kc, :], in0=memT[:, kc, :],
                                                      scalar=pv[:, PV["mem_g"] + kc:PV["mem_g"] + kc + 1], in1=rstd[:, 0:256],
                                                      op0=ALU.mult, op1=ALU.mult),
              reads=[Bmem, Br, Bpv], writes=[Bmemn])
    mkT = c.sb([128, 4, 2, 256], BF16, "mkT")
    mv = c.sb([128, 2, 1024], BF16, "mv")
    Bmk, Bmv = Buf("mk"), Buf("mv")
    for wi in range(2):
        wt, Bw = wload(c, wrot, io["w_kv"], wi * 512, 512)
        for hh in range(2):
            h = wi * 2 + hh
            pss = []
            for ci in range(2):
                ps, Bp = PS.next()
                cc = hh * 2 + ci
                for kc in range(8):
                    S.pe(lambda e, ps=ps, kc=kc, cc=cc, wt=wt: e.matmul(ps[:, 0:256], wt[:, kc, cc * 128:(cc + 1) * 128], memn[:, kc, :],
                                                                       start=(kc == 0), stop=(kc == 7)),
                         reads=[Bw, Bmemn], writes=[Bp])
                pss.append((ps, Bp))

            def sqk(i):
                sq, Bs = sqr.next()
                ps, Bp = pss[i]
                S.act(lambda e: e.activation(out=sq[:, 0:256], in_=ps[:, 0:256], func=AF.Square), reads=[Bp], writes=[Bs])
                return sq[:, 0:256], Bs
            rstd, Br = rms_stats(c, PS, 2, sqk, 1.0 / 256, ones_f, Bcst, tmpf, N=256)
            for ci in range(2):
                ps, Bp = pss[ci]
                S.dve(lambda e, ps=ps, ci=ci, h=h, rstd=rstd: e.scalar_tensor_tensor(
                    out=mkT[:, h, ci, :], in0=ps[:, 0:256], scalar=pv[:, PV["mk_g"] + ci:PV["mk_g"] + ci + 1],
                    in1=rstd[:, 0:256], op0=ALU.mult, op1=ALU.mult), reads=[Bp, Br, Bpv], writes=[Bmk])
    for wi in range(2):
        wt, Bw = wload(c, wrot, io["w_kv"], 1024 + wi * 512, 512)
        for mc in range(2):
            ps, Bp = PS.next()
            for kc in range(8):
                S.pe(lambda e, ps=ps, kc=kc, mc=mc, wt=wt: e.matmul(ps[:, 0:512], memn[:, kc, mc * 128:(mc + 1) * 128], wt[:, kc, 0:512],
                                                                   start=(kc == 0), stop=(kc == 7)),
                     reads=[Bw, Bmemn], writes=[Bp])
            S.act(lambda e, ps=ps, mc=mc, wi=wi: e.activation(out=mv[:, mc, wi * 512:(wi + 1) * 512], in_=ps[:, 0:512], func=AF.Copy),
                  reads=[Bp], writes=[Bmv])

    if 'dbg_mk' in io:
        c.store(io['dbg_mk'], mkT[:].rearrange("p h c m -> p (h c m)"), [Bmk])
        c.store(io['dbg_mv'], mv[:].rearrange("p c d -> p (c d)"), [Bmv])
        c.store(io['dbg_memn'], memn[:].rearrange("p c d -> p (c d)"), [Bmemn])
    if io.get('stop', 99) <= 2:
        return
    w_in = io["w_in"]

    def proj_fm(wt, Bw, cc, t):
        ps, Bp = PS.next()
        for kc in range(8):
            S.pe(lambda e, kc=kc: e.matmul(ps[:, 0:512], wt[:, kc, cc * 128:(cc + 1) * 128], hT[:, kc, TS[t]],
                                           start=(kc == 0), stop=(kc == 7)),
                 reads=[Bw, Bh[t]], writes=[Bp])
        return ps, Bp

    def fam_qknorm(col0, gcol_ap, Bg, dst):
        for wi in range(2):
            wt, Bw = wload(c, wrot, w_in, col0 + wi * 512, 512)
            for cc in range(4):
                row0 = (wi * 4 + cc) * 128
                for t in range(NT):
                    ps, Bp = proj_fm(wt, Bw, cc, t)

                    def sqf(i, ps=ps, Bp=Bp):
                        sq, Bs = sqr.next()
                        S.act(lambda e: e.activation(out=sq[:], in_=ps[:, 0:512], func=AF.Square), reads=[Bp], writes=[Bs])
                        return sq[:], Bs
                    rstd, Br = rms_stats(c, PS, 1, sqf, 1.0 / 64, bones_f, Bcst, tmpf)
                    ob, Bo = outb.next()
                    S.dve(lambda e, ps=ps, ob=ob, rstd=rstd: e.scalar_tensor_tensor(out=ob[:], in0=ps[:, 0:512], scalar=gcol_ap, in1=rstd[:],
                                                                                    op0=ALU.mult, op1=ALU.mult),
                          reads=[Bp, Br, Bg], writes=[Bo])
                    c.store(dst[row0:row0 + 128, TS[t]], ob[:], [Bo])

    def fam_copy(col0, dst, scale, bf):
        for wi in range(2):
            wt, Bw = wload(c, wrot, w_in, col0 + wi * 512, 512)
            for cc in range(4):
                row0 = (wi * 4 + cc) * 128
                for t in range(NT):
                    ps, Bp = proj_fm(wt, Bw, cc, t)
                    ob, Bo = (outb if bf else outf).next()
                    S.act(lambda e, ps=ps, ob=ob: e.activation(out=ob[:], in_=ps[:, 0:512], func=AF.Copy, scale=scale),
                          reads=[Bp], writes=[Bo])
                    c.store(dst[row0:row0 + 128, TS[t]], ob[:], [Bo])

    def fam_v(col0, dst):
        for wi in range(2):
            wt, Bw = wload(c, wrot, w_in, col0 + wi * 512, 512)
            for tb in range(16):
                ps, Bp = PS.next()
                for kc in range(8):
                    S.pe(lambda e, ps=ps, kc=kc, tb=tb, wt=wt: e.matmul(ps[:, 0:512], hT[:, kc, tb * 128:(tb + 1) * 128], wt[:, kc, 0:512],
                                                                       start=(kc == 0), stop=(kc == 7)),
                         reads=[Bw, Bh[tb // 4]], writes=[Bp])
                ob, Bo = outb.next()
                S.act(lambda e, ps=ps, ob=ob: e.activation(out=ob[:], in_=ps[:, 0:512], func=AF.Copy), reads=[Bp], writes=[Bo])
                for gi in range(2):
                    c.store(dst[wi * 2 + gi, tb * 128:(tb + 1) * 128, :], ob[:, gi * 256:(gi + 1) * 256], [Bo])

    fam_qknorm(COLS["fox_q"], dpar[:, 0:1], Bdp, io["fq"])
    if io.get('stop', 99) <= 3:
        return
    fam_qknorm(COLS["fox_k"], pv[:, PV["fk_g"]:PV["fk_g"] + 1], Bpv, io["fk"])
    if io.get('stop', 99) <= 4:
        return
    fam_v(COLS["fox_v"], io["fv"])
    if io.get('stop', 99) <= 5:
        return
    wf = c.sb([128, 8, 16], BF16, "wf")
    Bwf = Buf("wf")
    S.dma(lambda e: e.dma_start(out=wf[:], in_=w_in[:, COLS["fox_f"]:COLS["fox_f"] + 16].rearrange("(kc p) n -> p kc n", p=128)),
          writes=[Bwf], q="pool")
    for t in range(NT):
        ps, Bp = PS.next()
        for kc in range(8):
            S.pe(lambda e, ps=ps, kc=kc, t=t: e.matmul(ps[0:16, 0:512], wf[:, kc, 0:16], hT[:, kc, TS[t]], start=(kc == 0), stop=(kc == 7)),
                 reads=[Bwf, Bh[t]], writes=[Bp])
        of, Bo = outf.next()
        S.act(lambda e, ps=ps, of=of: e.activation(out=of[0:16, :], in_=ps[0:16, 0:512], func=AF.Exp, bias=dpar[0:16, 3:4], scale=-1.0),
              reads=[Bp, Bdp], writes=[Bo])
        S.act(lambda e, of=of: e.activation(out=of[0:16, :], in_=of[0:16, :], func=AF.Ln, bias=1.0), reads=[Bo], writes=[Bo])
        S.dve(lambda e, of=of: e.tensor_scalar(out=of[0:16, :], in0=of[0:16, :], scalar1=-1.0, scalar2=None, op0=ALU.mult),
              reads=[Bo], writes=[Bo])
        c.store(io["logf"][:, TS[t]], of[0:16, :], [Bo])
    if io.get('stop', 99) <= 6:
        return
    fam_copy(COLS["lru_x"], io["lx"], 1.0, False)
    fam_copy(COLS["lru_g"], io["lg"], 1.0, False)
    fam_copy(COLS["sb_q"], io["sq"], 0.125, True)
    fam_copy(COLS["sb_k"], io["sk"], 1.0, True)
    fam_v(COLS["sb_v"], io["sv"])
    if io.get('stop', 99) <= 7:
        return
    mqr = Rot(c, [128, 2, 512], BF16, 2, "mq")
    Er = Rot(c, [128, 2, 512], BF16, 2, "E")
    for wi in range(2):
        wt, Bw = wload(c, wrot, w_in, COLS["mem_q"] + wi * 512, 512)
        for hh in range(2):
            h = wi * 2 + hh
            for t in range(NT):
                pss = [proj_fm(wt, Bw, hh * 2 + ci, t) for ci in range(2)]

                def sqf(i, pss=pss):
                    sq, Bs = sqr.next()
                    ps, Bp = pss[i]
                    S.act(lambda e: e.activation(out=sq[:], in_=ps[:, 0:512], func=AF.Square), reads=[Bp], writes=[Bs])
                    return sq[:], Bs
                rstd, Br = rms_stats(c, PS, 2, sqf, 1.0 / 256, ones_f, Bcst, tmpf)
                mq, Bmq = mqr.next()
                for ci in range(2):
                    ps, Bp = pss[ci]
                    S.dve(lambda e, ps=ps, ci=ci, mq=mq, rstd=rstd: e.scalar_tensor_tensor(
                        out=mq[:, ci, :], in0=ps[:, 0:512], scalar=dpar[:, 1 + ci:2 + ci], in1=rstd[:], op0=ALU.mult, op1=ALU.mult),
                        reads=[Bp, Br, Bdp], writes=[Bmq])
                E, BE = Er.next()
                for mc in range(2):
                    ps, Bp = PS.next()
                    for dc in range(2):
                        S.pe(lambda e, ps=ps, dc=dc, mc=mc, h=h, mq=mq: e.matmul(ps[:, 0:512], mkT[:, h, dc, mc * 128:(mc + 1) * 128], mq[:, dc, :],
                                                                                start=(dc == 0), stop=(dc == 1)),
                             reads=[Bmk, Bmq], writes=[Bp])
                    S.act(lambda e, ps=ps, mc=mc, E=E: e.activation(out=E[:, mc, :], in_=ps[:, 0:512], func=AF.Exp), reads=[Bp], writes=[BE])
                if 'dbg_E' in io and h == 0 and t == 0:
                    c.store(io['dbg_E'], E[:].rearrange("p c q -> p (c q)"), [BE])
                    c.store(io['dbg_mq'], mq[:].rearrange("p c q -> p (c q)"), [Bmq])
                psd, Bpd = PS.next()
                for mc in range(2):
                    S.pe(lambda e, mc=mc, E=E, psd=psd: e.matmul(psd[:, 0:512], ones_b, E[:, mc, :], start=(mc == 0), stop=(mc == 1)),
                         reads=[BE, Bcst], writes=[Bpd])
                rden, Brd = tmpf.next()
                S.dve(lambda e, rden=rden, psd=psd: e.reciprocal(out=rden[:], in_=psd[:, 0:512]), reads=[Bpd], writes=[Brd])
                for dcp in range(2):
                    ps, Bp = PS.next()
                    for mc in range(2):
                        S.pe(lambda e, ps=ps, mc=mc, dcp=dcp, h=h, E=E: e.matmul(ps[:, 0:512], mv[:, mc, h * 256 + dcp * 128:h * 256 + (dcp + 1) * 128],
                                                                                E[:, mc, :], start=(mc == 0), stop=(mc == 1)),
                             reads=[BE, Bmv], writes=[Bp])
                    ob, Bo = outb.next()
                    S.dve(lambda e, ps=ps, ob=ob, rden=rden: e.tensor_tensor(out=ob[:], in0=ps[:, 0:512], in1=rden[:], op=ALU.mult),
                          reads=[Bp, Brd], writes=[Bo])
                    r0 = h * 256 + dcp * 128
                    c.store(io["ymem"][r0:r0 + 128, TS[t]], ob[:], [Bo])
    if io.get('stop', 99) <= 8:
        return
    for wi in range(io.get('ngw', 8)):
        wt, Bw = wload(c, wrot, w_in, COLS["gates"] + wi * 512, 512)
        for cc in range(4):
            gc = wi * 4 + cc
            for t in range(NT):
                ps, Bp = proj_fm(wt, Bw, cc, t)
                ob, Bo = outb.next()
                S.act(lambda e, ps=ps, ob=ob, gc=gc: e.activation(out=ob[:], in_=ps[:, 0:512], func=AF.Sigmoid,
                                                                 bias=pv[:, PV["b_gate"] + gc:PV["b_gate"] + gc + 1]),
                      reads=[Bp, Bpv], writes=[Bo])
                c.store(io["gates"][gc // 8][(gc % 8) * 128:(gc % 8 + 1) * 128, TS[t]], ob[:], [Bo])


def build_A(stop=99, ngw=8):
    c = Ctx()
    io = dict(stop=stop, ngw=ngw, xT=c.din("xT", [D, TOK]), memT=c.din("memT", [D, 256]), w_in=c.din("w_in", [D, N_IN]), w_kv=c.din("w_kv", [D, 2048]),
              pv=c.din("pv", [128, NPV]), cst=c.din("cst", [128, NCST]),
              fq=c.dout("fq", [D, TOK], BF16), fk=c.dout("fk", [D, TOK], BF16), fv=c.dout("fv", [4, TOK, 256], BF16),
              logf=c.dout("logf", [16, TOK]), lx=c.dout("lx", [D, TOK]), lg=c.dout("lg", [D, TOK]),
              sq=c.dout("sq", [D, TOK], BF16), sk=c.dout("sk", [D, TOK], BF16), sv=c.dout("sv", [4, TOK, 256], BF16),
              ymem=c.dout("ymem", [D, TOK], BF16), gates=[c.dout("gates%d" % i, [D, TOK], BF16) for i in range(4)])
    if stop == 88:
        io.update(dbg_E=c.dout('dbg_E', [128, 1024], BF16), dbg_mq=c.dout('dbg_mq', [128, 1024], BF16))
    if stop <= 2:
        io.update(dbg_mk=c.dout('dbg_mk', [128, 2048], BF16), dbg_mv=c.dout('dbg_mv', [128, 2048], BF16), dbg_memn=c.dout('dbg_memn', [128, 2048], BF16))
    phase_A(c, io)
    return c.finish()


def phase_B(c, io):
    S = c.S
    cst, cstb, Bcst = load_consts(c, io["cst"])
    ident_f = cst[:, CST["ident"]:CST["ident"] + 128]
    ones_f = cst[:, CST["ones"]:CST["ones"] + 128]
    tri_b = cstb[:, CST["tri"]:CST["tri"] + 128]
    strict_b = cstb[:, CST["strict"]:CST["strict"] + 128]
    niu_b = cstb[:, CST["niu"]:CST["niu"] + 128]
    negones = c.sb([128, 128], BF16, "negones")
    Bno = Buf("negones")
    S.dve(I("tensor_scalar", out=negones[:], in0=cst[:, CST["ones"]:CST["ones"] + 128], scalar1=-1.0, scalar2=None, op0=ALU.mult),
          reads=[Bcst], writes=[Bno])
    parts = io.get("parts", ("fox", "sb", "lru"))

    accPS = PsRot(c, 2)
    sPS = PsRot(c, 4)
    bcPS = PsRot(c, 2)
    qr = Rot(c, [65, S_LEN], BF16, 2, "qt")
    kr = Rot(c, [65, S_LEN], BF16, 2, "kt")
    vr = Rot(c, [128, 64, 66], BF16, 2, "v1")
    Pr = Rot(c, [128, 512], BF16, 3, "P")
    yor = Rot(c, [64, 512], BF16, 2, "yo")

    def load_head(qd, kd, vd, h):
        qt, Bq = qr.next()
        kt, Bk = kr.next()
        v1, Bv = vr.next()
        for j in range(4):
            r0 = j * 256 + h * 64
            S.dma(I("dma_start", out=qt[0:64, j * 2048:(j + 1) * 2048], in_=qd[r0:r0 + 64, :]), writes=[Bq], join=(j > 0))
            S.dma(I("dma_start", out=kt[0:64, j * 2048:(j + 1) * 2048], in_=kd[r0:r0 + 64, :]), writes=[Bk], join=(j > 0))
            S.dma(I("dma_start", out=v1[:, j * 16:(j + 1) * 16, 0:64],
                    in_=vd[j * 2048:(j + 1) * 2048, h * 64:(h + 1) * 64].rearrange("(kb p) f -> p kb f", p=128)),
                  writes=[Bv], join=(j > 0))
        return qt, Bq, kt, Bk, v1, Bv

    if "fox" in parts:
        cumb = c.sb([4, S_LEN], BF16, "cumb")
        negF = c.sb([128, 256], F32, "negF")
        Bcumb, BnegF = Buf("cumb"), Buf("negF")
        lfr = Rot(c, [4, 1024], F32, 2, "lf")
        cur = Rot(c, [4, 1024], F32, 2, "cum")
        recr = Rot(c, [128, 512], F32, 2, "rec")
        bcsr = Rot(c, [64, 512], F32, 2, "bcs")
        prevcm, Bprev = None, None
        for jj in range(8):
            j, off = jj // 2, (jj % 2) * 1024
            lf, Blf = lfr.next()
            S.dma(I("dma_start", out=lf[:], in_=io["logf"][j * 4:(j + 1) * 4, off:off + 1024]), writes=[Blf])
            S.dve(I("tensor_scalar", out=lf[:], in0=lf[:], scalar1=0.5, scalar2=None, op0=ALU.mult), reads=[Blf], writes=[Blf])
            cm, Bcm = cur.next()
            init = 0.0 if jj == 0 else prevcm[:, 1023:1024]
            S.dve(I("tensor_tensor_scan", out=cm[:], data0=lf[:], data1=lf[:], initial=init, op0=ALU.add, op1=ALU.add),
                  reads=[Blf] + ([Bprev] if jj else []), writes=[Bcm])
            S.dve(I("tensor_copy", out=cumb[:, jj * 1024:(jj + 1) * 1024], in_=cm[:]), reads=[Bcm], writes=[Bcumb])
            pt, Bpt = sPS.next()
            for kb in range(8):
                S.pe(I("transpose", out=pt[:, kb * 4:(kb + 1) * 4], in_=cm[0:4, kb * 128:(kb + 1) * 128], identity=ident_f[0:4, 0:4]),
                     reads=[Bcm, Bcst], writes=[Bpt])
            S.dve(I("tensor_scalar", out=negF[:, jj * 32:(jj + 1) * 32], in0=pt[:, 0:32], scalar1=-1.0, scalar2=None, op0=ALU.mult),
                  reads=[Bpt], writes=[BnegF])
            prevcm, Bprev = cm, Bcm
        for h in range(4):
            qt, Bq, kt, Bk, v1, Bv = load_head(io["fq"], io["fk"], io["fv"], h)
            S.dma(I("dma_start", out=qt[64:65, :], in_=cumb[h:h + 1, :]), reads=[Bcumb], writes=[Bq], join=True)
            S.dve(I("memset", ap=kt[64:65, :], constant=1.0), writes=[Bk], join=True)
            S.dve(I("memset", ap=v1[:, :, 64:66], constant=1.0), writes=[Bv], join=True)
            for I_ in range(16):
                acc, Bacc = accPS.next()
                nJ = 4 * I_ + 4
                for J in range(nJ):
                    c0 = max(0, J - 4 * I_) * 128
                    sc, Bs = sPS.next()
                    S.pe(I("matmul", out=sc[:, c0:512], lhsT=kt[0:65, J * 128:(J + 1) * 128], rhs=qt[0:65, I_ * 512 + c0:(I_ + 1) * 512],
                           start=True, stop=True), reads=[Bk, Bq], writes=[Bs])
                    pt, Bp = Pr.next()
                    S.act(I("activation", out=pt[:, c0:512], in_=sc[:, c0:512], func=AF.Exp, bias=negF[:, J * 4 + h:J * 4 + h + 1]),
                          reads=[Bs, BnegF], writes=[Bp])
                    if J >= 4 * I_:
                        S.dve(I("tensor_tensor", out=pt[:, c0:c0 + 128], in0=pt[:, c0:c0 + 128], in1=tri_b, op=ALU.mult),
                              reads=[Bp, Bcst], writes=[Bp])
                    S.pe(I("matmul", out=acc[0:65, c0:512], lhsT=v1[:, J, 0:65], rhs=pt[:, c0:512], start=(J == 0), stop=(J == nJ - 1)),
                         reads=[Bv, Bp], writes=[Bacc])
                rec, Brec = recr.next()
                S.dve(I("reciprocal", out=rec[64:65, :], in_=acc[64:65, 0:512]), reads=[Bacc], writes=[Brec])
                bc, Bbc = bcPS.next()
                S.pe(I("matmul", out=bc[0:64, 0:512], lhsT=ones_f[64:65, 0:64], rhs=rec[64:65, :], start=True, stop=True),
                     reads=[Brec, Bcst], writes=[Bbc])
                bcs, Bbcs = bcsr.next()
                S.act(I("activation", out=bcs[:], in_=bc[0:64, 0:512], func=AF.Copy), reads=[Bbc], writes=[Bbcs])
                yo, Byo = yor.next()
                S.dve(I("tensor_tensor", out=yo[:], in0=acc[0:64, 0:512], in1=bcs[:], op=ALU.mult), reads=[Bacc, Bbcs], writes=[Byo])
                r0 = (I_ // 4) * 256 + h * 64
                c.store(io["yfox"][r0:r0 + 64, (I_ % 4) * 512:(I_ % 4 + 1) * 512], yo[:], [Byo])

    if "sb" in parts:
        er = Rot(c, [128, 512], F32, 2, "ee")
        spr = Rot(c, [128, 512], BF16, 3, "sp")
        saccr = Rot(c, [128, 512], F32, 2, "sacc")
        sbfr = Rot(c, [128, 512], BF16, 2, "sbf")
        for h in range(4):
            qt, Bq, kt, Bk, v1, Bv = load_head(io["sq"], io["sk"], io["sv"], h)
            for I_ in range(16):
                acc, Bacc = accPS.next()
                sacc, Bsa = saccr.next()
                sbf, Bsb = sbfr.next()
                S.dve(I("memset", ap=sacc[:], constant=0.0), writes=[Bsa])
                nJ = 4 * I_ + 4
                for u, J in enumerate(range(nJ - 1, -1, -1)):
                    c0 = max(0, J - 4 * I_) * 128
                    kblk = kt[0:64, J * 128:(J + 1) * 128]
                    qblk = qt[0:64, I_ * 512 + c0:(I_ + 1) * 512]
                    z, Bz = sPS.next()
                    S.pe(I("matmul", out=z[:, c0:512], lhsT=kblk, rhs=qblk, start=True, stop=True), reads=[Bk, Bq], writes=[Bz])
                    ee, Be = er.next()
                    S.act(I("activation", out=ee[:, c0:512], in_=z[:, c0:512], func=AF.Exp), reads=[Bz], writes=[Be])
                    sp, Bsp = spr.next()
                    S.act(I("activation", out=sp[:, c0:512], in_=ee[:, c0:512], func=AF.Ln, bias=1.0), reads=[Be], writes=[Bsp])
                    if J >= 4 * I_:
                        S.dve(I("tensor_tensor", out=sp[:, c0:c0 + 128], in0=sp[:, c0:c0 + 128], in1=strict_b, op=ALU.mult),
                              reads=[Bsp, Bcst], writes=[Bsp])
                    la, Bla = sPS.next()
                    S.pe(I("matmul", out=la[:, c0:512], lhsT=kblk, rhs=qblk, start=True, stop=False), reads=[Bk, Bq], writes=[Bla])
                    S.pe(I("matmul", out=la[:, c0:512], lhsT=niu_b, rhs=sp[:, c0:512], start=False, stop=(u == 0)),
                         reads=[Bsp, Bcst], writes=[Bla])
                    if u > 0:
                        S.pe(I("matmul", out=la[:, c0:512], lhsT=negones[:], rhs=sbf[:, c0:512], start=False, stop=True),
                             reads=[Bsb, Bno], writes=[Bla])
                    at, Bat = Pr.next()
                    S.act(I("activation", out=at[:, c0:512], in_=la[:, c0:512], func=AF.Exp), reads=[Bla], writes=[Bat])
                    if J >= 4 * I_:
                        S.dve(I("tensor_tensor", out=at[:, c0:c0 + 128], in0=at[:, c0:c0 + 128], in1=strict_b, op=ALU.mult),
                              reads=[Bat, Bcst], writes=[Bat])
                    S.pe(I("matmul", out=acc[0:64, c0:512], lhsT=v1[:, J, 0:64], rhs=at[:, c0:512], start=(u == 0), stop=(u == nJ - 1)),
                         reads=[Bv, Bat], writes=[Bacc])
                    if J > 0:
                        c0n = max(0, J - 1 - 4 * I_) * 128
                        S.dve(I("tensor_tensor", out=sacc[:, c0:512], in0=sacc[:, c0:512], in1=sp[:, c0:512], op=ALU.add),
                              reads=[Bsa, Bsp], writes=[Bsa])
                        S.dve(I("tensor_copy", out=sbf[:, c0n:512], in_=sacc[:, c0n:512]), reads=[Bsa], writes=[Bsb])
                yo, Byo = yor.next()
                S.act(I("activation", out=yo[:], in_=acc[0:64, 0:512], func=AF.Copy), reads=[Bacc], writes=[Byo])
                r0 = (I_ // 4) * 256 + h * 64
                c.store(io["ysb"][r0:r0 + 64, (I_ % 4) * 512:(I_ % 4 + 1) * 512], yo[:], [Byo])

    if "lru" in parts:
        LT = 1024
        pl = c.sb([128, NPL], F32, "pl")
        Bpl = Buf("pl")
        S.dma(I("dma_start", out=pl[:], in_=io["pl"]), writes=[Bpl])
        wbd = c.sb([128, 512], BF16, "wbd")
        Bwbd = Buf("wbd")
        S.dma(I("dma_start", out=wbd[:], in_=io["wbd"]), writes=[Bwbd], q="pool")
        cpar = c.sb([128, 2], F32, "cpar")
        Bcp = Buf("cpar")
        S.act(I("activation", out=cpar[:], in_=pl[:, PL["lam"]:PL["lam"] + 2], func=AF.Exp, scale=-1.0), reads=[Bpl], writes=[Bcp])
        S.act(I("activation", out=cpar[:], in_=cpar[:], func=AF.Ln, bias=1.0), reads=[Bcp], writes=[Bcp])
        S.dve(I("tensor_scalar", out=cpar[:], in0=cpar[:], scalar1=-8.0, scalar2=None, op0=ALU.mult), reads=[Bcp], writes=[Bcp])
        xinr = Rot(c, [128, 3 + LT], F32, 2, "xin")
        gr = Rot(c, [128, LT], F32, 2, "g")
        hsr = Rot(c, [128, LT], F32, 2, "hs")
        xc, Bxc = c.sb([128, LT], F32, "xc"), Buf("xc")
        xcb, Bxcb = c.sb([128, LT], BF16, "xcb"), Buf("xcb")
        rr, Brr = c.sb([128, LT], F32, "r"), Buf("r")
        ii, Bii = c.sb([128, LT], F32, "i"), Buf("i")
        aa, Baa = c.sb([128, LT], F32, "a"), Buf("a")
        tmp, Btmp = c.sb([128, LT], F32, "tmp"), Buf("tmp")
        t2, Bt2 = c.sb([128, LT], F32, "t2"), Buf("t2")
        ybr = Rot(c, [128, LT], BF16, 2, "yb")
        for ci in range(2):
            pxin, Bpx, phs, Bph = None, None, None, None
            for jt in range(S_LEN // LT):
                j, off = (jt * LT) // 2048, (jt * LT) % 2048
                r0 = j * 256 + ci * 128
                xin, Bxi = xinr.next()
                S.dma(I("dma_start", out=xin[:, 3:3 + LT], in_=io["lx"][r0:r0 + 128, off:off + LT]), writes=[Bxi])
                if jt == 0:
                    S.dve(I("memset", ap=xin[:, 0:3], constant=0.0), writes=[Bxi], join=True)
                else:
                    S.dve(I("tensor_copy", out=xin[:, 0:3], in_=pxin[:, LT:LT + 3]), reads=[Bpx], writes=[Bxi], join=True)
                g, Bg = gr.next()
                S.dma(I("dma_start", out=g[:], in_=io["lg"][r0:r0 + 128, off:off + LT]), writes=[Bg])
                cw = lambda i: pl[:, PL["cw"] + 2 * i + ci:PL["cw"] + 2 * i + ci + 1]
                S.dve(I("tensor_scalar", out=xc[:], in0=xin[:, 3:3 + LT], scalar1=cw(3), scalar2=pl[:, PL["cb"] + ci:PL["cb"] + ci + 1],
                        op0=ALU.mult, op1=ALU.add), reads=[Bxi, Bpl], writes=[Bxc])
                for i in range(3):
                    S.dve(I("scalar_tensor_tensor", out=xc[:], in0=xin[:, i:i + LT], scalar=cw(i), in1=xc[:], op0=ALU.mult, op1=ALU.add),
                          reads=[Bxi, Bpl, Bxc], writes=[Bxc])
                S.act(I("activation", out=xcb[:], in_=xc[:], func=AF.Copy), reads=[Bxc], writes=[Bxcb])
                for s in range(LT // 512):
                    ss = slice(s * 512, (s + 1) * 512)
                    for which, dst, Bd, bcol in ((0, rr, Brr, PL["ba"]), (1, ii, Bii, PL["bx"])):
                        ps, Bp = sPS.next()
                        wsl = wbd[:, (which * 2 + ci) * 128:(which * 2 + ci + 1) * 128]
                        S.pe(I("matmul", out=ps[:, 0:512], lhsT=wsl, rhs=xcb[:, ss], start=True, stop=True), reads=[Bwbd, Bxcb], writes=[Bp])
                        S.act(I("activation", out=dst[:, ss], in_=ps[:, 0:512], func=AF.Sigmoid, bias=pl[:, bcol + ci:bcol + ci + 1]),
                              reads=[Bp, Bpl], writes=[Bd], join=(s > 0))
                S.act(I("activation", out=aa[:], in_=rr[:], func=AF.Exp, scale=cpar[:, ci:ci + 1]), reads=[Brr, Bcp], writes=[Baa])
                S.dve(I("tensor_scalar", out=aa[:], in0=aa[:], scalar1=1.0, scalar2=None, op0=ALU.min), reads=[Baa], writes=[Baa])
                S.dve(I("tensor_tensor", out=tmp[:], in0=aa[:], in1=aa[:], op=ALU.mult), reads=[Baa], writes=[Btmp])
                S.act(I("activation", out=tmp[:], in_=tmp[:], func=AF.Sqrt, bias=1.0, scale=-1.0), reads=[Btmp], writes=[Btmp])
                S.dve(I("tensor_tensor", out=tmp[:], in0=tmp[:], in1=ii[:], op=ALU.mult), reads=[Btmp, Bii], writes=[Btmp])
                S.dve(I("tensor_tensor", out=tmp[:], in0=tmp[:], in1=xc[:], op=ALU.mult), reads=[Btmp, Bxc], writes=[Btmp])
                hs, Bhs = hsr.next()
                init = 0.0 if jt == 0 else phs[:, LT - 1:LT]
                S.dve(I("tensor_tensor_scan", out=hs[:], data0=aa[:], data1=tmp[:], initial=init, op0=ALU.mult, op1=ALU.add),
                      reads=[Baa, Btmp] + ([Bph] if jt else []), writes=[Bhs])
                S.dve(I("tensor_tensor", out=t2[:], in0=g[:], in1=g[:], op=ALU.mult), reads=[Bg], writes=[Bt2])
                S.dve(I("tensor_scalar", out=t2[:], in0=t2[:], scalar1=0.0713548163, scalar2=1.5957691216, op0=ALU.mult, op1=ALU.add),
                      reads=[Bt2], writes=[Bt2])
                S.dve(I("tensor_tensor", out=t2[:], in0=t2[:], in1=g[:], op=ALU.mult), reads=[Bt2, Bg], writes=[Bt2])
                S.act(I("activation", out=t2[:], in_=t2[:], func=AF.Sigmoid), reads=[Bt2], writes=[Bt2])
                S.dve(I("tensor_tensor", out=t2[:], in0=t2[:], in1=g[:], op=ALU.mult), reads=[Bt2, Bg], writes=[Bt2])
                yb, Byb = ybr.next()
                S.dve(I("tensor_tensor", out=yb[:], in0=t2[:], in1=hs[:], op=ALU.mult), reads=[Bt2, Bhs], writes=[Byb])
                c.store(io["ylru"][r0:r0 + 128, off:off + LT], yb[:], [Byb])
                pxin, Bpx, phs, Bph = xin, Bxi, hs, Bhs


def build_B(parts=("fox", "sb", "lru")):
    c = Ctx()
    io = dict(parts=parts, cst=c.din("cst", [128, NCST]),
              fq=c.din("fq", [D, TOK], BF16), fk=c.din("fk", [D, TOK], BF16), fv=c.din("fv", [S_LEN, 256], BF16),
              logf=c.din("logf", [16, TOK]), sq=c.din("sq", [D, TOK], BF16), sk=c.din("sk", [D, TOK], BF16),
              sv=c.din("sv", [S_LEN, 256], BF16), lx=c.din("lx", [D, TOK]), lg=c.din("lg", [D, TOK]),
              pl=c.din("pl", [128, NPL]), wbd=c.din("wbd", [128, 512]),
              yfox=c.dout("yfox", [D, TOK], BF16), ysb=c.dout("ysb", [D, TOK], BF16), ylru=c.dout("ylru", [D, TOK], BF16))
    phase_B(c, io)
    return c.finish()


def load_x(c, xd):
    S = c.S
    xT = c.sb([128, 8, TOK], F32, "xT")
    Bx = [Buf("x%d" % t) for t in range(NT)]
    xsrc = xd.rearrange("(kc p) t -> p kc t", p=128)
    for t in range(NT):
        S.dma(I("dma_start", out=xT[:, :, t * 512:(t + 1) * 512], in_=xsrc[:, :, t * 512:(t + 1) * 512]), writes=[Bx[t]])
    return xT, Bx


def store_x(c, xT, Bx, xd):
    xdst = xd.rearrange("(kc p) t -> p kc t", p=128)
    for t in range(NT):
        c.store(xdst[:, :, t * 512:(t + 1) * 512], xT[:, :, t * 512:(t + 1) * 512], [Bx[t]])


def phase_C1(c, io, xT, Bx):
    S = c.S
    cst, cstb, Bcst = load_consts(c, io["cst"])
    ones_f = cst[:, CST["ones"]:CST["ones"] + 128]
    pv = c.sb([128, NPV], F32, "pv")
    Bpv = Buf("pv")
    S.dma(I("dma_start", out=pv[:], in_=io["pv"]), writes=[Bpv])
    PS = PsRot(c, 8)
    tmpf = Rot(c, [128, 512], F32, 3, "tmpf")
    sqr = Rot(c, [128, 512], F32, 3, "sqr")
    wrot = Rot(c, [128, 8, 512], BF16, 3, "w")
    ytr = Rot(c, [128, 8, 512], BF16, 2, "yt")
    gtr = Rot(c, [128, 8, 512], BF16, 2, "gt")
    mixed, Bmix = c.sb([128, 8, 512], F32, "mixed"), Buf("mixed")
    mixb, Bmixb = c.sb([128, 8, 512], BF16, "mixb"), Buf("mixb")
    h2r = Rot(c, [128, 8, 512], BF16, 2, "h2")
    ysrc = [io["yfox"], io["ylru"], io["ysb"], io["ymem"]]
    for t in range(NT):
        ts = slice(t * 512, (t + 1) * 512)
        for i in range(4):
            yt, Byt = ytr.next()
            S.dma(I("dma_start", out=yt[:], in_=ysrc[i][:, ts].rearrange("(kc p) t -> p kc t", p=128)), writes=[Byt])
            gt, Bgt = gtr.next()
            S.dma(I("dma_start", out=gt[:], in_=io["gates"][i][:, ts].rearrange("(kc p) t -> p kc t", p=128)), writes=[Bgt])
            for half in range(2):
                wt, Bw = wload(c, wrot, io["w_branch"][i], half * 512, 512)
                for o4 in range(4):
                    oc = half * 4 + o4
                    ps, Bp = PS.next()
                    for kc in range(8):
                        S.pe(I("matmul", out=ps[:, 0:512], lhsT=wt[:, kc, o4 * 128:(o4 + 1) * 128], rhs=yt[:, kc, :],
                               start=(kc == 0), stop=(kc == 7)), reads=[Bw, Byt], writes=[Bp])
                    if i == 0:
                        S.dve(I("tensor_tensor", out=mixed[:, oc, :], in0=ps[:, 0:512], in1=gt[:, oc, :], op=ALU.mult),
                              reads=[Bp, Bgt], writes=[Bmix], join=(oc > 0))
                    else:
                        tm, Btm = tmpf.next()
                        S.dve(I("tensor_tensor", out=tm[:], in0=ps[:, 0:512], in1=gt[:, oc, :], op=ALU.mult), reads=[Bp, Bgt], writes=[Btm])
                        S.dve(I("tensor_tensor", out=mixed[:, oc, :], in0=mixed[:, oc, :], in1=tm[:], op=ALU.add),
                              reads=[Btm, Bmix], writes=[Bmix])
        S.act(I("activation", out=mixb[:], in_=mixed[:], func=AF.Copy), reads=[Bmix], writes=[Bmixb])
        for half in range(2):
            wt, Bw = wload(c, wrot, io["w_out"], half * 512, 512)
            for o4 in range(4):
                oc = half * 4 + o4
                ps, Bp = PS.next()
                for kc in range(8):
                    S.pe(I("matmul", out=ps[:, 0:512], lhsT=wt[:, kc, o4 * 128:(o4 + 1) * 128], rhs=mixb[:, kc, :],
                           start=(kc == 0), stop=(kc == 7)), reads=[Bw, Bmixb], writes=[Bp])
                S.dve(I("tensor_tensor", out=xT[:, oc, ts], in0=xT[:, oc, ts], in1=ps[:, 0:512], op=ALU.add), reads=[Bp, Bx[t]], writes=[Bx[t]])
        h2, Bh2 = h2r.next()
        rmsnorm_tile(c, PS, xT, Bx[t], ts, PV["ffn_g"], pv, Bpv, ones_f, Bcst, tmpf, sqr, h2, slice(0, 512), Bh2)
        c.store(io["h2"][:, ts].rearrange("(kc p) t -> p kc t", p=128), h2[:], [Bh2])


def phase_C2(c, io, xT, Bx):
    S = c.S
    pv = c.sb([128, NPV], F32, "pv")
    Bpv = Buf("pv")
    S.dma(I("dma_start", out=pv[:], in_=io["pv"]), writes=[Bpv])
    PS = PsRot(c, 8)
    wrot = Rot(c, [128, 8, 512], BF16, 4, "w")
    wdrot = Rot(c, [128, NCC, 512], BF16, 2, "wd")
    h2r = Rot(c, [128, 8, 514], BF16, 2, "h2e")
    Gr = Rot(c, [128, 514], F32, 3, "G")
    gpr = Rot(c, [128, 512], F32, 3, "gp")
    hid, Bhid = c.sb([128, NCC, 512], BF16, "hid"), Buf("hid")
    h2src = io["h2"].rearrange("(kc p) t -> p kc t", p=128)
    for t in range(NT):
        ts = slice(t * 512, (t + 1) * 512)
        h2e, Bh = h2r.next()
        S.dma(I("dma_start", out=h2e[:, :, 2:514], in_=h2src[:, :, ts]), writes=[Bh])
        hsrc = io["h2halo"].rearrange("(kc p) t -> p kc t", p=128) if t == 0 else h2src[:, :, t * 512 - 2:t * 512]
        S.dma(I("dma_start", out=h2e[:, :, 0:2], in_=hsrc), writes=[Bh], join=True)
        for w6 in range(6):
            ncol = 512 if w6 < 5 else 256
            wg, Bwg = wload(c, wrot, io["w_up"], w6 * 512, ncol)
            wv, Bwv = wload(c, wrot, io["w_up"], DFF + w6 * 512, ncol)
            for c4 in range(ncol // 128):
                cc = w6 * 4 + c4
                pg, Bpg = PS.next()
                for kc in range(8):
                    S.pe(I("matmul", out=pg[:, 0:512], lhsT=wg[:, kc, c4 * 128:(c4 + 1) * 128], rhs=h2e[:, kc, 2:514],
                           start=(kc == 0), stop=(kc == 7)), reads=[Bwg, Bh], writes=[Bpg])
                ph, Bph = PS.next()
                for kc in range(8):
                    S.pe(I("matmul", out=ph[:, 0:2], lhsT=wg[:, kc, c4 * 128:(c4 + 1) * 128], rhs=h2e[:, kc, 0:2],
                           start=(kc == 0), stop=(kc == 7)), reads=[Bwg, Bh], writes=[Bph])
                G, BG = Gr.next()
                S.act(I("activation", out=G[:, 2:514], in_=pg[:, 0:512], func=AF.Copy), reads=[Bpg], writes=[BG])
                S.act(I("activation", out=G[:, 0:2], in_=ph[:, 0:2], func=AF.Copy), reads=[Bph], writes=[BG], join=True)
                gp, Bgp = gpr.next()
                cw = lambda i: pv[:, PV["ffn_cw"] + NCC * i + cc:PV["ffn_cw"] + NCC * i + cc + 1]
                S.dve(I("tensor_scalar", out=gp[:], in0=G[:, 2:514], scalar1=cw(2), scalar2=pv[:, PV["ffn_cb"] + cc:PV["ffn_cb"] + cc + 1],
                        op0=ALU.mult, op1=ALU.add), reads=[BG, Bpv], writes=[Bgp])
                S.dve(I("scalar_tensor_tensor", out=gp[:], in0=G[:, 1:513], scalar=cw(1), in1=gp[:], op0=ALU.mult, op1=ALU.add),
                      reads=[BG, Bpv, Bgp], writes=[Bgp])
                S.dve(I("scalar_tensor_tensor", out=gp[:], in0=G[:, 0:512], scalar=cw(0), in1=gp[:], op0=ALU.mult, op1=ALU.add),
                      reads=[BG, Bpv, Bgp], writes=[Bgp])
                S.act(I("activation", out=gp[:], in_=gp[:], func=AF.Silu), reads=[Bgp], writes=[Bgp])
                pvv, Bpvv = PS.next()
                for kc in range(8):
                    S.pe(I("matmul", out=pvv[:, 0:512], lhsT=wv[:, kc, c4 * 128:(c4 + 1) * 128], rhs=h2e[:, kc, 2:514],
                           start=(kc == 0), stop=(kc == 7)), reads=[Bwv, Bh], writes=[Bpvv])
                S.dve(I("tensor_tensor", out=hid[:, cc, :], in0=pvv[:, 0:512], in1=gp[:], op=ALU.mult), reads=[Bpvv, Bgp], writes=[Bhid],
                      join=(cc > 0))
        for half in range(2):
            wd, Bwd = wload(c, wdrot, io["w_down"], half * 512, 512, nk=NCC)
            for o4 in range(4):
                oc = half * 4 + o4
                ps, Bp = PS.next()
                for cc in range(NCC):
                    S.pe(I("matmul", out=ps[:, 0:512], lhsT=wd[:, cc, o4 * 128:(o4 + 1) * 128], rhs=hid[:, cc, :],
                           start=(cc == 0), stop=(cc == NCC - 1)), reads=[Bwd, Bhid], writes=[Bp])
                S.dve(I("tensor_tensor", out=xT[:, oc, ts], in0=xT[:, oc, ts], in1=ps[:, 0:512], op=ALU.add), reads=[Bp, Bx[t]], writes=[Bx[t]])


def build_C1():
    c = Ctx()
    io = dict(cst=c.din("cst", [128, NCST]), pv=c.din("pv", [128, NPV]), xT=c.din("xT", [D, TOK]),
              yfox=c.din("yfox", [D, TOK], BF16), ylru=c.din("ylru", [D, TOK], BF16), ysb=c.din("ysb", [D, TOK], BF16),
              ymem=c.din("ymem", [D, TOK], BF16), gates=[c.din("gates%d" % i, [D, TOK], BF16) for i in range(4)],
              w_branch=[c.din("w_branch%d" % i, [D, D]) for i in range(4)], w_out=c.din("w_out", [D, D]),
              xo=c.dout("xo", [D, TOK]), h2=c.dout("h2", [D, TOK], BF16))
    xT, Bx = load_x(c, io["xT"])
    phase_C1(c, io, xT, Bx)
    store_x(c, xT, Bx, io["xo"])
    return c.finish()


def build_C2():
    c = Ctx()
    io = dict(pv=c.din("pv", [128, NPV]), xT=c.din("xT", [D, TOK]), h2=c.din("h2", [D, TOK], BF16), h2halo=c.din("h2halo", [D, 2], BF16),
              w_up=c.din("w_up", [D, 2 * DFF]), w_down=c.din("w_down", [DFF, D]), xo=c.dout("xo", [D, TOK]))
    xT, Bx = load_x(c, io["xT"])
    phase_C2(c, io, xT, Bx)
    store_x(c, xT, Bx, io["xo"])
    return c.finish()


_NC = {}


def _prog(name, builder):
    if name not in _NC:
        _NC[name] = builder()
    return _NC[name]


def _run(nc, maps):
    return run_bass_kernel_spmd(nc, maps, core_ids=list(range(8))).results


def kernel(_nlayers=4, **inp):
    inp = {k: np.asarray(v) for k, v in inp.items()}
    bf = ml_dtypes.bfloat16
    cst = host_consts()
    xs = [np.ascontiguousarray(inp["x"][c // 4, (c % 4) * TOK:(c % 4 + 1) * TOK].T) for c in range(8)]
    memT = [np.ascontiguousarray(inp["mem"][b].T) for b in range(2)]
    zhalo = np.zeros((D, 2), bf)
    for l in range(_nlayers):
        pv = host_pvec(inp, l)
        w_in = np.ascontiguousarray(inp["w_in"][l])
        w_kv = np.ascontiguousarray(inp["w_mem_kv"][l])
        rA = _run(_prog("A", build_A), [dict(xT=xs[c], memT=memT[c // 4], w_in=w_in, w_kv=w_kv, pv=pv, cst=cst) for c in range(8)])
        mapsB = []
        for c in range(8):
            b, g = c // 4, c % 4
            src = [rA[b * 4 + j] for j in range(4)]

            def rows(key, n, src=src, g=g):
                return np.ascontiguousarray(np.concatenate([np.asarray(s[key])[g * n:(g + 1) * n] for s in src], 0))

            def toks(key, src=src, g=g):
                return np.ascontiguousarray(np.concatenate([np.asarray(s[key])[g] for s in src], 0))
            pl, wbd = host_plru(inp, l, g)
            mapsB.append(dict(cst=cst, fq=rows("fq", 256), fk=rows("fk", 256), fv=toks("fv"), logf=rows("logf", 4),
                              sq=rows("sq", 256), sk=rows("sk", 256), sv=toks("sv"), lx=rows("lx", 256), lg=rows("lg", 256),
                              pl=pl, wbd=wbd))
        rB = _run(_prog("B", build_B), mapsB)
        mapsC = []
        for c in range(8):
            b, j = c // 4, c % 4

            def gath(key, b=b, j=j):
                return np.ascontiguousarray(np.concatenate([np.asarray(rB[b * 4 + g][key])[j * 256:(j + 1) * 256] for g in range(4)], 0))
            m = dict(cst=cst, pv=pv, xT=xs[c], yfox=gath("yfox"), ylru=gath("ylru"), ysb=gath("ysb"), ymem=np.asarray(rA[c]["ymem"]),
                     w_out=np.ascontiguousarray(inp["w_out"][l]))
            for i in range(4):
                m["gates%d" % i] = np.asarray(rA[c]["gates%d" % i])
                m["w_branch%d" % i] = np.ascontiguousarray(inp["w_branch"][l][i])
            mapsC.append(m)
        rC1 = _run(_prog("C1", build_C1), mapsC)
        mapsC2 = []
        for c in range(8):
            halo = zhalo if c % 4 == 0 else np.ascontiguousarray(np.asarray(rC1[c - 1]["h2"])[:, -2:])
            mapsC2.append(dict(pv=pv, xT=np.asarray(rC1[c]["xo"]), h2=np.asarray(rC1[c]["h2"]), h2halo=halo,
                               w_up=np.ascontiguousarray(inp["w_up"][l]), w_down=np.ascontiguousarray(inp["w_down"][l])))
        rC2 = _run(_prog("C2", build_C2), mapsC2)
        xs = [np.asarray(rC2[c]["xo"]) for c in range(8)]
    out = np.zeros((2, S_LEN, D), np.float32)
    for c in range(8):
        out[c // 4, (c % 4) * TOK:(c % 4 + 1) * TOK] = xs[c].T
    return out
```

```python
import contextlib
import numpy as np
import ml_dtypes
import concourse.bass as bass
import concourse.mybir as mybir
from concourse.bass_utils import run_bass_kernel_spmd

F32 = mybir.dt.float32
BF16 = mybir.dt.bfloat16
AF = mybir.ActivationFunctionType
ALU = mybir.AluOpType

ENGS = ("pe", "act", "dve", "pool", "sp")
NDMASEM = 8

D = 1024
S_LEN = 8192
TOK = 2048
NT = 4
DFF = 2816
NCC = 22
N_IN = 13328
EPS = 1e-6
COLS = dict(fox_q=0, fox_k=1024, fox_v=2048, fox_f=3072, lru_x=3088, lru_g=4112,
            sb_q=5136, sb_k=6160, sb_v=7184, mem_q=8208, gates=9232)


class Buf:
    __slots__ = ("name", "w", "r", "pre")

    def __init__(self, name=""):
        self.name = name
        self.w = []
        self.r = {}
        self.pre = []


def I(name, **kw):
    return (name, kw)


class Ins:
    __slots__ = ("eng", "fn", "deps", "signal", "sem", "val", "dma", "slot")

    def __init__(self, eng, fn, dma):
        self.eng = eng
        self.fn = fn
        self.deps = []
        self.signal = False
        self.sem = None
        self.val = 0
        self.dma = dma
        self.slot = -1


class Sched:
    def __init__(self, nc):
        self.nc = nc
        self.q = {e: [] for e in ENGS}
        self.ndma = {e: 0 for e in ENGS}
        self.uid = 0

    def add(self, eng, fn, reads=(), writes=(), dma=False, join=False):
        ins = Ins(eng, fn, dma)
        deps = {}
        for b in reads:
            for d in b.w:
                deps[id(d)] = d
        for b in writes:
            if join:
                for d in b.pre:
                    deps[id(d)] = d
            else:
                pre = list(b.w) + list(b.r.values())
                for d in pre:
                    deps[id(d)] = d
                b.pre = pre
        for b in writes:
            if join:
                b.w.append(ins)
            else:
                b.w = [ins]
                b.r = {}
        for b in reads:
            if dma:
                self.uid += 1
                b.r[("dma", self.uid)] = ins
            else:
                b.r[eng] = ins
        if dma:
            n = self.ndma[eng]
            ins.slot = n % NDMASEM
            self.ndma[eng] = n + 1
        deps.pop(id(ins), None)
        ins.deps = list(deps.values())
        for d in ins.deps:
            if not (eng == "pe" and d.eng == "pe" and not d.dma):
                d.signal = True
        self.q[eng].append(ins)
        return ins

    def pe(self, fn, reads=(), writes=(), join=False):
        return self.add("pe", fn, reads, writes, join=join)

    def act(self, fn, reads=(), writes=(), join=False):
        return self.add("act", fn, reads, writes, join=join)

    def dve(self, fn, reads=(), writes=(), join=False):
        return self.add("dve", fn, reads, writes, join=join)

    def pool(self, fn, reads=(), writes=(), join=False):
        return self.add("pool", fn, reads, writes, join=join)

    def dma(self, fn, reads=(), writes=(), q="sp", join=False):
        return self.add(q, fn, reads, writes, dma=True, join=join)

    def emit(self, st, final_wait=()):
        nc = self.nc
        esem = {e: st.enter_context(nc.semaphore("tl_" + e)) for e in ENGS}
        dsem = {e: [st.enter_context(nc.semaphore("dm_%s%d" % (e, k))) for k in range(NDMASEM)]
                for e in ENGS if self.ndma[e] > 0}
        for e in ENGS:
            cnt = 0
            dcnt = [0] * NDMASEM
            prev = [None] * NDMASEM
            for ins in self.q[e]:
                if ins.dma:
                    k = ins.slot
                    dcnt[k] += 16
                    ins.sem = dsem[e][k]
                    ins.val = dcnt[k]
                    if prev[k] is not None:
                        ins.deps.append(prev[k])
                    prev[k] = ins
                elif ins.signal:
                    cnt += 1
                    ins.sem = esem[e]
                    ins.val = cnt
        fin = list(final_wait)
        block = st.enter_context(nc.Block())
        engobj = {"pe": block.tensor, "act": block.scalar, "dve": block.vector,
                  "pool": block.gpsimd, "sp": block.sync}

        def make(e):
            def body(eng):
                waited = {}

                def dowaits(deps):
                    need = {}
                    for d in deps:
                        if d.sem is None:
                            continue
                        if (not d.dma) and d.eng == e and e == "pe":
                            continue
                        k = id(d.sem)
                        if k not in need or need[k][1] < d.val:
                            need[k] = (d.sem, d.val)
                    for k, (s, v) in need.items():
                        if waited.get(k, 0) >= v:
                            continue
                        eng.wait_ge(s, v)
                        waited[k] = v

                for ins in self.q[e]:
                    dowaits(ins.deps)
                    if ins.fn is None:
                        continue
                    r = getattr(eng, ins.fn[0])(**ins.fn[1]) if isinstance(ins.fn, tuple) else ins.fn(eng)
                    if ins.dma:
                        r.then_inc(ins.sem, 16)
                    elif ins.signal:
                        r.then_inc(ins.sem, 1)
                if e == "sp":
                    dowaits(fin)
            return body

        for e in ENGS:
            if self.q[e] or (e == "sp" and fin):
                engobj[e](make(e))


class Ctx:
    def __init__(self):
        self.nc = bass.Bass("TRN2", target_bir_lowering=False)
        self.st = contextlib.ExitStack()
        self.S = Sched(self.nc)
        self.n = 0
        self.outs = []

    def din(self, name, shape, dt=F32):
        return self.nc.dram_tensor(name, list(shape), dt, kind="ExternalInput").ap()

    def dout(self, name, shape, dt=F32):
        return self.nc.dram_tensor(name, list(shape), dt, kind="ExternalOutput").ap()

    def sb(self, shape, dt, name="t"):
        self.n += 1
        return self.st.enter_context(self.nc.sbuf_tensor("%s_%d" % (name, self.n), list(shape), dt))

    def psum(self):
        self.n += 1
        return self.st.enter_context(self.nc.psum_tensor("ps_%d" % self.n, [128, 512], F32))

    def store(self, dst, src, reads):
        ins = self.S.dma(lambda e: e.dma_start(out=dst, in_=src), reads=reads)
        self.outs.append(ins)
        return ins

    def finish(self):
        self.S.emit(self.st, final_wait=self.outs)
        self.st.close()
        return self.nc


class Rot:
    def __init__(self, c, shape, dt, n, name):
        self.t = [c.sb(shape, dt, name) for _ in range(n)]
        self.b = [Buf(name) for _ in range(n)]
        self.i = 0

    def next(self):
        k = self.i % len(self.t)
        self.i += 1
        return self.t[k], self.b[k]


class PsRot:
    def __init__(self, c, n):
        self.t = [c.psum() for _ in range(n)]
        self.b = [Buf("ps") for _ in range(n)]
        self.i = 0

    def next(self):
        k = self.i % len(self.t)
        self.i += 1
        return self.t[k], self.b[k]


PV = {}
_o = 0
for _n, _w in (("attn_g", 8), ("mem_g", 8), ("ffn_g", 8), ("fq_g", 1), ("fk_g", 1), ("mq_g", 2), ("mk_g", 2),
               ("b_gate", 32), ("b_f", 1), ("ffn_cw", 66), ("ffn_cb", 22)):
    PV[_n] = _o
    _o += _w
NPV = _o
PL = dict(cw=0, cb=8, ba=10, bx=12, lam=14)
NPL = 16
CST = dict(ident=0, tri=128, strict=256, bones=384, ones=512, niu=640)
NCST = 768


def host_consts():
    c = np.zeros((128, NCST), np.float32)
    k = np.arange(128)[:, None]
    q = np.arange(128)[None, :]
    c[:, 0:128] = np.eye(128)
    c[:, 128:256] = (q >= k)
    c[:, 256:384] = (k < q)
    c[:, 384:512] = ((k // 64) == (q // 64))
    c[:, 512:640] = 1.0
    c[:, 640:768] = -(k >= q).astype(np.float32)
    return c


def host_pvec(inp, l):
    pv = np.zeros((128, NPV), np.float32)

    def cols(v):
        return np.ascontiguousarray(v.reshape(-1, 128).T)
    pv[:, PV["attn_g"]:PV["attn_g"] + 8] = cols(inp["attn_norm_g"][l])
    pv[:, PV["mem_g"]:PV["mem_g"] + 8] = cols(inp["mem_norm_g"][l])
    pv[:, PV["ffn_g"]:PV["ffn_g"] + 8] = cols(inp["ffn_norm_g"][l])
    pv[:, PV["fq_g"]] = np.tile(inp["fox_q_norm_g"][l], 2)
    pv[:, PV["fk_g"]] = np.tile(inp["fox_k_norm_g"][l], 2)
    pv[:, PV["mq_g"]:PV["mq_g"] + 2] = cols(inp["mem_q_norm_g"][l])
    pv[:, PV["mk_g"]:PV["mk_g"] + 2] = cols(inp["mem_k_norm_g"][l])
    pv[:, PV["b_gate"]:PV["b_gate"] + 32] = cols(inp["b_gate"][l].reshape(-1))
    pv[0:16, PV["b_f"]] = inp["b_forget"][l]
    for i in range(3):
        pv[:, PV["ffn_cw"] + 22 * i:PV["ffn_cw"] + 22 * (i + 1)] = cols(inp["ffn_conv_w"][l][i])
    pv[:, PV["ffn_cb"]:PV["ffn_cb"] + 22] = cols(inp["ffn_conv_b"][l])
    return pv


def host_plru(inp, l, g):
    pl = np.zeros((128, NPL), np.float32)
    sl = slice(256 * g, 256 * (g + 1))

    def cols(v):
        return np.ascontiguousarray(v.reshape(-1, 128).T)
    for i in range(4):
        pl[:, PL["cw"] + 2 * i:PL["cw"] + 2 * i + 2] = cols(inp["lru_conv_w"][l][i, sl])
    pl[:, PL["cb"]:PL["cb"] + 2] = cols(inp["lru_conv_b"][l][sl])
    pl[:, PL["ba"]:PL["ba"] + 2] = cols(inp["lru_b_a"][l][sl])
    pl[:, PL["bx"]:PL["bx"] + 2] = cols(inp["lru_b_x"][l][sl])
    pl[:, PL["lam"]:PL["lam"] + 2] = cols(inp["lru_lambda"][l][sl])
    wbd = np.zeros((128, 2, 2, 128), np.float32)
    for ci in range(2):
        for blk in range(2):
            n = 4 * g + 2 * ci + blk
            wbd[64 * blk:64 * blk + 64, 0, ci, 64 * blk:64 * blk + 64] = inp["lru_w_a"][l][n]
            wbd[64 * blk:64 * blk + 64, 1, ci, 64 * blk:64 * blk + 64] = inp["lru_w_x"][l][n]
    return pl, wbd.reshape(128, 512)


def load_consts(c, cst_d):
    S = c.S
    cst = c.sb([128, NCST], F32, "cst")
    cstb = c.sb([128, NCST], BF16, "cstb")
    B = Buf("cst")
    S.dma(lambda e: e.dma_start(out=cst[:], in_=cst_d), writes=[B])
    S.dve(lambda e: e.tensor_copy(out=cstb[:], in_=cst[:]), reads=[B], writes=[B])
    return cst, cstb, B


def rms_stats(c, PS, n_items, sq_fn, inv_n, ones_ap, Bcst, tmp_rot, N=512):
    S = c.S
    ps, Bp = PS.next()
    for i in range(n_items):
        ap, B = sq_fn(i)
        S.pe(lambda e, ap=ap, i=i: e.matmul(ps[:, 0:N], ones_ap, ap, start=(i == 0), stop=(i == n_items - 1)),
             reads=[B, Bcst], writes=[Bp])
    t, Bt = tmp_rot.next()
    S.dve(lambda e: e.tensor_scalar(out=t[:, 0:N], in0=ps[:, 0:N], scalar1=inv_n, scalar2=EPS, op0=ALU.mult, op1=ALU.add),
          reads=[Bp], writes=[Bt])
    S.act(lambda e: e.activation(out=t[:, 0:N], in_=t[:, 0:N], func=AF.Ln), reads=[Bt], writes=[Bt])
    S.act(lambda e: e.activation(out=t[:, 0:N], in_=t[:, 0:N], func=AF.Exp, scale=-0.5), reads=[Bt], writes=[Bt])
    return t, Bt


def wload(c, wrot, w_d, c0, ncols, nk=8):
    wt, Bw = wrot.next()
    src = w_d[:, c0:c0 + ncols].rearrange("(kc p) n -> p kc n", p=128)
    c.S.dma(lambda e: e.dma_start(out=wt[:, 0:nk, 0:ncols], in_=src), writes=[Bw], q="pool")
    return wt, Bw


def rmsnorm_tile(c, PS, xT, Bx, ts, gcol0, pv, Bpv, ones_f, Bcst, tmpf, sqr, hT, hts, Bh):
    S = c.S

    def sqf(kc):
        sq, Bs = sqr.next()
        S.act(lambda e: e.activation(out=sq[:], in_=xT[:, kc, ts], func=AF.Square), reads=[Bx], writes=[Bs])
        return sq[:], Bs
    rstd, Br = rms_stats(c, PS, 8, sqf, 1.0 / 1024, ones_f, Bcst, tmpf)
    for kc in range(8):
        S.dve(lambda e, kc=kc: e.scalar_tensor_tensor(out=hT[:, kc, hts], in0=xT[:, kc, ts],
                                                      scalar=pv[:, gcol0 + kc:gcol0 + kc + 1], in1=rstd[:],
                                                      op0=ALU.mult, op1=ALU.mult),
              reads=[Bx, Br, Bpv], writes=[Bh])


def phase_A(c, io):
    S = c.S
    cst, cstb, Bcst = load_consts(c, io["cst"])
    ones_f = cst[:, CST["ones"]:CST["ones"] + 128]
    bones_f = cst[:, CST["bones"]:CST["bones"] + 128]
    ones_b = cstb[:, CST["ones"]:CST["ones"] + 128]
    pv = c.sb([128, NPV], F32, "pv")
    Bpv = Buf("pv")
    S.dma(lambda e: e.dma_start(out=pv[:], in_=io["pv"]), writes=[Bpv])
    dpar = c.sb([128, 8], F32, "dpar")
    Bdp = Buf("dpar")
    S.dve(lambda e: e.tensor_scalar(out=dpar[:, 0:1], in0=pv[:, PV["fq_g"]:PV["fq_g"] + 1], scalar1=0.125, scalar2=None, op0=ALU.mult),
          reads=[Bpv], writes=[Bdp])
    S.dve(lambda e: e.tensor_scalar(out=dpar[:, 1:3], in0=pv[:, PV["mq_g"]:PV["mq_g"] + 2], scalar1=0.0625, scalar2=None, op0=ALU.mult),
          reads=[Bpv], writes=[Bdp])
    S.dve(lambda e: e.tensor_scalar(out=dpar[:, 3:4], in0=pv[:, PV["b_f"]:PV["b_f"] + 1], scalar1=-1.0, scalar2=None, op0=ALU.mult),
          reads=[Bpv], writes=[Bdp])

    PS = PsRot(c, 8)
    tmpf = Rot(c, [128, 512], F32, 4, "tmpf")
    sqr = Rot(c, [128, 512], F32, 4, "sqr")
    outb = Rot(c, [128, 512], BF16, 4, "outb")
    outf = Rot(c, [128, 512], F32, 3, "outf")
    wrot = Rot(c, [128, 8, 512], BF16, 3, "w")

    xT = c.sb([128, 8, TOK], F32, "xT")
    hT = c.sb([128, 8, TOK], BF16, "hT")
    Bx = [Buf("x%d" % t) for t in range(NT)]
    Bh = [Buf("h%d" % t) for t in range(NT)]
    xsrc = io["xT"].rearrange("(kc p) t -> p kc t", p=128)
    TS = [slice(t * 512, (t + 1) * 512) for t in range(NT)]
    for t in range(NT):
        S.dma(lambda e, t=t: e.dma_start(out=xT[:, :, TS[t]], in_=xsrc[:, :, TS[t]]), writes=[Bx[t]])
    for t in range(NT):
        rmsnorm_tile(c, PS, xT, Bx[t], TS[t], PV["attn_g"], pv, Bpv, ones_f, Bcst, tmpf, sqr, hT, TS[t], Bh[t])

    if io.get('stop', 99) <= 1:
        return
    memT = c.sb([128, 8, 256], F32, "memT")
    memn = c.sb([128, 8, 256], BF16, "memn")
    Bmem, Bmemn = Buf("mem"), Buf("memn")
    S.dma(lambda e: e.dma_start(out=memT[:], in_=io["memT"].rearrange("(kc p) m -> p kc m", p=128)), writes=[Bmem])

    def sqm(kc):
        sq, Bs = sqr.next()
        S.act(lambda e: e.activation(out=sq[:, 0:256], in_=memT[:, kc, :], func=AF.Square), reads=[Bmem], writes=[Bs])
        return sq[:, 0:256], Bs
    rstd, Br = rms_stats(c, PS, 8, sqm, 1.0 / 1024, ones_f, Bcst, tmpf, N=256)
    for kc in range(8):
        S.dve(lambda e, kc=kc, rstd=rstd: e.scalar_tensor_tensor(out=memn[:, kc, :], in0=memT[:, kc, :],
                                                      scalar=pv[:, PV["mem_g"] + kc:PV["mem_g"] + kc + 1], in1=rstd[:, 0:256],
                                                      op0=ALU.mult, op1=ALU.mult),
              reads=[Bmem, Br, Bpv], writes=[Bmemn])
    mkT = c.sb([128, 4, 2, 256], BF16, "mkT")
    mv = c.sb([128, 2, 1024], BF16, "mv")
    Bmk, Bmv = Buf("mk"), Buf("mv")
    for wi in range(2):
        wt, Bw = wload(c, wrot, io["w_kv"], wi * 512, 512)
        for hh in range(2):
            h = wi * 2 + hh
            pss = []
            for ci in range(2):
                ps, Bp = PS.next()
                cc = hh * 2 + ci
                for kc in range(8):
                    S.pe(lambda e, ps=ps, kc=kc, cc=cc, wt=wt: e.matmul(ps[:, 0:256], wt[:, kc, cc * 128:(cc + 1) * 128], memn[:, kc, :],
                                                                       start=(kc == 0), stop=(kc == 7)),
                         reads=[Bw, Bmemn], writes=[Bp])
                pss.append((ps, Bp))

            def sqk(i):
                sq, Bs = sqr.next()
                ps, Bp = pss[i]
                S.act(lambda e: e.activation(out=sq[:, 0:256], in_=ps[:, 0:256], func=AF.Square), reads=[Bp], writes=[Bs])
                return sq[:, 0:256], Bs
            rstd, Br = rms_stats(c, PS, 2, sqk, 1.0 / 256, ones_f, Bcst, tmpf, N=256)
            for ci in range(2):
                ps, Bp = pss[ci]
                S.dve(lambda e, ps=ps, ci=ci, h=h, rstd=rstd: e.scalar_tensor_tensor(
                    out=mkT[:, h, ci, :], in0=ps[:, 0:256], scalar=pv[:, PV["mk_g"] + ci:PV["mk_g"] + ci + 1],
                    in1=rstd[:, 0:256], op0=ALU.mult, op1=ALU.mult), reads=[Bp, Br, Bpv], writes=[Bmk])
    for wi in range(2):
        wt, Bw = wload(c, wrot, io["w_kv"], 1024 + wi * 512, 512)
        for mc in range(2):
            ps, Bp = PS.next()
            for kc in range(8):
                S.pe(lambda e, ps=ps, kc=kc, mc=mc, wt=wt: e.matmul(ps[:, 0:512], memn[:, kc, mc * 128:(mc + 1) * 128], wt[:, kc, 0:512],
                                                                   start=(kc == 0), stop=(kc == 7)),
                     reads=[Bw, Bmemn], writes=[Bp])
            S.act(lambda e, ps=ps, mc=mc, wi=wi: e.activation(out=mv[:, mc, wi * 512:(wi + 1) * 512], in_=ps[:, 0:512], func=AF.Copy),
                  reads=[Bp], writes=[Bmv])

    if 'dbg_mk' in io:
        c.store(io['dbg_mk'], mkT[:].rearrange("p h c m -> p (h c m)"), [Bmk])
        c.store(io['dbg_mv'], mv[:].rearrange("p c d -> p (c d)"), [Bmv])
        c.store(io['dbg_memn'], memn[:].rearrange("p c d -> p (c d)"), [Bmemn])
    if io.get('stop', 99) <= 2:
        return
    w_in = io["w_in"]

    def proj_fm(wt, Bw, cc, t):
        ps, Bp = PS.next()
        for kc in range(8):
            S.pe(lambda e, kc=kc: e.matmul(ps[:, 0:512], wt[:, kc, cc * 128:(cc + 1) * 128], hT[:, kc, TS[t]],
                                           start=(kc == 0), stop=(kc == 7)),
                 reads=[Bw, Bh[t]], writes=[Bp])
        return ps, Bp

    def fam_qknorm(col0, gcol_ap, Bg, dst):
        for wi in range(2):
            wt, Bw = wload(c, wrot, w_in, col0 + wi * 512, 512)
            for cc in range(4):
                row0 = (wi * 4 + cc) * 128
                for t in range(NT):
                    ps, Bp = proj_fm(wt, Bw, cc, t)

                    def sqf(i, ps=ps, Bp=Bp):
                        sq, Bs = sqr.next()
                        S.act(lambda e: e.activation(out=sq[:], in_=ps[:, 0:512], func=AF.Square), reads=[Bp], writes=[Bs])
                        return sq[:], Bs
                    rstd, Br = rms_stats(c, PS, 1, sqf, 1.0 / 64, bones_f, Bcst, tmpf)
                    ob, Bo = outb.next()
                    S.dve(lambda e, ps=ps, ob=ob, rstd=rstd: e.scalar_tensor_tensor(out=ob[:], in0=ps[:, 0:512], scalar=gcol_ap, in1=rstd[:],
                                                                                    op0=ALU.mult, op1=ALU.mult),
                          reads=[Bp, Br, Bg], writes=[Bo])
                    c.store(dst[row0:row0 + 128, TS[t]], ob[:], [Bo])

    def fam_copy(col0, dst, scale, bf):
        for wi in range(2):
            wt, Bw = wload(c, wrot, w_in, col0 + wi * 512, 512)
            for cc in range(4):
                row0 = (wi * 4 + cc) * 128
                for t in range(NT):
                    ps, Bp = proj_fm(wt, Bw, cc, t)
                    ob, Bo = (outb if bf else outf).next()
                    S.act(lambda e, ps=ps, ob=ob: e.activation(out=ob[:], in_=ps[:, 0:512], func=AF.Copy, scale=scale),
                          reads=[Bp], writes=[Bo])
                    c.store(dst[row0:row0 + 128, TS[t]], ob[:], [Bo])

    def fam_v(col0, dst):
        for wi in range(2):
            wt, Bw = wload(c, wrot, w_in, col0 + wi * 512, 512)
            for tb in range(16):
                ps, Bp = PS.next()
                for kc in range(8):
                    S.pe(lambda e, ps=ps, kc=kc, tb=tb, wt=wt: e.matmul(ps[:, 0:512], hT[:, kc, tb * 128:(tb + 1) * 128], wt[:, kc, 0:512],
                                                                       start=(kc == 0), stop=(kc == 7)),
                         reads=[Bw, Bh[tb // 4]], writes=[Bp])
                ob, Bo = outb.next()
                S.act(lambda e, ps=ps, ob=ob: e.activation(out=ob[:], in_=ps[:, 0:512], func=AF.Copy), reads=[Bp], writes=[Bo])
                for gi in range(2):
                    c.store(dst[wi * 2 + gi, tb * 128:(tb + 1) * 128, :], ob[:, gi * 256:(gi + 1) * 256], [Bo])

    fam_qknorm(COLS["fox_q"], dpar[:, 0:1], Bdp, io["fq"])
    if io.get('stop', 99) <= 3:
        return
    fam_qknorm(COLS["fox_k"], pv[:, PV["fk_g"]:PV["fk_g"] + 1], Bpv, io["fk"])
    if io.get('stop', 99) <= 4:
        return
    fam_v(COLS["fox_v"], io["fv"])
    if io.get('stop', 99) <= 5:
        return
    wf = c.sb([128, 8, 16], BF16, "wf")
    Bwf = Buf("wf")
    S.dma(lambda e: e.dma_start(out=wf[:], in_=w_in[:, COLS["fox_f"]:COLS["fox_f"] + 16].rearrange("(kc p) n -> p kc n", p=128)),
          writes=[Bwf], q="pool")
    for t in range(NT):
        ps, Bp = PS.next()
        for kc in range(8):
            S.pe(lambda e, ps=ps, kc=kc, t=t: e.matmul(ps[0:16, 0:512], wf[:, kc, 0:16], hT[:, kc, TS[t]], start=(kc == 0), stop=(kc == 7)),
                 reads=[Bwf, Bh[t]], writes=[Bp])
        of, Bo = outf.next()
        S.act(lambda e, ps=ps, of=of: e.activation(out=of[0:16, :], in_=ps[0:16, 0:512], func=AF.Exp, bias=dpar[0:16, 3:4], scale=-1.0),
              reads=[Bp, Bdp], writes=[Bo])
        S.act(lambda e, of=of: e.activation(out=of[0:16, :], in_=of[0:16, :], func=AF.Ln, bias=1.0), reads=[Bo], writes=[Bo])
        S.dve(lambda e, of=of: e.tensor_scalar(out=of[0:16, :], in0=of[0:16, :], scalar1=-1.0, scalar2=None, op0=ALU.mult),
              reads=[Bo], writes=[Bo])
        c.store(io["logf"][:, TS[t]], of[0:16, :], [Bo])
    if io.get('stop', 99) <= 6:
        return
    fam_copy(COLS["lru_x"], io["lx"], 1.0, False)
    fam_copy(COLS["lru_g"], io["lg"], 1.0, False)
    fam_copy(COLS["sb_q"], io["sq"], 0.125, True)
    fam_copy(COLS["sb_k"], io["sk"], 1.0, True)
    fam_v(COLS["sb_v"], io["sv"])
    if io.get('stop', 99) <= 7:
        return
    mqr = Rot(c, [128, 2, 512], BF16, 2, "mq")
    Er = Rot(c, [128, 2, 512], BF16, 2, "E")
    for wi in range(2):
        wt, Bw = wload(c, wrot, w_in, COLS["mem_q"] + wi * 512, 512)
        for hh in range(2):
            h = wi * 2 + hh
            for t in range(NT):
                pss = [proj_fm(wt, Bw, hh * 2 + ci, t) for ci in range(2)]

                def sqf(i, pss=pss):
                    sq, Bs = sqr.next()
                    ps, Bp = pss[i]
                    S.act(lambda e: e.activation(out=sq[:], in_=ps[:, 0:512], func=AF.Square), reads=[Bp], writes=[Bs])
                    return sq[:], Bs
                rstd, Br = rms_stats(c, PS, 2, sqf, 1.0 / 256, ones_f, Bcst, tmpf)
                mq, Bmq = mqr.next()
                for ci in range(2):
                    ps, Bp = pss[ci]
                    S.dve(lambda e, ps=ps, ci=ci, mq=mq, rstd=rstd: e.scalar_tensor_tensor(
                        out=mq[:, ci, :], in0=ps[:, 0:512], scalar=dpar[:, 1 + ci:2 + ci], in1=rstd[:], op0=ALU.mult, op1=ALU.mult),
                        reads=[Bp, Br, Bdp], writes=[Bmq])
                E, BE = Er.next()
                for mc in range(2):
                    ps, Bp = PS.next()
                    for dc in range(2):
                        S.pe(lambda e, ps=ps, dc=dc, mc=mc, h=h, mq=mq: e.matmul(ps[:, 0:512], mkT[:, h, dc, mc * 128:(mc + 1) * 128], mq[:, dc, :],
                                                                                start=(dc == 0), stop=(dc == 1)),
                             reads=[Bmk, Bmq], writes=[Bp])
                    S.act(lambda e, ps=ps, mc=mc, E=E: e.activation(out=E[:, mc, :], in_=ps[:, 0:512], func=AF.Exp), reads=[Bp], writes=[BE])
                if 'dbg_E' in io and h == 0 and t == 0:
                    c.store(io['dbg_E'], E[:].rearrange("p c q -> p (c q)"), [BE])
                    c.store(io['dbg_mq'], mq[:].rearrange("p c q -> p (c q)"), [Bmq])
                psd, Bpd = PS.next()
                for mc in range(2):
                    S.pe(lambda e, mc=mc, E=E, psd=psd: e.matmul(psd[:, 0:512], ones_b, E[:, mc, :], start=(mc == 0), stop=(mc == 1)),
                         reads=[BE, Bcst], writes=[Bpd])
                rden, Brd = tmpf.next()
                S.dve(lambda e, rden=rden, psd=psd: e.reciprocal(out=rden[:], in_=psd[:, 0:512]), reads=[Bpd], writes=[Brd])
                for dcp in range(2):
                    ps, Bp = PS.next()
                    for mc in range(2):
                        S.pe(lambda e, ps=ps, mc=mc, dcp=dcp, h=h, E=E: e.matmul(ps[:, 0:512], mv[:, mc, h * 256 + dcp * 128:h * 256 + (dcp + 1) * 128],
                                                                                E[:, mc, :], start=(mc == 0), stop=(mc == 1)),
                             reads=[BE, Bmv], writes=[Bp])
                    ob, Bo = outb.next()
                    S.dve(lambda e, ps=ps, ob=ob, rden=rden: e.tensor_tensor(out=ob[:], in0=ps[:, 0:512], in1=rden[:], op=ALU.mult),
                          reads=[Bp, Brd], writes=[Bo])
                    r0 = h * 256 + dcp * 128
                    c.store(io["ymem"][r0:r0 + 128, TS[t]], ob[:], [Bo])
    if io.get('stop', 99) <= 8:
        return
    for wi in range(io.get('ngw', 8)):
        wt, Bw = wload(c, wrot, w_in, COLS["gates"] + wi * 512, 512)
        for cc in range(4):
            gc = wi * 4 + cc
            for t in range(NT):
                ps, Bp = proj_fm(wt, Bw, cc, t)
                ob, Bo = outb.next()
                S.act(lambda e, ps=ps, ob=ob, gc=gc: e.activation(out=ob[:], in_=ps[:, 0:512], func=AF.Sigmoid,
                                                                 bias=pv[:, PV["b_gate"] + gc:PV["b_gate"] + gc + 1]),
                      reads=[Bp, Bpv], writes=[Bo])
                c.store(io["gates"][gc // 8][(gc % 8) * 128:(gc % 8 + 1) * 128, TS[t]], ob[:], [Bo])


def build_A(stop=99, ngw=8):
    c = Ctx()
    io = dict(stop=stop, ngw=ngw, xT=c.din("xT", [D, TOK]), memT=c.din("memT", [D, 256]), w_in=c.din("w_in", [D, N_IN]), w_kv=c.din("w_kv", [D, 2048]),
              pv=c.din("pv", [128, NPV]), cst=c.din("cst", [128, NCST]),
              fq=c.dout("fq", [D, TOK], BF16), fk=c.dout("fk", [D, TOK], BF16), fv=c.dout("fv", [4, TOK, 256], BF16),
              logf=c.dout("logf", [16, TOK]), lx=c.dout("lx", [D, TOK]), lg=c.dout("lg", [D, TOK]),
              sq=c.dout("sq", [D, TOK], BF16), sk=c.dout("sk", [D, TOK], BF16), sv=c.dout("sv", [4, TOK, 256], BF16),
              ymem=c.dout("ymem", [D, TOK], BF16), gates=[c.dout("gates%d" % i, [D, TOK], BF16) for i in range(4)])
    if stop == 88:
        io.update(dbg_E=c.dout('dbg_E', [128, 1024], BF16), dbg_mq=c.dout('dbg_mq', [128, 1024], BF16))
    if stop <= 2:
        io.update(dbg_mk=c.dout('dbg_mk', [128, 2048], BF16), dbg_mv=c.dout('dbg_mv', [128, 2048], BF16), dbg_memn=c.dout('dbg_memn', [128, 2048], BF16))
    phase_A(c, io)
    return c.finish()


def phase_B(c, io):
    S = c.S
    cst, cstb, Bcst = load_consts(c, io["cst"])
    ident_f = cst[:, CST["ident"]:CST["ident"] + 128]
    ones_f = cst[:, CST["ones"]:CST["ones"] + 128]
    tri_b = cstb[:, CST["tri"]:CST["tri"] + 128]
    strict_b = cstb[:, CST["strict"]:CST["strict"] + 128]
    niu_b = cstb[:, CST["niu"]:CST["niu"] + 128]
    negones = c.sb([128, 128], BF16, "negones")
    Bno = Buf("negones")
    S.dve(I("tensor_scalar", out=negones[:], in0=cst[:, CST["ones"]:CST["ones"] + 128], scalar1=-1.0, scalar2=None, op0=ALU.mult),
          reads=[Bcst], writes=[Bno])
    parts = io.get("parts", ("fox", "sb", "lru"))

    accPS = PsRot(c, 2)
    sPS = PsRot(c, 4)
    bcPS = PsRot(c, 2)
    qr = Rot(c, [65, S_LEN], BF16, 2, "qt")
    kr = Rot(c, [65, S_LEN], BF16, 2, "kt")
    vr = Rot(c, [128, 64, 66], BF16, 2, "v1")
    Pr = Rot(c, [128, 512], BF16, 4, "P")
    yor = Rot(c, [64, 512], BF16, 2, "yo")

    def load_head(qd, kd, vd, h):
        qt, Bq = qr.next()
        kt, Bk = kr.next()
        v1, Bv = vr.next()
        for j in range(4):
            r0 = j * 256 + h * 64
            S.dma(I("dma_start", out=qt[0:64, j * 2048:(j + 1) * 2048], in_=qd[r0:r0 + 64, :]), writes=[Bq], join=(j > 0))
            S.dma(I("dma_start", out=kt[0:64, j * 2048:(j + 1) * 2048], in_=kd[r0:r0 + 64, :]), writes=[Bk], join=(j > 0))
            S.dma(I("dma_start", out=v1[:, j * 16:(j + 1) * 16, 0:64],
                    in_=vd[j * 2048:(j + 1) * 2048, h * 64:(h + 1) * 64].rearrange("(kb p) f -> p kb f", p=128)),
                  writes=[Bv], join=(j > 0))
        return qt, Bq, kt, Bk, v1, Bv

    if "fox" in parts:
        cumb = c.sb([4, S_LEN], BF16, "cumb")
        negF = c.sb([128, 256], F32, "negF")
        Bcumb, BnegF = Buf("cumb"), Buf("negF")
        lfr = Rot(c, [4, 1024], F32, 2, "lf")
        cur = Rot(c, [4, 1024], F32, 2, "cum")
        recr = Rot(c, [128, 512], F32, 2, "rec")
        bcsr = Rot(c, [64, 512], F32, 2, "bcs")
        prevcm, Bprev = None, None
        for jj in range(8):
            j, off = jj // 2, (jj % 2) * 1024
            lf, Blf = lfr.next()
            S.dma(I("dma_start", out=lf[:], in_=io["logf"][j * 4:(j + 1) * 4, off:off + 1024]), writes=[Blf])
            S.dve(I("tensor_scalar", out=lf[:], in0=lf[:], scalar1=0.5, scalar2=None, op0=ALU.mult), reads=[Blf], writes=[Blf])
            cm, Bcm = cur.next()
            init = 0.0 if jj == 0 else prevcm[:, 1023:1024]
            S.dve(I("tensor_tensor_scan", out=cm[:], data0=lf[:], data1=lf[:], initial=init, op0=ALU.add, op1=ALU.add),
                  reads=[Blf] + ([Bprev] if jj else []), writes=[Bcm])
            S.dve(I("tensor_copy", out=cumb[:, jj * 1024:(jj + 1) * 1024], in_=cm[:]), reads=[Bcm], writes=[Bcumb])
            pt, Bpt = sPS.next()
            for kb in range(8):
                S.pe(I("transpose", out=pt[:, kb * 4:(kb + 1) * 4], in_=cm[0:4, kb * 128:(kb + 1) * 128], identity=ident_f[0:4, 0:4]),
                     reads=[Bcm, Bcst], writes=[Bpt])
            S.dve(I("tensor_scalar", out=negF[:, jj * 32:(jj + 1) * 32], in0=pt[:, 0:32], scalar1=-1.0, scalar2=None, op0=ALU.mult),
                  reads=[Bpt], writes=[BnegF])
            prevcm, Bprev = cm, Bcm
        for h in range(4):
            qt, Bq, kt, Bk, v1, Bv = load_head(io["fq"], io["fk"], io["fv"], h)
            S.dma(I("dma_start", out=qt[64:65, :], in_=cumb[h:h + 1, :]), reads=[Bcumb], writes=[Bq], join=True)
            S.dve(I("memset", ap=kt[64:65, :], constant=1.0), writes=[Bk], join=True)
            S.dve(I("memset", ap=v1[:, :, 64:66], constant=1.0), writes=[Bv], join=True)
            units = [(I_, J) for I_ in range(16) for J in range(4 * I_ + 4)]
            st = {}

            def fox_a(u):
                I_, J = units[u]
                c0 = max(0, J - 4 * I_) * 128
                sc, Bs = sPS.next()
                S.pe(I("matmul", out=sc[:, c0:512], lhsT=kt[0:65, J * 128:(J + 1) * 128], rhs=qt[0:65, I_ * 512 + c0:(I_ + 1) * 512],
                       start=True, stop=True), reads=[Bk, Bq], writes=[Bs])
                pt, Bp = Pr.next()
                S.act(I("activation", out=pt[:, c0:512], in_=sc[:, c0:512], func=AF.Exp, bias=negF[:, J * 4 + h:J * 4 + h + 1]),
                      reads=[Bs, BnegF], writes=[Bp])
                if J >= 4 * I_:
                    S.dve(I("tensor_tensor", out=pt[:, c0:c0 + 128], in0=pt[:, c0:c0 + 128], in1=tri_b, op=ALU.mult),
                          reads=[Bp, Bcst], writes=[Bp])
                st[u] = (pt, Bp, c0)

            def fox_b(u):
                I_, J = units[u]
                nJ = 4 * I_ + 4
                pt, Bp, c0 = st.pop(u)
                if J == 0:
                    st["acc"] = accPS.next()
                acc, Bacc = st["acc"]
                S.pe(I("matmul", out=acc[0:65, c0:512], lhsT=v1[:, J, 0:65], rhs=pt[:, c0:512], start=(J == 0), stop=(J == nJ - 1)),
                     reads=[Bv, Bp], writes=[Bacc])
                if J == nJ - 1:
                    rec, Brec = recr.next()
                    S.dve(I("reciprocal", out=rec[64:65, :], in_=acc[64:65, 0:512]), reads=[Bacc], writes=[Brec])
                    bc, Bbc = bcPS.next()
                    S.pe(I("matmul", out=bc[0:64, 0:512], lhsT=ones_f[64:65, 0:64], rhs=rec[64:65, :], start=True, stop=True),
                         reads=[Brec, Bcst], writes=[Bbc])
                    bcs, Bbcs = bcsr.next()
                    S.act(I("activation", out=bcs[:], in_=bc[0:64, 0:512], func=AF.Copy), reads=[Bbc], writes=[Bbcs])
                    yo, Byo = yor.next()
                    S.dve(I("tensor_tensor", out=yo[:], in0=acc[0:64, 0:512], in1=bcs[:], op=ALU.mult), reads=[Bacc, Bbcs], writes=[Byo])
                    r0 = (I_ // 4) * 256 + h * 64
                    c.store(io["yfox"][r0:r0 + 64, (I_ % 4) * 512:(I_ % 4 + 1) * 512], yo[:], [Byo])

            SK = 2
            for u in range(len(units) + SK):
                if u < len(units):
                    fox_a(u)
                if u >= SK:
                    fox_b(u - SK)

    if "sb" in parts:
        er = Rot(c, [128, 512], F32, 2, "ee")
        spr = Rot(c, [128, 512], BF16, 3, "sp")
        saccr = Rot(c, [128, 512], F32, 2, "sacc")
        sbfr = Rot(c, [128, 512], BF16, 2, "sbf")
        for h in range(4):
            qt, Bq, kt, Bk, v1, Bv = load_head(io["sq"], io["sk"], io["sv"], h)
            units = [(I_, J) for I_ in range(16) for J in range(4 * I_ + 3, -1, -1)]
            st = {}

            def sb_a(u):
                I_, J = units[u]
                c0 = max(0, J - 4 * I_) * 128
                kblk = kt[0:64, J * 128:(J + 1) * 128]
                qblk = qt[0:64, I_ * 512 + c0:(I_ + 1) * 512]
                z, Bz = sPS.next()
                S.pe(I("matmul", out=z[:, c0:512], lhsT=kblk, rhs=qblk, start=True, stop=True), reads=[Bk, Bq], writes=[Bz])
                ee, Be = er.next()
                S.act(I("activation", out=ee[:, c0:512], in_=z[:, c0:512], func=AF.Exp), reads=[Bz], writes=[Be])
                sp, Bsp = spr.next()
                S.act(I("activation", out=sp[:, c0:512], in_=ee[:, c0:512], func=AF.Ln, bias=1.0), reads=[Be], writes=[Bsp])
                if J >= 4 * I_:
                    S.dve(I("tensor_tensor", out=sp[:, c0:c0 + 128], in0=sp[:, c0:c0 + 128], in1=strict_b, op=ALU.mult),
                          reads=[Bsp, Bcst], writes=[Bsp])
                st[u] = (sp, Bsp, c0, kblk, qblk)

            def sb_b(u):
                I_, J = units[u]
                nJ = 4 * I_ + 4
                first = (J == nJ - 1)
                sp, Bsp, c0, kblk, qblk = st.pop(u)
                if first:
                    st["acc"] = accPS.next()
                    st["sacc"] = saccr.next()
                    st["sbf"] = sbfr.next()
                    S.dve(I("memset", ap=st["sacc"][0][:], constant=0.0), writes=[st["sacc"][1]])
                acc, Bacc = st["acc"]
                sacc, Bsa = st["sacc"]
                sbf, Bsb = st["sbf"]
                la, Bla = sPS.next()
                S.pe(I("matmul", out=la[:, c0:512], lhsT=kblk, rhs=qblk, start=True, stop=False), reads=[Bk, Bq], writes=[Bla])
                S.pe(I("matmul", out=la[:, c0:512], lhsT=niu_b, rhs=sp[:, c0:512], start=False, stop=first),
                     reads=[Bsp, Bcst], writes=[Bla])
                if not first:
                    S.pe(I("matmul", out=la[:, c0:512], lhsT=negones[:], rhs=sbf[:, c0:512], start=False, stop=True),
                         reads=[Bsb, Bno], writes=[Bla])
                at, Bat = Pr.next()
                S.act(I("activation", out=at[:, c0:512], in_=la[:, c0:512], func=AF.Exp), reads=[Bla], writes=[Bat])
                if J >= 4 * I_:
                    S.dve(I("tensor_tensor", out=at[:, c0:c0 + 128], in0=at[:, c0:c0 + 128], in1=strict_b, op=ALU.mult),
                          reads=[Bat, Bcst], writes=[Bat])
                S.pe(I("matmul", out=acc[0:64, c0:512], lhsT=v1[:, J, 0:64], rhs=at[:, c0:512], start=first, stop=(J == 0)),
                     reads=[Bv, Bat], writes=[Bacc])
                if J > 0:
                    c0n = max(0, J - 1 - 4 * I_) * 128
                    S.dve(I("tensor_tensor", out=sacc[:, c0:512], in0=sacc[:, c0:512], in1=sp[:, c0:512], op=ALU.add),
                          reads=[Bsa, Bsp], writes=[Bsa])
                    S.dve(I("tensor_copy", out=sbf[:, c0n:512], in_=sacc[:, c0n:512]), reads=[Bsa], writes=[Bsb])
                else:
                    yo, Byo = yor.next()
                    S.act(I("activation", out=yo[:], in_=acc[0:64, 0:512], func=AF.Copy), reads=[Bacc], writes=[Byo])
                    r0 = (I_ // 4) * 256 + h * 64
                    c.store(io["ysb"][r0:r0 + 64, (I_ % 4) * 512:(I_ % 4 + 1) * 512], yo[:], [Byo])

            for u in range(len(units) + 1):
                if u < len(units):
                    sb_a(u)
                if u >= 1:
                    sb_b(u - 1)

    if "lru" in parts:
        LT = 1024
        pl = c.sb([128, NPL], F32, "pl")
        Bpl = Buf("pl")
        S.dma(I("dma_start", out=pl[:], in_=io["pl"]), writes=[Bpl])
        wbd = c.sb([128, 512], BF16, "wbd")
        Bwbd = Buf("wbd")
        S.dma(I("dma_start", out=wbd[:], in_=io["wbd"]), writes=[Bwbd], q="pool")
        cpar = c.sb([128, 2], F32, "cpar")
        Bcp = Buf("cpar")
        S.act(I("activation", out=cpar[:], in_=pl[:, PL["lam"]:PL["lam"] + 2], func=AF.Exp, scale=-1.0), reads=[Bpl], writes=[Bcp])
        S.act(I("activation", out=cpar[:], in_=cpar[:], func=AF.Ln, bias=1.0), reads=[Bcp], writes=[Bcp])
        S.dve(I("tensor_scalar", out=cpar[:], in0=cpar[:], scalar1=-8.0, scalar2=None, op0=ALU.mult), reads=[Bcp], writes=[Bcp])
        xinr = Rot(c, [128, 3 + LT], F32, 2, "xin")
        gr = Rot(c, [128, LT], F32, 2, "g")
        hsr = Rot(c, [128, LT], F32, 2, "hs")
        xc, Bxc = c.sb([128, LT], F32, "xc"), Buf("xc")
        xcb, Bxcb = c.sb([128, LT], BF16, "xcb"), Buf("xcb")
        rr, Brr = c.sb([128, LT], F32, "r"), Buf("r")
        ii, Bii = c.sb([128, LT], F32, "i"), Buf("i")
        aa, Baa = c.sb([128, LT], F32, "a"), Buf("a")
        tmp, Btmp = c.sb([128, LT], F32, "tmp"), Buf("tmp")
        t2, Bt2 = c.sb([128, LT], F32, "t2"), Buf("t2")
        ybr = Rot(c, [128, LT], BF16, 2, "yb")
        for ci in range(2):
            pxin, Bpx, phs, Bph = None, None, None, None
            for jt in range(S_LEN // LT):
                j, off = (jt * LT) // 2048, (jt * LT) % 2048
                r0 = j * 256 + ci * 128
                xin, Bxi = xinr.next()
                S.dma(I("dma_start", out=xin[:, 3:3 + LT], in_=io["lx"][r0:r0 + 128, off:off + LT]), writes=[Bxi])
                if jt == 0:
                    S.dve(I("memset", ap=xin[:, 0:3], constant=0.0), writes=[Bxi], join=True)
                else:
                    S.dve(I("tensor_copy", out=xin[:, 0:3], in_=pxin[:, LT:LT + 3]), reads=[Bpx], writes=[Bxi], join=True)
                g, Bg = gr.next()
                S.dma(I("dma_start", out=g[:], in_=io["lg"][r0:r0 + 128, off:off + LT]), writes=[Bg])
                cw = lambda i: pl[:, PL["cw"] + 2 * i + ci:PL["cw"] + 2 * i + ci + 1]
                S.dve(I("tensor_scalar", out=xc[:], in0=xin[:, 3:3 + LT], scalar1=cw(3), scalar2=pl[:, PL["cb"] + ci:PL["cb"] + ci + 1],
                        op0=ALU.mult, op1=ALU.add), reads=[Bxi, Bpl], writes=[Bxc])
                for i in range(3):
                    S.dve(I("scalar_tensor_tensor", out=xc[:], in0=xin[:, i:i + LT], scalar=cw(i), in1=xc[:], op0=ALU.mult, op1=ALU.add),
                          reads=[Bxi, Bpl, Bxc], writes=[Bxc])
                S.act(I("activation", out=xcb[:], in_=xc[:], func=AF.Copy), reads=[Bxc], writes=[Bxcb])
                for s in range(LT // 512):
                    ss = slice(s * 512, (s + 1) * 512)
                    for which, dst, Bd, bcol in ((0, rr, Brr, PL["ba"]), (1, ii, Bii, PL["bx"])):
                        ps, Bp = sPS.next()
                        wsl = wbd[:, (which * 2 + ci) * 128:(which * 2 + ci + 1) * 128]
                        S.pe(I("matmul", out=ps[:, 0:512], lhsT=wsl, rhs=xcb[:, ss], start=True, stop=True), reads=[Bwbd, Bxcb], writes=[Bp])
                        S.act(I("activation", out=dst[:, ss], in_=ps[:, 0:512], func=AF.Sigmoid, bias=pl[:, bcol + ci:bcol + ci + 1]),
                              reads=[Bp, Bpl], writes=[Bd], join=(s > 0))
                S.act(I("activation", out=aa[:], in_=rr[:], func=AF.Exp, scale=cpar[:, ci:ci + 1]), reads=[Brr, Bcp], writes=[Baa])
                S.dve(I("tensor_scalar", out=aa[:], in0=aa[:], scalar1=1.0, scalar2=None, op0=ALU.min), reads=[Baa], writes=[Baa])
                S.dve(I("tensor_tensor", out=tmp[:], in0=aa[:], in1=aa[:], op=ALU.mult), reads=[Baa], writes=[Btmp])
                S.act(I("activation", out=tmp[:], in_=tmp[:], func=AF.Sqrt, bias=1.0, scale=-1.0), reads=[Btmp], writes=[Btmp])
                S.dve(I("tensor_tensor", out=tmp[:], in0=tmp[:], in1=ii[:], op=ALU.mult), reads=[Btmp, Bii], writes=[Btmp])
                S.dve(I("tensor_tensor", out=tmp[:], in0=tmp[:], in1=xc[:], op=ALU.mult), reads=[Btmp, Bxc], writes=[Btmp])
                hs, Bhs = hsr.next()
                init = 0.0 if jt == 0 else phs[:, LT - 1:LT]
                S.dve(I("tensor_tensor_scan", out=hs[:], data0=aa[:], data1=tmp[:], initial=init, op0=ALU.mult, op1=ALU.add),
                      reads=[Baa, Btmp] + ([Bph] if jt else []), writes=[Bhs])
                S.dve(I("tensor_tensor", out=t2[:], in0=g[:], in1=g[:], op=ALU.mult), reads=[Bg], writes=[Bt2])
                S.dve(I("tensor_scalar", out=t2[:], in0=t2[:], scalar1=0.0713548163, scalar2=1.5957691216, op0=ALU.mult, op1=ALU.add),
                      reads=[Bt2], writes=[Bt2])
                S.dve(I("tensor_tensor", out=t2[:], in0=t2[:], in1=g[:], op=ALU.mult), reads=[Bt2, Bg], writes=[Bt2])
                S.act(I("activation", out=t2[:], in_=t2[:], func=AF.Sigmoid), reads=[Bt2], writes=[Bt2])
                S.dve(I("tensor_tensor", out=t2[:], in0=t2[:], in1=g[:], op=ALU.mult), reads=[Bt2, Bg], writes=[Bt2])
                yb, Byb = ybr.next()
                S.dve(I("tensor_tensor", out=yb[:], in0=t2[:], in1=hs[:], op=ALU.mult), reads=[Bt2, Bhs], writes=[Byb])
                c.store(io["ylru"][r0:r0 + 128, off:off + LT], yb[:], [Byb])
                pxin, Bpx, phs, Bph = xin, Bxi, hs, Bhs


def build_B(parts=("fox", "sb", "lru")):
    c = Ctx()
    io = dict(parts=parts, cst=c.din("cst", [128, NCST]),
              fq=c.din("fq", [D, TOK], BF16), fk=c.din("fk", [D, TOK], BF16), fv=c.din("fv", [S_LEN, 256], BF16),
              logf=c.din("logf", [16, TOK]), sq=c.din("sq", [D, TOK], BF16), sk=c.din("sk", [D, TOK], BF16),
              sv=c.din("sv", [S_LEN, 256], BF16), lx=c.din("lx", [D, TOK]), lg=c.din("lg", [D, TOK]),
              pl=c.din("pl", [128, NPL]), wbd=c.din("wbd", [128, 512]),
              yfox=c.dout("yfox", [D, TOK], BF16), ysb=c.dout("ysb", [D, TOK], BF16), ylru=c.dout("ylru", [D, TOK], BF16))
    phase_B(c, io)
    return c.finish()


def load_x(c, xd):
    S = c.S
    xT = c.sb([128, 8, TOK], F32, "xT")
    Bx = [Buf("x%d" % t) for t in range(NT)]
    xsrc = xd.rearrange("(kc p) t -> p kc t", p=128)
    for t in range(NT):
        S.dma(I("dma_start", out=xT[:, :, t * 512:(t + 1) * 512], in_=xsrc[:, :, t * 512:(t + 1) * 512]), writes=[Bx[t]])
    return xT, Bx


def store_x(c, xT, Bx, xd):
    xdst = xd.rearrange("(kc p) t -> p kc t", p=128)
    for t in range(NT):
        c.store(xdst[:, :, t * 512:(t + 1) * 512], xT[:, :, t * 512:(t + 1) * 512], [Bx[t]])


def phase_C1(c, io, xT, Bx):
    S = c.S
    cst, cstb, Bcst = load_consts(c, io["cst"])
    ones_f = cst[:, CST["ones"]:CST["ones"] + 128]
    pv = c.sb([128, NPV], F32, "pv")
    Bpv = Buf("pv")
    S.dma(I("dma_start", out=pv[:], in_=io["pv"]), writes=[Bpv])
    PS = PsRot(c, 8)
    tmpf = Rot(c, [128, 512], F32, 3, "tmpf")
    sqr = Rot(c, [128, 512], F32, 3, "sqr")
    wrot = Rot(c, [128, 8, 512], BF16, 3, "w")
    ytr = Rot(c, [128, 8, 512], BF16, 2, "yt")
    gtr = Rot(c, [128, 8, 512], BF16, 2, "gt")
    mixed, Bmix = c.sb([128, 8, 512], F32, "mixed"), Buf("mixed")
    mixb, Bmixb = c.sb([128, 8, 512], BF16, "mixb"), Buf("mixb")
    h2r = Rot(c, [128, 8, 512], BF16, 2, "h2")
    ysrc = [io["yfox"], io["ylru"], io["ysb"], io["ymem"]]
    for t in range(NT):
        ts = slice(t * 512, (t + 1) * 512)
        for i in range(4):
            yt, Byt = ytr.next()
            S.dma(I("dma_start", out=yt[:], in_=ysrc[i][:, ts].rearrange("(kc p) t -> p kc t", p=128)), writes=[Byt])
            gt, Bgt = gtr.next()
            S.dma(I("dma_start", out=gt[:], in_=io["gates"][i][:, ts].rearrange("(kc p) t -> p kc t", p=128)), writes=[Bgt])
            for half in range(2):
                wt, Bw = wload(c, wrot, io["w_branch"][i], half * 512, 512)
                for o4 in range(4):
                    oc = half * 4 + o4
                    ps, Bp = PS.next()
                    for kc in range(8):
                        S.pe(I("matmul", out=ps[:, 0:512], lhsT=wt[:, kc, o4 * 128:(o4 + 1) * 128], rhs=yt[:, kc, :],
                               start=(kc == 0), stop=(kc == 7)), reads=[Bw, Byt], writes=[Bp])
                    if i == 0:
                        S.dve(I("tensor_tensor", out=mixed[:, oc, :], in0=ps[:, 0:512], in1=gt[:, oc, :], op=ALU.mult),
                              reads=[Bp, Bgt], writes=[Bmix], join=(oc > 0))
                    else:
                        tm, Btm = tmpf.next()
                        S.dve(I("tensor_tensor", out=tm[:], in0=ps[:, 0:512], in1=gt[:, oc, :], op=ALU.mult), reads=[Bp, Bgt], writes=[Btm])
                        S.dve(I("tensor_tensor", out=mixed[:, oc, :], in0=mixed[:, oc, :], in1=tm[:], op=ALU.add),
                              reads=[Btm, Bmix], writes=[Bmix])
        S.act(I("activation", out=mixb[:], in_=mixed[:], func=AF.Copy), reads=[Bmix], writes=[Bmixb])
        for half in range(2):
            wt, Bw = wload(c, wrot, io["w_out"], half * 512, 512)
            for o4 in range(4):
                oc = half * 4 + o4
                ps, Bp = PS.next()
                for kc in range(8):
                    S.pe(I("matmul", out=ps[:, 0:512], lhsT=wt[:, kc, o4 * 128:(o4 + 1) * 128], rhs=mixb[:, kc, :],
                           start=(kc == 0), stop=(kc == 7)), reads=[Bw, Bmixb], writes=[Bp])
                S.dve(I("tensor_tensor", out=xT[:, oc, ts], in0=xT[:, oc, ts], in1=ps[:, 0:512], op=ALU.add), reads=[Bp, Bx[t]], writes=[Bx[t]])
        h2, Bh2 = h2r.next()
        rmsnorm_tile(c, PS, xT, Bx[t], ts, PV["ffn_g"], pv, Bpv, ones_f, Bcst, tmpf, sqr, h2, slice(0, 512), Bh2)
        c.store(io["h2"][:, ts].rearrange("(kc p) t -> p kc t", p=128), h2[:], [Bh2])


def phase_C2(c, io, xT, Bx):
    S = c.S
    pv = c.sb([128, NPV], F32, "pv")
    Bpv = Buf("pv")
    S.dma(I("dma_start", out=pv[:], in_=io["pv"]), writes=[Bpv])
    PS = PsRot(c, 8)
    wrot = Rot(c, [128, 8, 512], BF16, 4, "w")
    wdrot = Rot(c, [128, NCC, 512], BF16, 2, "wd")
    h2r = Rot(c, [128, 8, 514], BF16, 2, "h2e")
    Gr = Rot(c, [128, 514], F32, 3, "G")
    gpr = Rot(c, [128, 512], F32, 3, "gp")
    hid, Bhid = c.sb([128, NCC, 512], BF16, "hid"), Buf("hid")
    h2src = io["h2"].rearrange("(kc p) t -> p kc t", p=128)
    for t in range(NT):
        ts = slice(t * 512, (t + 1) * 512)
        h2e, Bh = h2r.next()
        S.dma(I("dma_start", out=h2e[:, :, 2:514], in_=h2src[:, :, ts]), writes=[Bh])
        hsrc = io["h2halo"].rearrange("(kc p) t -> p kc t", p=128) if t == 0 else h2src[:, :, t * 512 - 2:t * 512]
        S.dma(I("dma_start", out=h2e[:, :, 0:2], in_=hsrc), writes=[Bh], join=True)
        for w6 in range(6):
            ncol = 512 if w6 < 5 else 256
            wg, Bwg = wload(c, wrot, io["w_up"], w6 * 512, ncol)
            wv, Bwv = wload(c, wrot, io["w_up"], DFF + w6 * 512, ncol)
            for c4 in range(ncol // 128):
                cc = w6 * 4 + c4
                pg, Bpg = PS.next()
                for kc in range(8):
                    S.pe(I("matmul", out=pg[:, 0:512], lhsT=wg[:, kc, c4 * 128:(c4 + 1) * 128], rhs=h2e[:, kc, 2:514],
                           start=(kc == 0), stop=(kc == 7)), reads=[Bwg, Bh], writes=[Bpg])
                ph, Bph = PS.next()
                for kc in range(8):
                    S.pe(I("matmul", out=ph[:, 0:2], lhsT=wg[:, kc, c4 * 128:(c4 + 1) * 128], rhs=h2e[:, kc, 0:2],
                           start=(kc == 0), stop=(kc == 7)), reads=[Bwg, Bh], writes=[Bph])
                G, BG = Gr.next()
                S.act(I("activation", out=G[:, 2:514], in_=pg[:, 0:512], func=AF.Copy), reads=[Bpg], writes=[BG])
                S.act(I("activation", out=G[:, 0:2], in_=ph[:, 0:2], func=AF.Copy), reads=[Bph], writes=[BG], join=True)
                gp, Bgp = gpr.next()
                cw = lambda i: pv[:, PV["ffn_cw"] + NCC * i + cc:PV["ffn_cw"] + NCC * i + cc + 1]
                S.dve(I("tensor_scalar", out=gp[:], in0=G[:, 2:514], scalar1=cw(2), scalar2=pv[:, PV["ffn_cb"] + cc:PV["ffn_cb"] + cc + 1],
                        op0=ALU.mult, op1=ALU.add), reads=[BG, Bpv], writes=[Bgp])
                S.dve(I("scalar_tensor_tensor", out=gp[:], in0=G[:, 1:513], scalar=cw(1), in1=gp[:], op0=ALU.mult, op1=ALU.add),
                      reads=[BG, Bpv, Bgp], writes=[Bgp])
                S.dve(I("scalar_tensor_tensor", out=gp[:], in0=G[:, 0:512], scalar=cw(0), in1=gp[:], op0=ALU.mult, op1=ALU.add),
                      reads=[BG, Bpv, Bgp], writes=[Bgp])
                S.act(I("activation", out=gp[:], in_=gp[:], func=AF.Silu), reads=[Bgp], writes=[Bgp])
                pvv, Bpvv = PS.next()
                for kc in range(8):
                    S.pe(I("matmul", out=pvv[:, 0:512], lhsT=wv[:, kc, c4 * 128:(c4 + 1) * 128], rhs=h2e[:, kc, 2:514],
                           start=(kc == 0), stop=(kc == 7)), reads=[Bwv, Bh], writes=[Bpvv])
                S.dve(I("tensor_tensor", out=hid[:, cc, :], in0=pvv[:, 0:512], in1=gp[:], op=ALU.mult), reads=[Bpvv, Bgp], writes=[Bhid],
                      join=(cc > 0))
        for half in range(2):
            wd, Bwd = wload(c, wdrot, io["w_down"], half * 512, 512, nk=NCC)
            for o4 in range(4):
                oc = half * 4 + o4
                ps, Bp = PS.next()
                for cc in range(NCC):
                    S.pe(I("matmul", out=ps[:, 0:512], lhsT=wd[:, cc, o4 * 128:(o4 + 1) * 128], rhs=hid[:, cc, :],
                           start=(cc == 0), stop=(cc == NCC - 1)), reads=[Bwd, Bhid], writes=[Bp])
                S.dve(I("tensor_tensor", out=xT[:, oc, ts], in0=xT[:, oc, ts], in1=ps[:, 0:512], op=ALU.add), reads=[Bp, Bx[t]], writes=[Bx[t]])


def build_C1():
    c = Ctx()
    io = dict(cst=c.din("cst", [128, NCST]), pv=c.din("pv", [128, NPV]), xT=c.din("xT", [D, TOK]),
              yfox=c.din("yfox", [D, TOK], BF16), ylru=c.din("ylru", [D, TOK], BF16), ysb=c.din("ysb", [D, TOK], BF16),
              ymem=c.din("ymem", [D, TOK], BF16), gates=[c.din("gates%d" % i, [D, TOK], BF16) for i in range(4)],
              w_branch=[c.din("w_branch%d" % i, [D, D]) for i in range(4)], w_out=c.din("w_out", [D, D]),
              xo=c.dout("xo", [D, TOK]), h2=c.dout("h2", [D, TOK], BF16))
    xT, Bx = load_x(c, io["xT"])
    phase_C1(c, io, xT, Bx)
    store_x(c, xT, Bx, io["xo"])
    return c.finish()


def build_C2():
    c = Ctx()
    io = dict(pv=c.din("pv", [128, NPV]), xT=c.din("xT", [D, TOK]), h2=c.din("h2", [D, TOK], BF16), h2halo=c.din("h2halo", [D, 2], BF16),
              w_up=c.din("w_up", [D, 2 * DFF]), w_down=c.din("w_down", [DFF, D]), xo=c.dout("xo", [D, TOK]))
    xT, Bx = load_x(c, io["xT"])
    phase_C2(c, io, xT, Bx)
    store_x(c, xT, Bx, io["xo"])
    return c.finish()


_NC = {}


def _prog(name, builder):
    if name not in _NC:
        _NC[name] = builder()
    return _NC[name]


def _run(nc, maps):
    return run_bass_kernel_spmd(nc, maps, core_ids=list(range(8))).results


def kernel(_nlayers=4, **inp):
    inp = {k: np.asarray(v) for k, v in inp.items()}
    bf = ml_dtypes.bfloat16
    cst = host_consts()
    xs = [np.ascontiguousarray(inp["x"][c // 4, (c % 4) * TOK:(c % 4 + 1) * TOK].T) for c in range(8)]
    memT = [np.ascontiguousarray(inp["mem"][b].T) for b in range(2)]
    zhalo = np.zeros((D, 2), bf)
    for l in range(_nlayers):
        pv = host_pvec(inp, l)
        w_in = np.ascontiguousarray(inp["w_in"][l])
        w_kv = np.ascontiguousarray(inp["w_mem_kv"][l])
        rA = _run(_prog("A", build_A), [dict(xT=xs[c], memT=memT[c // 4], w_in=w_in, w_kv=w_kv, pv=pv, cst=cst) for c in range(8)])
        mapsB = []
        for c in range(8):
            b, g = c // 4, c % 4
            src = [rA[b * 4 + j] for j in range(4)]

            def rows(key, n, src=src, g=g):
                return np.ascontiguousarray(np.concatenate([np.asarray(s[key])[g * n:(g + 1) * n] for s in src], 0))

            def toks(key, src=src, g=g):
                return np.ascontiguousarray(np.concatenate([np.asarray(s[key])[g] for s in src], 0))
            pl, wbd = host_plru(inp, l, g)
            mapsB.append(dict(cst=cst, fq=rows("fq", 256), fk=rows("fk", 256), fv=toks("fv"), logf=rows("logf", 4),
                              sq=rows("sq", 256), sk=rows("sk", 256), sv=toks("sv"), lx=rows("lx", 256), lg=rows("lg", 256),
                              pl=pl, wbd=wbd))
        rB = _run(_prog("B", build_B), mapsB)
        mapsC = []
        for c in range(8):
            b, j = c // 4, c % 4

            def gath(key, b=b, j=j):
                return np.ascontiguousarray(np.concatenate([np.asarray(rB[b * 4 + g][key])[j * 256:(j + 1) * 256] for g in range(4)], 0))
            m = dict(cst=cst, pv=pv, xT=xs[c], yfox=gath("yfox"), ylru=gath("ylru"), ysb=gath("ysb"), ymem=np.asarray(rA[c]["ymem"]),
                     w_out=np.ascontiguousarray(inp["w_out"][l]))
            for i in range(4):
                m["gates%d" % i] = np.asarray(rA[c]["gates%d" % i])
                m["w_branch%d" % i] = np.ascontiguousarray(inp["w_branch"][l][i])
            mapsC.append(m)
        rC1 = _run(_prog("C1", build_C1), mapsC)
        mapsC2 = []
        for c in range(8):
            halo = zhalo if c % 4 == 0 else np.ascontiguousarray(np.asarray(rC1[c - 1]["h2"])[:, -2:])
            mapsC2.append(dict(pv=pv, xT=np.asarray(rC1[c]["xo"]), h2=np.asarray(rC1[c]["h2"]), h2halo=halo,
                               w_up=np.ascontiguousarray(inp["w_up"][l]), w_down=np.ascontiguousarray(inp["w_down"][l])))
        rC2 = _run(_prog("C2", build_C2), mapsC2)
        xs = [np.asarray(rC2[c]["xo"]) for c in range(8)]
    out = np.zeros((2, S_LEN, D), np.float32)
    for c in range(8):
        out[c // 4, (c % 4) * TOK:(c % 4 + 1) * TOK] = xs[c].T
    return out
```

```python
import contextlib
import numpy as np
import ml_dtypes
import concourse.bass as bass
import concourse.mybir as mybir
from concourse.bass_utils import run_bass_kernel_spmd

F32 = mybir.dt.float32
BF16 = mybir.dt.bfloat16
AF = mybir.ActivationFunctionType
ALU = mybir.AluOpType

ENGS = ("pe", "act", "dve", "pool", "sp")
NDMASEM = 8

D = 1024
S_LEN = 8192
TOK = 2048
NT = 4
DFF = 2816
NCC = 22
N_IN = 13328
EPS = 1e-6
COLS = dict(fox_q=0, fox_k=1024, fox_v=2048, fox_f=3072, lru_x=3088, lru_g=4112,
            sb_q=5136, sb_k=6160, sb_v=7184, mem_q=8208, gates=9232)


class Buf:
    __slots__ = ("name", "w", "r", "pre")

    def __init__(self, name=""):
        self.name = name
        self.w = []
        self.r = {}
        self.pre = []


def I(name, **kw):
    return (name, kw)


class Ins:
    __slots__ = ("eng", "fn", "deps", "signal", "sem", "val", "dma", "slot")

    def __init__(self, eng, fn, dma):
        self.eng = eng
        self.fn = fn
        self.deps = []
        self.signal = False
        self.sem = None
        self.val = 0
        self.dma = dma
        self.slot = -1


class Sched:
    def __init__(self, nc):
        self.nc = nc
        self.q = {e: [] for e in ENGS}
        self.ndma = {e: 0 for e in ENGS}
        self.uid = 0

    def add(self, eng, fn, reads=(), writes=(), dma=False, join=False):
        ins = Ins(eng, fn, dma)
        deps = {}
        for b in reads:
            for d in b.w:
                deps[id(d)] = d
        for b in writes:
            if join:
                for d in b.pre:
                    deps[id(d)] = d
            else:
                pre = list(b.w) + list(b.r.values())
                for d in pre:
                    deps[id(d)] = d
                b.pre = pre
        for b in writes:
            if join:
                b.w.append(ins)
            else:
                b.w = [ins]
                b.r = {}
        for b in reads:
            if dma:
                self.uid += 1
                b.r[("dma", self.uid)] = ins
            else:
                b.r[eng] = ins
        if dma:
            n = self.ndma[eng]
            ins.slot = n % NDMASEM
            self.ndma[eng] = n + 1
        deps.pop(id(ins), None)
        ins.deps = list(deps.values())
        for d in ins.deps:
            if not (eng == "pe" and d.eng == "pe" and not d.dma):
                d.signal = True
        self.q[eng].append(ins)
        return ins

    def pe(self, fn, reads=(), writes=(), join=False):
        return self.add("pe", fn, reads, writes, join=join)

    def act(self, fn, reads=(), writes=(), join=False):
        return self.add("act", fn, reads, writes, join=join)

    def dve(self, fn, reads=(), writes=(), join=False):
        return self.add("dve", fn, reads, writes, join=join)

    def pool(self, fn, reads=(), writes=(), join=False):
        return self.add("pool", fn, reads, writes, join=join)

    def dma(self, fn, reads=(), writes=(), q="sp", join=False):
        return self.add(q, fn, reads, writes, dma=True, join=join)

    def emit(self, st, final_wait=()):
        nc = self.nc
        esem = {e: st.enter_context(nc.semaphore("tl_" + e)) for e in ENGS}
        dsem = {e: [st.enter_context(nc.semaphore("dm_%s%d" % (e, k))) for k in range(NDMASEM)]
                for e in ENGS if self.ndma[e] > 0}
        for e in ENGS:
            cnt = 0
            dcnt = [0] * NDMASEM
            prev = [None] * NDMASEM
            for ins in self.q[e]:
                if ins.dma:
                    k = ins.slot
                    dcnt[k] += 16
                    ins.sem = dsem[e][k]
                    ins.val = dcnt[k]
                    if prev[k] is not None:
                        ins.deps.append(prev[k])
                    prev[k] = ins
                elif ins.signal:
                    cnt += 1
                    ins.sem = esem[e]
                    ins.val = cnt
        fin = list(final_wait)
        block = st.enter_context(nc.Block())
        engobj = {"pe": block.tensor, "act": block.scalar, "dve": block.vector,
                  "pool": block.gpsimd, "sp": block.sync}

        def make(e):
            def body(eng):
                waited = {}

                def dowaits(deps):
                    need = {}
                    for d in deps:
                        if d.sem is None:
                            continue
                        if (not d.dma) and d.eng == e and e == "pe":
                            continue
                        k = id(d.sem)
                        if k not in need or need[k][1] < d.val:
                            need[k] = (d.sem, d.val)
                    for k, (s, v) in need.items():
                        if waited.get(k, 0) >= v:
                            continue
                        eng.wait_ge(s, v)
                        waited[k] = v

                for ins in self.q[e]:
                    dowaits(ins.deps)
                    if ins.fn is None:
                        continue
                    r = getattr(eng, ins.fn[0])(**ins.fn[1]) if isinstance(ins.fn, tuple) else ins.fn(eng)
                    if ins.dma:
                        r.then_inc(ins.sem, 16)
                    elif ins.signal:
                        r.then_inc(ins.sem, 1)
                if e == "sp":
                    dowaits(fin)
            return body

        for e in ENGS:
            if self.q[e] or (e == "sp" and fin):
                engobj[e](make(e))


class Ctx:
    def __init__(self):
        self.nc = bass.Bass("TRN2", target_bir_lowering=False)
        self.st = contextlib.ExitStack()
        self.S = Sched(self.nc)
        self.n = 0
        self.outs = []

    def din(self, name, shape, dt=F32):
        return self.nc.dram_tensor(name, list(shape), dt, kind="ExternalInput").ap()

    def dout(self, name, shape, dt=F32):
        return self.nc.dram_tensor(name, list(shape), dt, kind="ExternalOutput").ap()

    def sb(self, shape, dt, name="t"):
        self.n += 1
        return self.st.enter_context(self.nc.sbuf_tensor("%s_%d" % (name, self.n), list(shape), dt))

    def psum(self):
        self.n += 1
        return self.st.enter_context(self.nc.psum_tensor("ps_%d" % self.n, [128, 512], F32))

    def store(self, dst, src, reads):
        ins = self.S.dma(lambda e: e.dma_start(out=dst, in_=src), reads=reads)
        self.outs.append(ins)
        return ins

    def finish(self):
        self.S.emit(self.st, final_wait=self.outs)
        self.st.close()
        return self.nc


class Rot:
    def __init__(self, c, shape, dt, n, name):
        self.t = [c.sb(shape, dt, name) for _ in range(n)]
        self.b = [Buf(name) for _ in range(n)]
        self.i = 0

    def next(self):
        k = self.i % len(self.t)
        self.i += 1
        return self.t[k], self.b[k]


class PsRot:
    def __init__(self, c, n):
        self.t = [c.psum() for _ in range(n)]
        self.b = [Buf("ps") for _ in range(n)]
        self.i = 0

    def next(self):
        k = self.i % len(self.t)
        self.i += 1
        return self.t[k], self.b[k]


PV = {}
_o = 0
for _n, _w in (("attn_g", 8), ("mem_g", 8), ("ffn_g", 8), ("fq_g", 1), ("fk_g", 1), ("mq_g", 2), ("mk_g", 2),
               ("b_gate", 32), ("b_f", 1), ("ffn_cw", 66), ("ffn_cb", 22)):
    PV[_n] = _o
    _o += _w
NPV = _o
PL = dict(cw=0, cb=8, ba=10, bx=12, lam=14)
NPL = 16
CST = dict(ident=0, tri=128, strict=256, bones=384, ones=512, niu=640)
NCST = 768


def host_consts():
    c = np.zeros((128, NCST), np.float32)
    k = np.arange(128)[:, None]
    q = np.arange(128)[None, :]
    c[:, 0:128] = np.eye(128)
    c[:, 128:256] = (q >= k)
    c[:, 256:384] = (k < q)
    c[:, 384:512] = ((k // 64) == (q // 64))
    c[:, 512:640] = 1.0
    c[:, 640:768] = -(k >= q).astype(np.float32)
    return c


def host_pvec(inp, l):
    pv = np.zeros((128, NPV), np.float32)

    def cols(v):
        return np.ascontiguousarray(v.reshape(-1, 128).T)
    pv[:, PV["attn_g"]:PV["attn_g"] + 8] = cols(inp["attn_norm_g"][l])
    pv[:, PV["mem_g"]:PV["mem_g"] + 8] = cols(inp["mem_norm_g"][l])
    pv[:, PV["ffn_g"]:PV["ffn_g"] + 8] = cols(inp["ffn_norm_g"][l])
    pv[:, PV["fq_g"]] = np.tile(inp["fox_q_norm_g"][l], 2)
    pv[:, PV["fk_g"]] = np.tile(inp["fox_k_norm_g"][l], 2)
    pv[:, PV["mq_g"]:PV["mq_g"] + 2] = cols(inp["mem_q_norm_g"][l])
    pv[:, PV["mk_g"]:PV["mk_g"] + 2] = cols(inp["mem_k_norm_g"][l])
    pv[:, PV["b_gate"]:PV["b_gate"] + 32] = cols(inp["b_gate"][l].reshape(-1))
    pv[0:16, PV["b_f"]] = inp["b_forget"][l]
    for i in range(3):
        pv[:, PV["ffn_cw"] + 22 * i:PV["ffn_cw"] + 22 * (i + 1)] = cols(inp["ffn_conv_w"][l][i])
    pv[:, PV["ffn_cb"]:PV["ffn_cb"] + 22] = cols(inp["ffn_conv_b"][l])
    return pv


def host_plru(inp, l, g):
    pl = np.zeros((128, NPL), np.float32)
    sl = slice(256 * g, 256 * (g + 1))

    def cols(v):
        return np.ascontiguousarray(v.reshape(-1, 128).T)
    for i in range(4):
        pl[:, PL["cw"] + 2 * i:PL["cw"] + 2 * i + 2] = cols(inp["lru_conv_w"][l][i, sl])
    pl[:, PL["cb"]:PL["cb"] + 2] = cols(inp["lru_conv_b"][l][sl])
    pl[:, PL["ba"]:PL["ba"] + 2] = cols(inp["lru_b_a"][l][sl])
    pl[:, PL["bx"]:PL["bx"] + 2] = cols(inp["lru_b_x"][l][sl])
    pl[:, PL["lam"]:PL["lam"] + 2] = cols(inp["lru_lambda"][l][sl])
    wbd = np.zeros((128, 2, 2, 128), np.float32)
    for ci in range(2):
        for blk in range(2):
            n = 4 * g + 2 * ci + blk
            wbd[64 * blk:64 * blk + 64, 0, ci, 64 * blk:64 * blk + 64] = inp["lru_w_a"][l][n]
            wbd[64 * blk:64 * blk + 64, 1, ci, 64 * blk:64 * blk + 64] = inp["lru_w_x"][l][n]
    return pl, wbd.reshape(128, 512)


def load_consts(c, cst_d):
    S = c.S
    cst = c.sb([128, NCST], F32, "cst")
    cstb = c.sb([128, NCST], BF16, "cstb")
    B = Buf("cst")
    S.dma(lambda e: e.dma_start(out=cst[:], in_=cst_d), writes=[B])
    S.dve(lambda e: e.tensor_copy(out=cstb[:], in_=cst[:]), reads=[B], writes=[B])
    return cst, cstb, B


def rms_stats(c, PS, n_items, sq_fn, inv_n, ones_ap, Bcst, tmp_rot, N=512):
    S = c.S
    ps, Bp = PS.next()
    for i in range(n_items):
        ap, B = sq_fn(i)
        S.pe(lambda e, ap=ap, i=i: e.matmul(ps[:, 0:N], ones_ap, ap, start=(i == 0), stop=(i == n_items - 1)),
             reads=[B, Bcst], writes=[Bp])
    t, Bt = tmp_rot.next()
    S.dve(lambda e: e.tensor_scalar(out=t[:, 0:N], in0=ps[:, 0:N], scalar1=inv_n, scalar2=EPS, op0=ALU.mult, op1=ALU.add),
          reads=[Bp], writes=[Bt])
    S.act(lambda e: e.activation(out=t[:, 0:N], in_=t[:, 0:N], func=AF.Ln), reads=[Bt], writes=[Bt])
    S.act(lambda e: e.activation(out=t[:, 0:N], in_=t[:, 0:N], func=AF.Exp, scale=-0.5), reads=[Bt], writes=[Bt])
    return t, Bt


def wload(c, wrot, w_d, c0, ncols, nk=8):
    wt, Bw = wrot.next()
    src = w_d[:, c0:c0 + ncols].rearrange("(kc p) n -> p kc n", p=128)
    c.S.dma(lambda e: e.dma_start(out=wt[:, 0:nk, 0:ncols], in_=src), writes=[Bw], q="pool")
    return wt, Bw


def rmsnorm_tile(c, PS, xT, Bx, ts, gcol0, pv, Bpv, ones_f, Bcst, tmpf, sqr, hT, hts, Bh):
    S = c.S

    def sqf(kc):
        sq, Bs = sqr.next()
        S.act(lambda e: e.activation(out=sq[:], in_=xT[:, kc, ts], func=AF.Square), reads=[Bx], writes=[Bs])
        return sq[:], Bs
    rstd, Br = rms_stats(c, PS, 8, sqf, 1.0 / 1024, ones_f, Bcst, tmpf)
    for kc in range(8):
        S.dve(lambda e, kc=kc: e.scalar_tensor_tensor(out=hT[:, kc, hts], in0=xT[:, kc, ts],
                                                      scalar=pv[:, gcol0 + kc:gcol0 + kc + 1], in1=rstd[:],
                                                      op0=ALU.mult, op1=ALU.mult),
              reads=[Bx, Br, Bpv], writes=[Bh])


def phase_A(c, io):
    S = c.S
    cst, cstb, Bcst = load_consts(c, io["cst"])
    ones_f = cst[:, CST["ones"]:CST["ones"] + 128]
    bones_f = cst[:, CST["bones"]:CST["bones"] + 128]
    ones_b = cstb[:, CST["ones"]:CST["ones"] + 128]
    pv = c.sb([128, NPV], F32, "pv")
    Bpv = Buf("pv")
    S.dma(lambda e: e.dma_start(out=pv[:], in_=io["pv"]), writes=[Bpv])
    dpar = c.sb([128, 8], F32, "dpar")
    Bdp = Buf("dpar")
    S.dve(lambda e: e.tensor_scalar(out=dpar[:, 0:1], in0=pv[:, PV["fq_g"]:PV["fq_g"] + 1], scalar1=0.125, scalar2=None, op0=ALU.mult),
          reads=[Bpv], writes=[Bdp])
    S.dve(lambda e: e.tensor_scalar(out=dpar[:, 1:3], in0=pv[:, PV["mq_g"]:PV["mq_g"] + 2], scalar1=0.0625, scalar2=None, op0=ALU.mult),
          reads=[Bpv], writes=[Bdp])
    S.dve(lambda e: e.tensor_scalar(out=dpar[:, 3:4], in0=pv[:, PV["b_f"]:PV["b_f"] + 1], scalar1=-1.0, scalar2=None, op0=ALU.mult),
          reads=[Bpv], writes=[Bdp])

    PS = PsRot(c, 8)
    tmpf = Rot(c, [128, 512], F32, 4, "tmpf")
    sqr = Rot(c, [128, 512], F32, 4, "sqr")
    outb = Rot(c, [128, 512], BF16, 4, "outb")
    outf = Rot(c, [128, 512], F32, 3, "outf")
    wrot = Rot(c, [128, 8, 512], BF16, 3, "w")

    xT = c.sb([128, 8, TOK], F32, "xT")
    hT = c.sb([128, 8, TOK], BF16, "hT")
    Bx = [Buf("x%d" % t) for t in range(NT)]
    Bh = [Buf("h%d" % t) for t in range(NT)]
    xsrc = io["xT"].rearrange("(kc p) t -> p kc t", p=128)
    TS = [slice(t * 512, (t + 1) * 512) for t in range(NT)]
    for t in range(NT):
        S.dma(lambda e, t=t: e.dma_start(out=xT[:, :, TS[t]], in_=xsrc[:, :, TS[t]]), writes=[Bx[t]])
    for t in range(NT):
        rmsnorm_tile(c, PS, xT, Bx[t], TS[t], PV["attn_g"], pv, Bpv, ones_f, Bcst, tmpf, sqr, hT, TS[t], Bh[t])

    if io.get('stop', 99) <= 1:
        return
    memT = c.sb([128, 8, 256], F32, "memT")
    memn = c.sb([128, 8, 256], BF16, "memn")
    Bmem, Bmemn = Buf("mem"), Buf("memn")
    S.dma(lambda e: e.dma_start(out=memT[:], in_=io["memT"].rearrange("(kc p) m -> p kc m", p=128)), writes=[Bmem])

    def sqm(kc):
        sq, Bs = sqr.next()
        S.act(lambda e: e.activation(out=sq[:, 0:256], in_=memT[:, kc, :], func=AF.Square), reads=[Bmem], writes=[Bs])
        return sq[:, 0:256], Bs
    rstd, Br = rms_stats(c, PS, 8, sqm, 1.0 / 1024, ones_f, Bcst, tmpf, N=256)
    for kc in range(8):
        S.dve(lambda e, kc=kc, rstd=rstd: e.scalar_tensor_tensor(out=memn[:, kc, :], in0=memT[:, kc, :],
                                                      scalar=pv[:, PV["mem_g"] + kc:PV["mem_g"] + kc + 1], in1=rstd[:, 0:256],
                                                      op0=ALU.mult, op1=ALU.mult),
              reads=[Bmem, Br, Bpv], writes=[Bmemn])
    mkT = c.sb([128, 4, 2, 256], BF16, "mkT")
    mv = c.sb([128, 2, 1024], BF16, "mv")
    Bmk, Bmv = Buf("mk"), Buf("mv")
    for wi in range(2):
        wt, Bw = wload(c, wrot, io["w_kv"], wi * 512, 512)
        for hh in range(2):
            h = wi * 2 + hh
            pss = []
            for ci in range(2):
                ps, Bp = PS.next()
                cc = hh * 2 + ci
                for kc in range(8):
                    S.pe(lambda e, ps=ps, kc=kc, cc=cc, wt=wt: e.matmul(ps[:, 0:256], wt[:, kc, cc * 128:(cc + 1) * 128], memn[:, kc, :],
                                                                       start=(kc == 0), stop=(kc == 7)),
                         reads=[Bw, Bmemn], writes=[Bp])
                pss.append((ps, Bp))

            def sqk(i):
                sq, Bs = sqr.next()
                ps, Bp = pss[i]
                S.act(lambda e: e.activation(out=sq[:, 0:256], in_=ps[:, 0:256], func=AF.Square), reads=[Bp], writes=[Bs])
                return sq[:, 0:256], Bs
            rstd, Br = rms_stats(c, PS, 2, sqk, 1.0 / 256, ones_f, Bcst, tmpf, N=256)
            for ci in range(2):
                ps, Bp = pss[ci]
                S.dve(lambda e, ps=ps, ci=ci, h=h, rstd=rstd: e.scalar_tensor_tensor(
                    out=mkT[:, h, ci, :], in0=ps[:, 0:256], scalar=pv[:, PV["mk_g"] + ci:PV["mk_g"] + ci + 1],
                    in1=rstd[:, 0:256], op0=ALU.mult, op1=ALU.mult), reads=[Bp, Br, Bpv], writes=[Bmk])
    for wi in range(2):
        wt, Bw = wload(c, wrot, io["w_kv"], 1024 + wi * 512, 512)
        for mc in range(2):
            ps, Bp = PS.next()
            for kc in range(8):
                S.pe(lambda e, ps=ps, kc=kc, mc=mc, wt=wt: e.matmul(ps[:, 0:512], memn[:, kc, mc * 128:(mc + 1) * 128], wt[:, kc, 0:512],
                                                                   start=(kc == 0), stop=(kc == 7)),
                     reads=[Bw, Bmemn], writes=[Bp])
            S.act(lambda e, ps=ps, mc=mc, wi=wi: e.activation(out=mv[:, mc, wi * 512:(wi + 1) * 512], in_=ps[:, 0:512], func=AF.Copy),
                  reads=[Bp], writes=[Bmv])

    if 'dbg_mk' in io:
        c.store(io['dbg_mk'], mkT[:].rearrange("p h c m -> p (h c m)"), [Bmk])
        c.store(io['dbg_mv'], mv[:].rearrange("p c d -> p (c d)"), [Bmv])
        c.store(io['dbg_memn'], memn[:].rearrange("p c d -> p (c d)"), [Bmemn])
    if io.get('stop', 99) <= 2:
        return
    w_in = io["w_in"]

    def proj_fm(wt, Bw, cc, t):
        ps, Bp = PS.next()
        for kc in range(8):
            S.pe(lambda e, kc=kc: e.matmul(ps[:, 0:512], wt[:, kc, cc * 128:(cc + 1) * 128], hT[:, kc, TS[t]],
                                           start=(kc == 0), stop=(kc == 7)),
                 reads=[Bw, Bh[t]], writes=[Bp])
        return ps, Bp

    def fam_qknorm(col0, gcol_ap, Bg, dst):
        for wi in range(2):
            wt, Bw = wload(c, wrot, w_in, col0 + wi * 512, 512)
            for cc in range(4):
                row0 = (wi * 4 + cc) * 128
                for t in range(NT):
                    ps, Bp = proj_fm(wt, Bw, cc, t)

                    def sqf(i, ps=ps, Bp=Bp):
                        sq, Bs = sqr.next()
                        S.act(lambda e: e.activation(out=sq[:], in_=ps[:, 0:512], func=AF.Square), reads=[Bp], writes=[Bs])
                        return sq[:], Bs
                    rstd, Br = rms_stats(c, PS, 1, sqf, 1.0 / 64, bones_f, Bcst, tmpf)
                    ob, Bo = outb.next()
                    S.dve(lambda e, ps=ps, ob=ob, rstd=rstd: e.scalar_tensor_tensor(out=ob[:], in0=ps[:, 0:512], scalar=gcol_ap, in1=rstd[:],
                                                                                    op0=ALU.mult, op1=ALU.mult),
                          reads=[Bp, Br, Bg], writes=[Bo])
                    c.store(dst[row0:row0 + 128, TS[t]], ob[:], [Bo])

    def fam_copy(col0, dst, scale, bf):
        for wi in range(2):
            wt, Bw = wload(c, wrot, w_in, col0 + wi * 512, 512)
            for cc in range(4):
                row0 = (wi * 4 + cc) * 128
                for t in range(NT):
                    ps, Bp = proj_fm(wt, Bw, cc, t)
                    ob, Bo = (outb if bf else outf).next()
                    S.act(lambda e, ps=ps, ob=ob: e.activation(out=ob[:], in_=ps[:, 0:512], func=AF.Copy, scale=scale),
                          reads=[Bp], writes=[Bo])
                    c.store(dst[row0:row0 + 128, TS[t]], ob[:], [Bo])

    def fam_v(col0, dst):
        for wi in range(2):
            wt, Bw = wload(c, wrot, w_in, col0 + wi * 512, 512)
            for tb in range(16):
                ps, Bp = PS.next()
                for kc in range(8):
                    S.pe(lambda e, ps=ps, kc=kc, tb=tb, wt=wt: e.matmul(ps[:, 0:512], hT[:, kc, tb * 128:(tb + 1) * 128], wt[:, kc, 0:512],
                                                                       start=(kc == 0), stop=(kc == 7)),
                         reads=[Bw, Bh[tb // 4]], writes=[Bp])
                ob, Bo = outb.next()
                S.act(lambda e, ps=ps, ob=ob: e.activation(out=ob[:], in_=ps[:, 0:512], func=AF.Copy), reads=[Bp], writes=[Bo])
                for gi in range(2):
                    c.store(dst[wi * 2 + gi, tb * 128:(tb + 1) * 128, :], ob[:, gi * 256:(gi + 1) * 256], [Bo])

    fam_qknorm(COLS["fox_q"], dpar[:, 0:1], Bdp, io["fq"])
    if io.get('stop', 99) <= 3:
        return
    fam_qknorm(COLS["fox_k"], pv[:, PV["fk_g"]:PV["fk_g"] + 1], Bpv, io["fk"])
    if io.get('stop', 99) <= 4:
        return
    fam_v(COLS["fox_v"], io["fv"])
    if io.get('stop', 99) <= 5:
        return
    wf = c.sb([128, 8, 16], BF16, "wf")
    Bwf = Buf("wf")
    S.dma(lambda e: e.dma_start(out=wf[:], in_=w_in[:, COLS["fox_f"]:COLS["fox_f"] + 16].rearrange("(kc p) n -> p kc n", p=128)),
          writes=[Bwf], q="pool")
    for t in range(NT):
        ps, Bp = PS.next()
        for kc in range(8):
            S.pe(lambda e, ps=ps, kc=kc, t=t: e.matmul(ps[0:16, 0:512], wf[:, kc, 0:16], hT[:, kc, TS[t]], start=(kc == 0), stop=(kc == 7)),
                 reads=[Bwf, Bh[t]], writes=[Bp])
        of, Bo = outf.next()
        S.act(lambda e, ps=ps, of=of: e.activation(out=of[0:16, :], in_=ps[0:16, 0:512], func=AF.Exp, bias=dpar[0:16, 3:4], scale=-1.0),
              reads=[Bp, Bdp], writes=[Bo])
        S.act(lambda e, of=of: e.activation(out=of[0:16, :], in_=of[0:16, :], func=AF.Ln, bias=1.0), reads=[Bo], writes=[Bo])
        S.dve(lambda e, of=of: e.tensor_scalar(out=of[0:16, :], in0=of[0:16, :], scalar1=-1.0, scalar2=None, op0=ALU.mult),
              reads=[Bo], writes=[Bo])
        c.store(io["logf"][:, TS[t]], of[0:16, :], [Bo])
    if io.get('stop', 99) <= 6:
        return
    fam_copy(COLS["lru_x"], io["lx"], 1.0, False)
    fam_copy(COLS["lru_g"], io["lg"], 1.0, False)
    fam_copy(COLS["sb_q"], io["sq"], 0.125, True)
    fam_copy(COLS["sb_k"], io["sk"], 1.0, True)
    fam_v(COLS["sb_v"], io["sv"])
    if io.get('stop', 99) <= 7:
        return
    mqr = Rot(c, [128, 2, 512], BF16, 2, "mq")
    Er = Rot(c, [128, 2, 512], BF16, 2, "E")
    for wi in range(2):
        wt, Bw = wload(c, wrot, w_in, COLS["mem_q"] + wi * 512, 512)
        for hh in range(2):
            h = wi * 2 + hh
            for t in range(NT):
                pss = [proj_fm(wt, Bw, hh * 2 + ci, t) for ci in range(2)]

                def sqf(i, pss=pss):
                    sq, Bs = sqr.next()
                    ps, Bp = pss[i]
                    S.act(lambda e: e.activation(out=sq[:], in_=ps[:, 0:512], func=AF.Square), reads=[Bp], writes=[Bs])
                    return sq[:], Bs
                rstd, Br = rms_stats(c, PS, 2, sqf, 1.0 / 256, ones_f, Bcst, tmpf)
                mq, Bmq = mqr.next()
                for ci in range(2):
                    ps, Bp = pss[ci]
                    S.dve(lambda e, ps=ps, ci=ci, mq=mq, rstd=rstd: e.scalar_tensor_tensor(
                        out=mq[:, ci, :], in0=ps[:, 0:512], scalar=dpar[:, 1 + ci:2 + ci], in1=rstd[:], op0=ALU.mult, op1=ALU.mult),
                        reads=[Bp, Br, Bdp], writes=[Bmq])
                E, BE = Er.next()
                for mc in range(2):
                    ps, Bp = PS.next()
                    for dc in range(2):
                        S.pe(lambda e, ps=ps, dc=dc, mc=mc, h=h, mq=mq: e.matmul(ps[:, 0:512], mkT[:, h, dc, mc * 128:(mc + 1) * 128], mq[:, dc, :],
                                                                                start=(dc == 0), stop=(dc == 1)),
                             reads=[Bmk, Bmq], writes=[Bp])
                    S.act(lambda e, ps=ps, mc=mc, E=E: e.activation(out=E[:, mc, :], in_=ps[:, 0:512], func=AF.Exp), reads=[Bp], writes=[BE])
                if 'dbg_E' in io and h == 0 and t == 0:
                    c.store(io['dbg_E'], E[:].rearrange("p c q -> p (c q)"), [BE])
                    c.store(io['dbg_mq'], mq[:].rearrange("p c q -> p (c q)"), [Bmq])
                psd, Bpd = PS.next()
                for mc in range(2):
                    S.pe(lambda e, mc=mc, E=E, psd=psd: e.matmul(psd[:, 0:512], ones_b, E[:, mc, :], start=(mc == 0), stop=(mc == 1)),
                         reads=[BE, Bcst], writes=[Bpd])
                rden, Brd = tmpf.next()
                S.dve(lambda e, rden=rden, psd=psd: e.reciprocal(out=rden[:], in_=psd[:, 0:512]), reads=[Bpd], writes=[Brd])
                for dcp in range(2):
                    ps, Bp = PS.next()
                    for mc in range(2):
                        S.pe(lambda e, ps=ps, mc=mc, dcp=dcp, h=h, E=E: e.matmul(ps[:, 0:512], mv[:, mc, h * 256 + dcp * 128:h * 256 + (dcp + 1) * 128],
                                                                                E[:, mc, :], start=(mc == 0), stop=(mc == 1)),
                             reads=[BE, Bmv], writes=[Bp])
                    ob, Bo = outb.next()
                    S.dve(lambda e, ps=ps, ob=ob, rden=rden: e.tensor_tensor(out=ob[:], in0=ps[:, 0:512], in1=rden[:], op=ALU.mult),
                          reads=[Bp, Brd], writes=[Bo])
                    r0 = h * 256 + dcp * 128
                    c.store(io["ymem"][r0:r0 + 128, TS[t]], ob[:], [Bo])
    if io.get('stop', 99) <= 8:
        return
    for wi in range(io.get('ngw', 8)):
        wt, Bw = wload(c, wrot, w_in, COLS["gates"] + wi * 512, 512)
        for cc in range(4):
            gc = wi * 4 + cc
            for t in range(NT):
                ps, Bp = proj_fm(wt, Bw, cc, t)
                ob, Bo = outb.next()
                S.act(lambda e, ps=ps, ob=ob, gc=gc: e.activation(out=ob[:], in_=ps[:, 0:512], func=AF.Sigmoid,
                                                                 bias=pv[:, PV["b_gate"] + gc:PV["b_gate"] + gc + 1]),
                      reads=[Bp, Bpv], writes=[Bo])
                c.store(io["gates"][gc // 8][(gc % 8) * 128:(gc % 8 + 1) * 128, TS[t]], ob[:], [Bo])


def build_A(stop=99, ngw=8):
    c = Ctx()
    io = dict(stop=stop, ngw=ngw, xT=c.din("xT", [D, TOK]), memT=c.din("memT", [D, 256]), w_in=c.din("w_in", [D, N_IN]), w_kv=c.din("w_kv", [D, 2048]),
              pv=c.din("pv", [128, NPV]), cst=c.din("cst", [128, NCST]),
              fq=c.dout("fq", [D, TOK], BF16), fk=c.dout("fk", [D, TOK], BF16), fv=c.dout("fv", [4, TOK, 256], BF16),
              logf=c.dout("logf", [16, TOK]), lx=c.dout("lx", [D, TOK]), lg=c.dout("lg", [D, TOK]),
              sq=c.dout("sq", [D, TOK], BF16), sk=c.dout("sk", [D, TOK], BF16), sv=c.dout("sv", [4, TOK, 256], BF16),
              ymem=c.dout("ymem", [D, TOK], BF16), gates=[c.dout("gates%d" % i, [D, TOK], BF16) for i in range(4)])
    if stop == 88:
        io.update(dbg_E=c.dout('dbg_E', [128, 1024], BF16), dbg_mq=c.dout('dbg_mq', [128, 1024], BF16))
    if stop <= 2:
        io.update(dbg_mk=c.dout('dbg_mk', [128, 2048], BF16), dbg_mv=c.dout('dbg_mv', [128, 2048], BF16), dbg_memn=c.dout('dbg_memn', [128, 2048], BF16))
    phase_A(c, io)
    return c.finish()


def phase_B(c, io):
    S = c.S
    cst, cstb, Bcst = load_consts(c, io["cst"])
    ident_f = cst[:, CST["ident"]:CST["ident"] + 128]
    ones_f = cst[:, CST["ones"]:CST["ones"] + 128]
    tri_b = cstb[:, CST["tri"]:CST["tri"] + 128]
    strict_b = cstb[:, CST["strict"]:CST["strict"] + 128]
    niu_b = cstb[:, CST["niu"]:CST["niu"] + 128]
    negones = c.sb([128, 128], BF16, "negones")
    Bno = Buf("negones")
    S.dve(I("tensor_scalar", out=negones[:], in0=cst[:, CST["ones"]:CST["ones"] + 128], scalar1=-1.0, scalar2=None, op0=ALU.mult),
          reads=[Bcst], writes=[Bno])
    negones_f = c.sb([128, 128], F32, "negones_f")
    Bnof = Buf("negones_f")
    S.dve(I("tensor_scalar", out=negones_f[:], in0=cst[:, CST["ones"]:CST["ones"] + 128], scalar1=-1.0, scalar2=None, op0=ALU.mult),
          reads=[Bcst], writes=[Bnof])
    parts = io.get("parts", ("fox", "sb", "lru"))

    accPS = PsRot(c, 2)
    sPS = PsRot(c, 4)
    bcPS = PsRot(c, 2)
    qr = Rot(c, [65, S_LEN], BF16, 2, "qt")
    kr = Rot(c, [65, S_LEN], BF16, 2, "kt")
    vr = Rot(c, [128, 64, 66], BF16, 2, "v1")
    Pr = Rot(c, [128, 512], BF16, 4, "P")
    yor = Rot(c, [64, 512], BF16, 2, "yo")

    def load_head(qd, kd, vd, h):
        qt, Bq = qr.next()
        kt, Bk = kr.next()
        v1, Bv = vr.next()
        for j in range(4):
            r0 = j * 256 + h * 64
            S.dma(I("dma_start", out=qt[0:64, j * 2048:(j + 1) * 2048], in_=qd[r0:r0 + 64, :]), writes=[Bq], join=(j > 0))
            S.dma(I("dma_start", out=kt[0:64, j * 2048:(j + 1) * 2048], in_=kd[r0:r0 + 64, :]), writes=[Bk], join=(j > 0))
            S.dma(I("dma_start", out=v1[:, j * 16:(j + 1) * 16, 0:64],
                    in_=vd[j * 2048:(j + 1) * 2048, h * 64:(h + 1) * 64].rearrange("(kb p) f -> p kb f", p=128)),
                  writes=[Bv], join=(j > 0))
        return qt, Bq, kt, Bk, v1, Bv

    if "fox" in parts:
        cumb = c.sb([4, S_LEN], BF16, "cumb")
        negF = c.sb([128, 256], F32, "negF")
        Bcumb, BnegF = Buf("cumb"), Buf("negF")
        lfr = Rot(c, [4, 1024], F32, 2, "lf")
        cur = Rot(c, [4, 1024], F32, 2, "cum")
        recr = Rot(c, [128, 512], F32, 2, "rec")
        bcsr = Rot(c, [64, 512], F32, 2, "bcs")
        prevcm, Bprev = None, None
        for jj in range(8):
            j, off = jj // 2, (jj % 2) * 1024
            lf, Blf = lfr.next()
            S.dma(I("dma_start", out=lf[:], in_=io["logf"][j * 4:(j + 1) * 4, off:off + 1024]), writes=[Blf])
            S.dve(I("tensor_scalar", out=lf[:], in0=lf[:], scalar1=0.5, scalar2=None, op0=ALU.mult), reads=[Blf], writes=[Blf])
            cm, Bcm = cur.next()
            init = 0.0 if jj == 0 else prevcm[:, 1023:1024]
            S.dve(I("tensor_tensor_scan", out=cm[:], data0=lf[:], data1=lf[:], initial=init, op0=ALU.add, op1=ALU.add),
                  reads=[Blf] + ([Bprev] if jj else []), writes=[Bcm])
            S.dve(I("tensor_copy", out=cumb[:, jj * 1024:(jj + 1) * 1024], in_=cm[:]), reads=[Bcm], writes=[Bcumb])
            pt, Bpt = sPS.next()
            for kb in range(8):
                S.pe(I("transpose", out=pt[:, kb * 4:(kb + 1) * 4], in_=cm[0:4, kb * 128:(kb + 1) * 128], identity=ident_f[0:4, 0:4]),
                     reads=[Bcm, Bcst], writes=[Bpt])
            S.dve(I("tensor_scalar", out=negF[:, jj * 32:(jj + 1) * 32], in0=pt[:, 0:32], scalar1=-1.0, scalar2=None, op0=ALU.mult),
                  reads=[Bpt], writes=[BnegF])
            prevcm, Bprev = cm, Bcm
        for h in range(4):
            qt, Bq, kt, Bk, v1, Bv = load_head(io["fq"], io["fk"], io["fv"], h)
            S.dma(I("dma_start", out=qt[64:65, :], in_=cumb[h:h + 1, :]), reads=[Bcumb], writes=[Bq], join=True)
            S.dve(I("memset", ap=kt[64:65, :], constant=1.0), writes=[Bk], join=True)
            S.dve(I("memset", ap=v1[:, :, 64:66], constant=1.0), writes=[Bv], join=True)
            units = [(I_, J) for I_ in range(16) for J in range(4 * I_ + 4)]
            st = {}

            def fox_a(u):
                I_, J = units[u]
                c0 = max(0, J - 4 * I_) * 128
                sc, Bs = sPS.next()
                S.pe(I("matmul", out=sc[:, c0:512], lhsT=kt[0:65, J * 128:(J + 1) * 128], rhs=qt[0:65, I_ * 512 + c0:(I_ + 1) * 512],
                       start=True, stop=True), reads=[Bk, Bq], writes=[Bs])
                pt, Bp = Pr.next()
                S.act(I("activation", out=pt[:, c0:512], in_=sc[:, c0:512], func=AF.Exp, bias=negF[:, J * 4 + h:J * 4 + h + 1]),
                      reads=[Bs, BnegF], writes=[Bp])
                if J >= 4 * I_:
                    S.dve(I("tensor_tensor", out=pt[:, c0:c0 + 128], in0=pt[:, c0:c0 + 128], in1=tri_b, op=ALU.mult),
                          reads=[Bp, Bcst], writes=[Bp])
                st[u] = (pt, Bp, c0)

            def fox_b(u):
                I_, J = units[u]
                nJ = 4 * I_ + 4
                pt, Bp, c0 = st.pop(u)
                if J == 0:
                    st["acc"] = accPS.next()
                acc, Bacc = st["acc"]
                S.pe(I("matmul", out=acc[0:65, c0:512], lhsT=v1[:, J, 0:65], rhs=pt[:, c0:512], start=(J == 0), stop=(J == nJ - 1)),
                     reads=[Bv, Bp], writes=[Bacc])
                if J == nJ - 1:
                    rec, Brec = recr.next()
                    S.dve(I("reciprocal", out=rec[64:65, :], in_=acc[64:65, 0:512]), reads=[Bacc], writes=[Brec])
                    bc, Bbc = bcPS.next()
                    S.pe(I("matmul", out=bc[0:64, 0:512], lhsT=ones_f[64:65, 0:64], rhs=rec[64:65, :], start=True, stop=True),
                         reads=[Brec, Bcst], writes=[Bbc])
                    bcs, Bbcs = bcsr.next()
                    S.act(I("activation", out=bcs[:], in_=bc[0:64, 0:512], func=AF.Copy), reads=[Bbc], writes=[Bbcs])
                    yo, Byo = yor.next()
                    S.dve(I("tensor_tensor", out=yo[:], in0=acc[0:64, 0:512], in1=bcs[:], op=ALU.mult), reads=[Bacc, Bbcs], writes=[Byo])
                    r0 = (I_ // 4) * 256 + h * 64
                    c.store(io["yfox"][r0:r0 + 64, (I_ % 4) * 512:(I_ % 4 + 1) * 512], yo[:], [Byo])

            SK = 2
            for u in range(len(units) + SK):
                if u < len(units):
                    fox_a(u)
                if u >= SK:
                    fox_b(u - SK)

    if "sb" in parts:
        er = Rot(c, [128, 512], F32, 2, "ee")
        spr = Rot(c, [128, 512], BF16, 3, "sp")
        saccr = Rot(c, [128, 512], F32, 4, "sacc")
        for h in range(4):
            qt, Bq, kt, Bk, v1, Bv = load_head(io["sq"], io["sk"], io["sv"], h)
            units = [(I_, J) for I_ in range(16) for J in range(4 * I_ + 3, -1, -1)]
            st = {}

            def sb_a(u):
                I_, J = units[u]
                c0 = max(0, J - 4 * I_) * 128
                kblk = kt[0:64, J * 128:(J + 1) * 128]
                qblk = qt[0:64, I_ * 512 + c0:(I_ + 1) * 512]
                z, Bz = sPS.next()
                S.pe(I("matmul", out=z[:, c0:512], lhsT=kblk, rhs=qblk, start=True, stop=True), reads=[Bk, Bq], writes=[Bz])
                ee, Be = er.next()
                S.act(I("activation", out=ee[:, c0:512], in_=z[:, c0:512], func=AF.Exp), reads=[Bz], writes=[Be])
                sp, Bsp = spr.next()
                S.act(I("activation", out=sp[:, c0:512], in_=ee[:, c0:512], func=AF.Ln, bias=1.0), reads=[Be], writes=[Bsp])
                if J >= 4 * I_:
                    S.dve(I("tensor_tensor", out=sp[:, c0:c0 + 128], in0=sp[:, c0:c0 + 128], in1=strict_b, op=ALU.mult),
                          reads=[Bsp, Bcst], writes=[Bsp])
                st[u] = (sp, Bsp, c0, kblk, qblk)

            def sb_b(u):
                I_, J = units[u]
                nJ = 4 * I_ + 4
                first = (J == nJ - 1)
                sp, Bsp, c0, kblk, qblk = st.pop(u)
                if first:
                    st["acc"] = accPS.next()
                    st["sacc"] = [saccr.next(), saccr.next()]
                    st["si"] = 0
                    for sa_, Bsa_ in st["sacc"]:
                        S.dve(I("memset", ap=sa_[:], constant=0.0), writes=[Bsa_])
                acc, Bacc = st["acc"]
                sacc, Bsa = st["sacc"][st["si"]]
                la, Bla = sPS.next()
                S.pe(I("matmul", out=la[:, c0:512], lhsT=kblk, rhs=qblk, start=True, stop=False), reads=[Bk, Bq], writes=[Bla])
                S.pe(I("matmul", out=la[:, c0:512], lhsT=niu_b, rhs=sp[:, c0:512], start=False, stop=first),
                     reads=[Bsp, Bcst], writes=[Bla])
                if not first:
                    S.pe(I("matmul", out=la[:, c0:512], lhsT=negones_f[:], rhs=sacc[:, c0:512], start=False, stop=True),
                         reads=[Bsa, Bnof], writes=[Bla])
                at, Bat = Pr.next()
                S.act(I("activation", out=at[:, c0:512], in_=la[:, c0:512], func=AF.Exp), reads=[Bla], writes=[Bat])
                if J >= 4 * I_:
                    S.dve(I("tensor_tensor", out=at[:, c0:c0 + 128], in0=at[:, c0:c0 + 128], in1=strict_b, op=ALU.mult),
                          reads=[Bat, Bcst], writes=[Bat])
                S.pe(I("matmul", out=acc[0:64, c0:512], lhsT=v1[:, J, 0:64], rhs=at[:, c0:512], start=first, stop=(J == 0)),
                     reads=[Bv, Bat], writes=[Bacc])
                if J > 0:
                    nxt, Bnx = st["sacc"][1 - st["si"]]
                    S.dve(I("tensor_tensor", out=nxt[:, c0:512], in0=sacc[:, c0:512], in1=sp[:, c0:512], op=ALU.add),
                          reads=[Bsa, Bsp], writes=[Bnx])
                    st["si"] = 1 - st["si"]
                else:
                    yo, Byo = yor.next()
                    S.act(I("activation", out=yo[:], in_=acc[0:64, 0:512], func=AF.Copy), reads=[Bacc], writes=[Byo])
                    r0 = (I_ // 4) * 256 + h * 64
                    c.store(io["ysb"][r0:r0 + 64, (I_ % 4) * 512:(I_ % 4 + 1) * 512], yo[:], [Byo])

            for u in range(len(units) + 1):
                if u < len(units):
                    sb_a(u)
                if u >= 1:
                    sb_b(u - 1)

    if "lru" in parts:
        LT = 1024
        pl = c.sb([128, NPL], F32, "pl")
        Bpl = Buf("pl")
        S.dma(I("dma_start", out=pl[:], in_=io["pl"]), writes=[Bpl])
        wbd = c.sb([128, 512], BF16, "wbd")
        Bwbd = Buf("wbd")
        S.dma(I("dma_start", out=wbd[:], in_=io["wbd"]), writes=[Bwbd], q="pool")
        cpar = c.sb([128, 2], F32, "cpar")
        Bcp = Buf("cpar")
        S.act(I("activation", out=cpar[:], in_=pl[:, PL["lam"]:PL["lam"] + 2], func=AF.Exp, scale=-1.0), reads=[Bpl], writes=[Bcp])
        S.act(I("activation", out=cpar[:], in_=cpar[:], func=AF.Ln, bias=1.0), reads=[Bcp], writes=[Bcp])
        S.dve(I("tensor_scalar", out=cpar[:], in0=cpar[:], scalar1=-8.0, scalar2=None, op0=ALU.mult), reads=[Bcp], writes=[Bcp])
        xinr = Rot(c, [128, 3 + LT], F32, 2, "xin")
        gr = Rot(c, [128, LT], F32, 2, "g")
        hsr = Rot(c, [128, LT], F32, 2, "hs")
        xc, Bxc = c.sb([128, LT], F32, "xc"), Buf("xc")
        xcb, Bxcb = c.sb([128, LT], BF16, "xcb"), Buf("xcb")
        rr, Brr = c.sb([128, LT], F32, "r"), Buf("r")
        ii, Bii = c.sb([128, LT], F32, "i"), Buf("i")
        aa, Baa = c.sb([128, LT], F32, "a"), Buf("a")
        tmp, Btmp = c.sb([128, LT], F32, "tmp"), Buf("tmp")
        t2, Bt2 = c.sb([128, LT], F32, "t2"), Buf("t2")
        ybr = Rot(c, [128, LT], BF16, 2, "yb")
        for ci in range(2):
            pxin, Bpx, phs, Bph = None, None, None, None
            for jt in range(S_LEN // LT):
                j, off = (jt * LT) // 2048, (jt * LT) % 2048
                r0 = j * 256 + ci * 128
                xin, Bxi = xinr.next()
                S.dma(I("dma_start", out=xin[:, 3:3 + LT], in_=io["lx"][r0:r0 + 128, off:off + LT]), writes=[Bxi])
                if jt == 0:
                    S.dve(I("memset", ap=xin[:, 0:3], constant=0.0), writes=[Bxi], join=True)
                else:
                    S.dve(I("tensor_copy", out=xin[:, 0:3], in_=pxin[:, LT:LT + 3]), reads=[Bpx], writes=[Bxi], join=True)
                g, Bg = gr.next()
                S.dma(I("dma_start", out=g[:], in_=io["lg"][r0:r0 + 128, off:off + LT]), writes=[Bg])
                cw = lambda i: pl[:, PL["cw"] + 2 * i + ci:PL["cw"] + 2 * i + ci + 1]
                S.dve(I("tensor_scalar", out=xc[:], in0=xin[:, 3:3 + LT], scalar1=cw(3), scalar2=pl[:, PL["cb"] + ci:PL["cb"] + ci + 1],
                        op0=ALU.mult, op1=ALU.add), reads=[Bxi, Bpl], writes=[Bxc])
                for i in range(3):
                    S.dve(I("scalar_tensor_tensor", out=xc[:], in0=xin[:, i:i + LT], scalar=cw(i), in1=xc[:], op0=ALU.mult, op1=ALU.add),
                          reads=[Bxi, Bpl, Bxc], writes=[Bxc])
                S.act(I("activation", out=xcb[:], in_=xc[:], func=AF.Copy), reads=[Bxc], writes=[Bxcb])
                for s in range(LT // 512):
                    ss = slice(s * 512, (s + 1) * 512)
                    for which, dst, Bd, bcol in ((0, rr, Brr, PL["ba"]), (1, ii, Bii, PL["bx"])):
                        ps, Bp = sPS.next()
                        wsl = wbd[:, (which * 2 + ci) * 128:(which * 2 + ci + 1) * 128]
                        S.pe(I("matmul", out=ps[:, 0:512], lhsT=wsl, rhs=xcb[:, ss], start=True, stop=True), reads=[Bwbd, Bxcb], writes=[Bp])
                        S.act(I("activation", out=dst[:, ss], in_=ps[:, 0:512], func=AF.Sigmoid, bias=pl[:, bcol + ci:bcol + ci + 1]),
                              reads=[Bp, Bpl], writes=[Bd], join=(s > 0))
                S.act(I("activation", out=aa[:], in_=rr[:], func=AF.Exp, scale=cpar[:, ci:ci + 1]), reads=[Brr, Bcp], writes=[Baa])
                S.dve(I("tensor_scalar", out=aa[:], in0=aa[:], scalar1=1.0, scalar2=None, op0=ALU.min), reads=[Baa], writes=[Baa])
                S.dve(I("tensor_tensor", out=tmp[:], in0=aa[:], in1=aa[:], op=ALU.mult), reads=[Baa], writes=[Btmp])
                S.act(I("activation", out=tmp[:], in_=tmp[:], func=AF.Sqrt, bias=1.0, scale=-1.0), reads=[Btmp], writes=[Btmp])
                S.dve(I("tensor_tensor", out=tmp[:], in0=tmp[:], in1=ii[:], op=ALU.mult), reads=[Btmp, Bii], writes=[Btmp])
                S.dve(I("tensor_tensor", out=tmp[:], in0=tmp[:], in1=xc[:], op=ALU.mult), reads=[Btmp, Bxc], writes=[Btmp])
                hs, Bhs = hsr.next()
                init = 0.0 if jt == 0 else phs[:, LT - 1:LT]
                S.dve(I("tensor_tensor_scan", out=hs[:], data0=aa[:], data1=tmp[:], initial=init, op0=ALU.mult, op1=ALU.add),
                      reads=[Baa, Btmp] + ([Bph] if jt else []), writes=[Bhs])
                S.dve(I("tensor_tensor", out=t2[:], in0=g[:], in1=g[:], op=ALU.mult), reads=[Bg], writes=[Bt2])
                S.dve(I("tensor_scalar", out=t2[:], in0=t2[:], scalar1=0.0713548163, scalar2=1.5957691216, op0=ALU.mult, op1=ALU.add),
                      reads=[Bt2], writes=[Bt2])
                S.dve(I("tensor_tensor", out=t2[:], in0=t2[:], in1=g[:], op=ALU.mult), reads=[Bt2, Bg], writes=[Bt2])
                S.act(I("activation", out=t2[:], in_=t2[:], func=AF.Sigmoid), reads=[Bt2], writes=[Bt2])
                S.dve(I("tensor_tensor", out=t2[:], in0=t2[:], in1=g[:], op=ALU.mult), reads=[Bt2, Bg], writes=[Bt2])
                yb, Byb = ybr.next()
                S.dve(I("tensor_tensor", out=yb[:], in0=t2[:], in1=hs[:], op=ALU.mult), reads=[Bt2, Bhs], writes=[Byb])
                c.store(io["ylru"][r0:r0 + 128, off:off + LT], yb[:], [Byb])
                pxin, Bpx, phs, Bph = xin, Bxi, hs, Bhs


def build_B(parts=("fox", "sb", "lru")):
    c = Ctx()
    io = dict(parts=parts, cst=c.din("cst", [128, NCST]),
              fq=c.din("fq", [D, TOK], BF16), fk=c.din("fk", [D, TOK], BF16), fv=c.din("fv", [S_LEN, 256], BF16),
              logf=c.din("logf", [16, TOK]), sq=c.din("sq", [D, TOK], BF16), sk=c.din("sk", [D, TOK], BF16),
              sv=c.din("sv", [S_LEN, 256], BF16), lx=c.din("lx", [D, TOK]), lg=c.din("lg", [D, TOK]),
              pl=c.din("pl", [128, NPL]), wbd=c.din("wbd", [128, 512]),
              yfox=c.dout("yfox", [D, TOK], BF16), ysb=c.dout("ysb", [D, TOK], BF16), ylru=c.dout("ylru", [D, TOK], BF16))
    phase_B(c, io)
    return c.finish()


def load_x(c, xd):
    S = c.S
    xT = c.sb([128, 8, TOK], F32, "xT")
    Bx = [Buf("x%d" % t) for t in range(NT)]
    xsrc = xd.rearrange("(kc p) t -> p kc t", p=128)
    for t in range(NT):
        S.dma(I("dma_start", out=xT[:, :, t * 512:(t + 1) * 512], in_=xsrc[:, :, t * 512:(t + 1) * 512]), writes=[Bx[t]])
    return xT, Bx


def store_x(c, xT, Bx, xd):
    xdst = xd.rearrange("(kc p) t -> p kc t", p=128)
    for t in range(NT):
        c.store(xdst[:, :, t * 512:(t + 1) * 512], xT[:, :, t * 512:(t + 1) * 512], [Bx[t]])


def phase_C1(c, io, xT, Bx):
    S = c.S
    cst, cstb, Bcst = load_consts(c, io["cst"])
    ones_f = cst[:, CST["ones"]:CST["ones"] + 128]
    pv = c.sb([128, NPV], F32, "pv")
    Bpv = Buf("pv")
    S.dma(I("dma_start", out=pv[:], in_=io["pv"]), writes=[Bpv])
    PS = PsRot(c, 8)
    tmpf = Rot(c, [128, 512], F32, 3, "tmpf")
    sqr = Rot(c, [128, 512], F32, 3, "sqr")
    wrot = Rot(c, [128, 8, 512], BF16, 3, "w")
    ytr = Rot(c, [128, 8, 512], BF16, 2, "yt")
    gtr = Rot(c, [128, 8, 512], BF16, 2, "gt")
    mixed, Bmix = c.sb([128, 8, 512], F32, "mixed"), Buf("mixed")
    mixb, Bmixb = c.sb([128, 8, 512], BF16, "mixb"), Buf("mixb")
    h2r = Rot(c, [128, 8, 512], BF16, 2, "h2")
    ysrc = [io["yfox"], io["ylru"], io["ysb"], io["ymem"]]
    for t in range(NT):
        ts = slice(t * 512, (t + 1) * 512)
        for i in range(4):
            yt, Byt = ytr.next()
            S.dma(I("dma_start", out=yt[:], in_=ysrc[i][:, ts].rearrange("(kc p) t -> p kc t", p=128)), writes=[Byt])
            gt, Bgt = gtr.next()
            S.dma(I("dma_start", out=gt[:], in_=io["gates"][i][:, ts].rearrange("(kc p) t -> p kc t", p=128)), writes=[Bgt])
            for half in range(2):
                wt, Bw = wload(c, wrot, io["w_branch"][i], half * 512, 512)
                for o4 in range(4):
                    oc = half * 4 + o4
                    ps, Bp = PS.next()
                    for kc in range(8):
                        S.pe(I("matmul", out=ps[:, 0:512], lhsT=wt[:, kc, o4 * 128:(o4 + 1) * 128], rhs=yt[:, kc, :],
                               start=(kc == 0), stop=(kc == 7)), reads=[Bw, Byt], writes=[Bp])
                    if i == 0:
                        S.dve(I("tensor_tensor", out=mixed[:, oc, :], in0=ps[:, 0:512], in1=gt[:, oc, :], op=ALU.mult),
                              reads=[Bp, Bgt], writes=[Bmix], join=(oc > 0))
                    else:
                        tm, Btm = tmpf.next()
                        S.dve(I("tensor_tensor", out=tm[:], in0=ps[:, 0:512], in1=gt[:, oc, :], op=ALU.mult), reads=[Bp, Bgt], writes=[Btm])
                        S.dve(I("tensor_tensor", out=mixed[:, oc, :], in0=mixed[:, oc, :], in1=tm[:], op=ALU.add),
                              reads=[Btm, Bmix], writes=[Bmix])
        S.act(I("activation", out=mixb[:], in_=mixed[:], func=AF.Copy), reads=[Bmix], writes=[Bmixb])
        for half in range(2):
            wt, Bw = wload(c, wrot, io["w_out"], half * 512, 512)
            for o4 in range(4):
                oc = half * 4 + o4
                ps, Bp = PS.next()
                for kc in range(8):
                    S.pe(I("matmul", out=ps[:, 0:512], lhsT=wt[:, kc, o4 * 128:(o4 + 1) * 128], rhs=mixb[:, kc, :],
                           start=(kc == 0), stop=(kc == 7)), reads=[Bw, Bmixb], writes=[Bp])
                S.dve(I("tensor_tensor", out=xT[:, oc, ts], in0=xT[:, oc, ts], in1=ps[:, 0:512], op=ALU.add), reads=[Bp, Bx[t]], writes=[Bx[t]])
        h2, Bh2 = h2r.next()
        rmsnorm_tile(c, PS, xT, Bx[t], ts, PV["ffn_g"], pv, Bpv, ones_f, Bcst, tmpf, sqr, h2, slice(0, 512), Bh2)
        c.store(io["h2"][:, ts].rearrange("(kc p) t -> p kc t", p=128), h2[:], [Bh2])


def phase_C2(c, io, xT, Bx):
    S = c.S
    pv = c.sb([128, NPV], F32, "pv")
    Bpv = Buf("pv")
    S.dma(I("dma_start", out=pv[:], in_=io["pv"]), writes=[Bpv])
    PS = PsRot(c, 8)
    wrot = Rot(c, [128, 8, 512], BF16, 4, "w")
    wdrot = Rot(c, [128, NCC, 512], BF16, 2, "wd")
    h2r = Rot(c, [128, 8, 514], BF16, 2, "h2e")
    Gr = Rot(c, [128, 514], F32, 3, "G")
    gpr = Rot(c, [128, 512], F32, 3, "gp")
    hid, Bhid = c.sb([128, NCC, 512], BF16, "hid"), Buf("hid")
    h2src = io["h2"].rearrange("(kc p) t -> p kc t", p=128)
    for t in range(NT):
        ts = slice(t * 512, (t + 1) * 512)
        h2e, Bh = h2r.next()
        S.dma(I("dma_start", out=h2e[:, :, 2:514], in_=h2src[:, :, ts]), writes=[Bh])
        hsrc = io["h2halo"].rearrange("(kc p) t -> p kc t", p=128) if t == 0 else h2src[:, :, t * 512 - 2:t * 512]
        S.dma(I("dma_start", out=h2e[:, :, 0:2], in_=hsrc), writes=[Bh], join=True)
        for w6 in range(6):
            ncol = 512 if w6 < 5 else 256
            wg, Bwg = wload(c, wrot, io["w_up"], w6 * 512, ncol)
            wv, Bwv = wload(c, wrot, io["w_up"], DFF + w6 * 512, ncol)
            for c4 in range(ncol // 128):
                cc = w6 * 4 + c4
                pg, Bpg = PS.next()
                for kc in range(8):
                    S.pe(I("matmul", out=pg[:, 0:512], lhsT=wg[:, kc, c4 * 128:(c4 + 1) * 128], rhs=h2e[:, kc, 2:514],
                           start=(kc == 0), stop=(kc == 7)), reads=[Bwg, Bh], writes=[Bpg])
                ph, Bph = PS.next()
                for kc in range(8):
                    S.pe(I("matmul", out=ph[:, 0:2], lhsT=wg[:, kc, c4 * 128:(c4 + 1) * 128], rhs=h2e[:, kc, 0:2],
                           start=(kc == 0), stop=(kc == 7)), reads=[Bwg, Bh], writes=[Bph])
                G, BG = Gr.next()
                S.act(I("activation", out=G[:, 2:514], in_=pg[:, 0:512], func=AF.Copy), reads=[Bpg], writes=[BG])
                S.act(I("activation", out=G[:, 0:2], in_=ph[:, 0:2], func=AF.Copy), reads=[Bph], writes=[BG], join=True)
                gp, Bgp = gpr.next()
                cw = lambda i: pv[:, PV["ffn_cw"] + NCC * i + cc:PV["ffn_cw"] + NCC * i + cc + 1]
                S.dve(I("tensor_scalar", out=gp[:], in0=G[:, 2:514], scalar1=cw(2), scalar2=pv[:, PV["ffn_cb"] + cc:PV["ffn_cb"] + cc + 1],
                        op0=ALU.mult, op1=ALU.add), reads=[BG, Bpv], writes=[Bgp])
                S.dve(I("scalar_tensor_tensor", out=gp[:], in0=G[:, 1:513], scalar=cw(1), in1=gp[:], op0=ALU.mult, op1=ALU.add),
                      reads=[BG, Bpv, Bgp], writes=[Bgp])
                S.dve(I("scalar_tensor_tensor", out=gp[:], in0=G[:, 0:512], scalar=cw(0), in1=gp[:], op0=ALU.mult, op1=ALU.add),
                      reads=[BG, Bpv, Bgp], writes=[Bgp])
                S.act(I("activation", out=gp[:], in_=gp[:], func=AF.Silu), reads=[Bgp], writes=[Bgp])
                pvv, Bpvv = PS.next()
                for kc in range(8):
                    S.pe(I("matmul", out=pvv[:, 0:512], lhsT=wv[:, kc, c4 * 128:(c4 + 1) * 128], rhs=h2e[:, kc, 2:514],
                           start=(kc == 0), stop=(kc == 7)), reads=[Bwv, Bh], writes=[Bpvv])
                S.dve(I("tensor_tensor", out=hid[:, cc, :], in0=pvv[:, 0:512], in1=gp[:], op=ALU.mult), reads=[Bpvv, Bgp], writes=[Bhid],
                      join=(cc > 0))
        for half in range(2):
            wd, Bwd = wload(c, wdrot, io["w_down"], half * 512, 512, nk=NCC)
            for o4 in range(4):
                oc = half * 4 + o4
                ps, Bp = PS.next()
                for cc in range(NCC):
                    S.pe(I("matmul", out=ps[:, 0:512], lhsT=wd[:, cc, o4 * 128:(o4 + 1) * 128], rhs=hid[:, cc, :],
                           start=(cc == 0), stop=(cc == NCC - 1)), reads=[Bwd, Bhid], writes=[Bp])
                S.dve(I("tensor_tensor", out=xT[:, oc, ts], in0=xT[:, oc, ts], in1=ps[:, 0:512], op=ALU.add), reads=[Bp, Bx[t]], writes=[Bx[t]])


def build_C1():
    c = Ctx()
    io = dict(cst=c.din("cst", [128, NCST]), pv=c.din("pv", [128, NPV]), xT=c.din("xT", [D, TOK]),
              yfox=c.din("yfox", [D, TOK], BF16), ylru=c.din("ylru", [D, TOK], BF16), ysb=c.din("ysb", [D, TOK], BF16),
              ymem=c.din("ymem", [D, TOK], BF16), gates=[c.din("gates%d" % i, [D, TOK], BF16) for i in range(4)],
              w_branch=[c.din("w_branch%d" % i, [D, D]) for i in range(4)], w_out=c.din("w_out", [D, D]),
              xo=c.dout("xo", [D, TOK]), h2=c.dout("h2", [D, TOK], BF16))
    xT, Bx = load_x(c, io["xT"])
    phase_C1(c, io, xT, Bx)
    store_x(c, xT, Bx, io["xo"])
    return c.finish()


def build_C2():
    c = Ctx()
    io = dict(pv=c.din("pv", [128, NPV]), xT=c.din("xT", [D, TOK]), h2=c.din("h2", [D, TOK], BF16), h2halo=c.din("h2halo", [D, 2], BF16),
              w_up=c.din("w_up", [D, 2 * DFF]), w_down=c.din("w_down", [DFF, D]), xo=c.dout("xo", [D, TOK]))
    xT, Bx = load_x(c, io["xT"])
    phase_C2(c, io, xT, Bx)
    store_x(c, xT, Bx, io["xo"])
    return c.finish()


_NC = {}


def _prog(name, builder):
    if name not in _NC:
        _NC[name] = builder()
    return _NC[name]


def _run(nc, maps):
    return run_bass_kernel_spmd(nc, maps, core_ids=list(range(8))).results


def kernel(_nlayers=4, **inp):
    inp = {k: np.asarray(v) for k, v in inp.items()}
    bf = ml_dtypes.bfloat16
    cst = host_consts()
    xs = [np.ascontiguousarray(inp["x"][c // 4, (c % 4) * TOK:(c % 4 + 1) * TOK].T) for c in range(8)]
    memT = [np.ascontiguousarray(inp["mem"][b].T) for b in range(2)]
    zhalo = np.zeros((D, 2), bf)
    for l in range(_nlayers):
        pv = host_pvec(inp, l)
        w_in = np.ascontiguousarray(inp["w_in"][l])
        w_kv = np.ascontiguousarray(inp["w_mem_kv"][l])
        rA = _run(_prog("A", build_A), [dict(xT=xs[c], memT=memT[c // 4], w_in=w_in, w_kv=w_kv, pv=pv, cst=cst) for c in range(8)])
        mapsB = []
        for c in range(8):
            b, g = c // 4, c % 4
            src = [rA[b * 4 + j] for j in range(4)]

            def rows(key, n, src=src, g=g):
                return np.ascontiguousarray(np.concatenate([np.asarray(s[key])[g * n:(g + 1) * n] for s in src], 0))

            def toks(key, src=src, g=g):
                return np.ascontiguousarray(np.concatenate([np.asarray(s[key])[g] for s in src], 0))
            pl, wbd = host_plru(inp, l, g)
            mapsB.append(dict(cst=cst, fq=rows("fq", 256), fk=rows("fk", 256), fv=toks("fv"), logf=rows("logf", 4),
                              sq=rows("sq", 256), sk=rows("sk", 256), sv=toks("sv"), lx=rows("lx", 256), lg=rows("lg", 256),
                              pl=pl, wbd=wbd))
        rB = _run(_prog("B", build_B), mapsB)
        mapsC = []
        for c in range(8):
            b, j = c // 4, c % 4

            def gath(key, b=b, j=j):
                return np.ascontiguousarray(np.concatenate([np.asarray(rB[b * 4 + g][key])[j * 256:(j + 1) * 256] for g in range(4)], 0))
            m = dict(cst=cst, pv=pv, xT=xs[c], yfox=gath("yfox"), ylru=gath("ylru"), ysb=gath("ysb"), ymem=np.asarray(rA[c]["ymem"]),
                     w_out=np.ascontiguousarray(inp["w_out"][l]))
            for i in range(4):
                m["gates%d" % i] = np.asarray(rA[c]["gates%d" % i])
                m["w_branch%d" % i] = np.ascontiguousarray(inp["w_branch"][l][i])
            mapsC.append(m)
        rC1 = _run(_prog("C1", build_C1), mapsC)
        mapsC2 = []
        for c in range(8):
            halo = zhalo if c % 4 == 0 else np.ascontiguousarray(np.asarray(rC1[c - 1]["h2"])[:, -2:])
            mapsC2.append(dict(pv=pv, xT=np.asarray(rC1[c]["xo"]), h2=np.asarray(rC1[c]["h2"]), h2halo=halo,
                               w_up=np.ascontiguousarray(inp["w_up"][l]), w_down=np.ascontiguousarray(inp["w_down"][l])))
        rC2 = _run(_prog("C2", build_C2), mapsC2)
        xs = [np.asarray(rC2[c]["xo"]) for c in range(8)]
    out = np.zeros((2, S_LEN, D), np.float32)
    for c in range(8):
        out[c // 4, (c % 4) * TOK:(c % 4 + 1) * TOK] = xs[c].T
    return out
```
